# Optimizing a Trainium2 kernel written in Bass

```python
import jax, jax.numpy as jnp
from jax import lax
import numpy as np

D_MODEL = 1024
BATCH = 4
SEQ = 4096
DEPTH = 2
DEC_BATCH = 128
DEC_SEQ = 4
PAST_LEN = 16384
PAGE_SIZE = 128

F32 = jnp.float32
HEAD_DIM = 64
ROT_DIM = HEAD_DIM // 4
ROPE_THETA = 500000.0
BLOCK = 128
A_Q_HEADS = 8
A_KV_HEADS = 2
A_GROUP = A_Q_HEADS // A_KV_HEADS
A_WINDOW = 128
B_DILATIONS = ((128, 1), (512, 4), (2048, 16))
N_B_GROUPS = 3
B_Q_HEADS = 8
B_KV_HEADS = 2
B_GROUP = B_Q_HEADS // B_KV_HEADS
A_QW = A_Q_HEADS * HEAD_DIM
A_KW = A_KV_HEADS * HEAD_DIM
B_QW = N_B_GROUPS * B_Q_HEADS * HEAD_DIM
B_KW = N_B_GROUPS * B_KV_HEADS * HEAD_DIM
MIX0 = A_QW + B_Q_HEADS * HEAD_DIM
IN0 = A_QW + 2 * A_KW + B_QW + 2 * B_KW + MIX0
C_HEADS = 4
C_HEAD_DIM = 256
MIX1 = C_HEADS * C_HEAD_DIM
IN1 = 5 * MIX1 + 2 * C_HEADS
CHUNK = 128
DN_ALPHA = (2 * DEPTH) ** 0.25
DN_BETA = (8 * DEPTH) ** -0.25
LN_EPS = 1e-5
MH_EPS = 1e-6
NEG_INF = -1e30
N_EVEN = (DEPTH + 1) // 2
N_ODD = DEPTH // 2

kernel_name = 'hybrid_swa_dilated_mlstm_deepnorm_step'


def layer_norm(x, g, b):
    xf = x.astype(F32)
    mu = xf.mean(-1, keepdims=True)
    var = jnp.square(xf - mu).mean(-1, keepdims=True)
    return ((xf - mu) * lax.rsqrt(var + LN_EPS) * g.astype(F32) + b.astype(F32)).astype(x.dtype)


def rope_partial(x, pos):
    half = ROT_DIM // 2
    inv = ROPE_THETA ** (-jnp.arange(half, dtype=F32) / half)
    ang = pos.astype(F32)[:, None] * inv[None, :]
    cos = jnp.cos(ang)[None, :, None, :]
    sin = jnp.sin(ang)[None, :, None, :]
    xr = x[..., :ROT_DIM].astype(F32)
    x1, x2 = xr[..., :half], xr[..., half:]
    rot = jnp.concatenate([x1 * cos - x2 * sin, x2 * cos + x1 * sin], axis=-1).astype(x.dtype)
    return jnp.concatenate([rot, x[..., ROT_DIM:]], axis=-1)


def _softmax_stats(s, mask, sink):
    s = jnp.where(mask, s, NEG_INF)
    m = s.max(-1)
    if sink is not None:
        m = jnp.maximum(m, sink)
    p = jnp.exp(s - m[..., None])
    den = p.sum(-1)
    if sink is not None:
        den = den + jnp.exp(sink - m)
    return p, den, m + jnp.log(den)


def banded_attention(q, k, v, span, sink=None):
    N, n = q.shape[0], q.shape[1]
    Hk, G, dh = q.shape[2], q.shape[3], q.shape[4]
    pad = (-n) % BLOCK
    if pad:
        q = jnp.pad(q, ((0, 0), (0, pad), (0, 0), (0, 0), (0, 0)))
        k = jnp.pad(k, ((0, 0), (0, pad), (0, 0), (0, 0)))
        v = jnp.pad(v, ((0, 0), (0, pad), (0, 0), (0, 0)))
    nb = (n + pad) // BLOCK
    qb = q.reshape(N, nb, BLOCK, Hk, G, dh)

    def with_prev(t):
        tb = t.reshape(N, nb, BLOCK, Hk, dh)
        prev = jnp.pad(tb, ((0, 0), (1, 0), (0, 0), (0, 0), (0, 0)))[:, :nb]
        return jnp.concatenate([prev, tb], axis=2)

    kk, vv = with_prev(k), with_prev(v)
    s = jnp.einsum('nbqhgd,nbkhd->nbhgqk', qb, kk, preferred_element_type=F32)
    qi = jnp.arange(BLOCK)[:, None] + BLOCK
    ki = jnp.arange(2 * BLOCK)[None, :]
    dist = qi - ki
    band = (dist >= 0) & (dist <= span)
    has_prev = (jnp.arange(nb)[:, None, None] > 0) | (ki[None] >= BLOCK)
    mask = (band[None] & has_prev)[None, :, None, None]
    sink_b = None if sink is None else sink.astype(F32)[None, None, :, :, None]
    p, den, lse = _softmax_stats(s, mask, sink_b)
    o = jnp.einsum('nbhgqk,nbkhd->nbqhgd', p, vv.astype(F32))
    o = o / jnp.moveaxis(den, -1, 2)[..., None]
    lse = jnp.moveaxis(lse, -1, 2)
    o = o.reshape(N, nb * BLOCK, Hk, G, dh)[:, :n]
    lse = lse.reshape(N, nb * BLOCK, Hk, G)[:, :n]
    return o, lse


def strided_gather_attention(q, k_all, v_all, n_past, span, dilation, sink=None):
    T = q.shape[1]
    idx = n_past + jnp.arange(T)[:, None] - dilation * jnp.arange(span + 1)[None, :]
    valid = idx >= 0
    idx = jnp.maximum(idx, 0)
    kg = k_all[:, idx]
    vg = v_all[:, idx]
    s = jnp.einsum('nthgd,ntjhd->nthgj', q, kg, preferred_element_type=F32)
    sink_b = None if sink is None else sink.astype(F32)[None, None]
    p, den, lse = _softmax_stats(s, valid[None, :, None, None, :], sink_b)
    o = jnp.einsum('nthgj,ntjhd->nthgd', p, vg.astype(F32)) / den[..., None]
    return o, lse


def _to_sub(t, dil):
    Bn, S = t.shape[0], t.shape[1]
    rest = t.shape[2:]
    return jnp.moveaxis(t.reshape((Bn, S // dil, dil) + rest), 2, 1).reshape((Bn * dil, S // dil) + rest)


def _from_sub(t, Bn, dil):
    n = t.shape[1]
    rest = t.shape[2:]
    return jnp.moveaxis(t.reshape((Bn, dil, n) + rest), 1, 2).reshape((Bn, n * dil) + rest)


def _layer0_project(x, pos, w_in):
    Bn, S, _ = x.shape
    h = jnp.einsum('bsd,de->bse', x, w_in)
    c1 = A_QW
    c2 = c1 + A_KW
    c3 = c2 + A_KW
    c4 = c3 + B_QW
    c5 = c4 + B_KW
    c6 = c5 + B_KW
    qa, ka, va, qb, kb, vb, z = jnp.split(h, [c1, c2, c3, c4, c5, c6], axis=-1)
    scale = HEAD_DIM ** -0.5
    qa = (rope_partial(qa.reshape(Bn, S, A_Q_HEADS, HEAD_DIM), pos) * scale).reshape(Bn, S, A_KV_HEADS, A_GROUP, HEAD_DIM)
    ka = rope_partial(ka.reshape(Bn, S, A_KV_HEADS, HEAD_DIM), pos)
    va = va.reshape(Bn, S, A_KV_HEADS, HEAD_DIM)
    qb = (rope_partial(qb.reshape(Bn, S, N_B_GROUPS * B_Q_HEADS, HEAD_DIM), pos) * scale).reshape(
        Bn, S, N_B_GROUPS, B_KV_HEADS, B_GROUP, HEAD_DIM)
    kb = rope_partial(kb.reshape(Bn, S, N_B_GROUPS * B_KV_HEADS, HEAD_DIM), pos).reshape(
        Bn, S, N_B_GROUPS, B_KV_HEADS, HEAD_DIM)
    vb = vb.reshape(Bn, S, N_B_GROUPS, B_KV_HEADS, HEAD_DIM)
    return qa, ka, va, qb, kb, vb, z


def _layer0_out(x, oa, obs, lses, z, w_out, ln_g, ln_b):
    Bn, S, _ = x.shape
    wts = jax.nn.softmax(jnp.stack(lses, axis=0), axis=0)
    ob = jnp.einsum('rbshg,rbshgd->bshgd', wts, jnp.stack(obs, axis=0))
    mix = jnp.concatenate([oa.reshape(Bn, S, A_QW), ob.reshape(Bn, S, MIX0 - A_QW)], axis=-1)
    mix = (mix * jax.nn.silu(z.astype(F32))).astype(x.dtype)
    y = jnp.einsum('bse,ed->bsd', mix, w_out)
    return layer_norm(DN_ALPHA * x + y, ln_g, ln_b)


def layer0_prompt(x, w_in, sinks, w_out, ln_g, ln_b):
    Bn, S, _ = x.shape
    pos = jnp.arange(S)
    qa, ka, va, qb, kb, vb, z = _layer0_project(x, pos, w_in)
    oa, _ = banded_attention(qa, ka, va, A_WINDOW, sinks.reshape(A_KV_HEADS, A_GROUP))
    la = min(A_WINDOW, S)
    a_state = jnp.stack([ka[:, S - la:], va[:, S - la:]], axis=2)
    obs, lses, b_states = [], [], []
    for g in range(N_B_GROUPS):
        win, dil = B_DILATIONS[g]
        o, lse = banded_attention(_to_sub(qb[:, :, g], dil), _to_sub(kb[:, :, g], dil),
                                  _to_sub(vb[:, :, g], dil), win // dil)
        obs.append(_from_sub(o, Bn, dil))
        lses.append(_from_sub(lse, Bn, dil))
        lg = min(win, S)
        b_states.append(jnp.stack([kb[:, S - lg:, g], vb[:, S - lg:, g]], axis=2))
    y = _layer0_out(x, oa, obs, lses, z, w_out, ln_g, ln_b)
    return y, a_state, b_states


def layer0_sample(x, cache_a, caches_b, w_in, sinks, w_out, ln_g, ln_b):
    Bn, T, _ = x.shape
    pos = PAST_LEN + jnp.arange(T)
    qa, ka, va, qb, kb, vb, z = _layer0_project(x, pos, w_in)
    kv_a = jnp.concatenate([cache_a.astype(ka.dtype), jnp.stack([ka, va], axis=2)], axis=1)
    oa, _ = strided_gather_attention(qa, kv_a[:, :, 0], kv_a[:, :, 1], cache_a.shape[1], A_WINDOW, 1,
                                     sinks.reshape(A_KV_HEADS, A_GROUP))
    a_state = kv_a[:, T:]
    obs, lses, b_states = [], [], []
    for g in range(N_B_GROUPS):
        win, dil = B_DILATIONS[g]
        cache = caches_b[g]
        kv_b = jnp.concatenate([cache.astype(kb.dtype), jnp.stack([kb[:, :, g], vb[:, :, g]], axis=2)], axis=1)
        o, lse = strided_gather_attention(qb[:, :, g], kv_b[:, :, 0], kv_b[:, :, 1], cache.shape[1], win // dil, dil)
        obs.append(o)
        lses.append(lse)
        b_states.append(kv_b[:, T:])
    y = _layer0_out(x, oa, obs, lses, z, w_out, ln_g, ln_b)
    return y, a_state, b_states


def _layer1_project(x, w_in, b_gates):
    Bn, S, _ = x.shape
    h = jnp.einsum('bsd,de->bse', x, w_in)
    q, k, v, o, z, gates = jnp.split(h, [MIX1, 2 * MIX1, 3 * MIX1, 4 * MIX1, 5 * MIX1], axis=-1)
    gates = gates.astype(F32) + b_gates.astype(F32)
    ig = gates[..., :C_HEADS]
    lf = jax.nn.log_sigmoid(gates[..., C_HEADS:])
    q = q.reshape(Bn, S, C_HEADS, C_HEAD_DIM).astype(F32)
    k = k.reshape(Bn, S, C_HEADS, C_HEAD_DIM).astype(F32) * (C_HEAD_DIM ** -0.5)
    v = v.reshape(Bn, S, C_HEADS, C_HEAD_DIM).astype(F32)
    return q, k, v, ig, lf, o, z


def _mlstm_chunk(carry, inp):
    c_mem, c_norm, c_max = carry
    q, k, v, ig, lf = inp
    L = q.shape[1]
    b = jnp.cumsum(lf, axis=1)
    causal = jnp.tril(jnp.ones((L, L), dtype=bool))[None, :, :, None]
    dmat = jnp.where(causal, b[:, :, None, :] - b[:, None, :, :] + ig[:, None, :, :], NEG_INF)
    inter = b + c_max[:, None, :]
    m = jnp.maximum(inter, dmat.max(axis=2))
    pw = jnp.exp(dmat - m[:, :, None, :])
    sc = jnp.einsum('bthd,bshd->btsh', q, k) * pw
    carry_w = jnp.exp(inter - m)
    num = jnp.einsum('btsh,bshd->bthd', sc, v) + carry_w[..., None] * jnp.einsum('bthk,bhkv->bthv', q, c_mem)
    den = sc.sum(axis=2) + carry_w * jnp.einsum('bthk,bhk->bth', q, c_norm)
    h = num / jnp.maximum(jnp.abs(den), jnp.exp(-m))[..., None]
    m_new = m[:, -1]
    ws = jnp.exp(b[:, -1:, :] - b + ig - m_new[:, None, :])
    decay = jnp.exp(b[:, -1] + c_max - m_new)
    c_mem_new = decay[..., None, None] * c_mem + jnp.einsum('bsh,bshk,bshv->bhkv', ws, k, v)
    c_norm_new = decay[..., None] * c_norm + jnp.einsum('bsh,bshk->bhk', ws, k)
    return (c_mem_new, c_norm_new, m_new), h


def _chunks(t):
    return jnp.moveaxis(t.reshape((t.shape[0], t.shape[1] // CHUNK, CHUNK) + t.shape[2:]), 1, 0)


def _layer1_out(x, h, o, z, mh_g, w_out, ln_g, ln_b):
    Bn, S, _ = x.shape
    h = jax.nn.sigmoid(o.astype(F32)).reshape(Bn, S, C_HEADS, C_HEAD_DIM) * h
    mu = h.mean(-1, keepdims=True)
    var = jnp.square(h - mu).mean(-1, keepdims=True)
    hn = ((h - mu) * lax.rsqrt(var + MH_EPS)).reshape(Bn, S, MIX1) * mh_g.astype(F32)
    mix = (hn * jax.nn.silu(z.astype(F32))).astype(x.dtype)
    y = jnp.einsum('bse,ed->bsd', mix, w_out)
    return layer_norm(DN_ALPHA * x + y, ln_g, ln_b)


def layer1_prompt(x, w_in, b_gates, mh_g, w_out, ln_g, ln_b):
    Bn, S, _ = x.shape
    q, k, v, ig, lf, o, z = _layer1_project(x, w_in, b_gates)
    init = (jnp.zeros((Bn, C_HEADS, C_HEAD_DIM, C_HEAD_DIM), F32),
            jnp.zeros((Bn, C_HEADS, C_HEAD_DIM), F32),
            jnp.zeros((Bn, C_HEADS), F32))
    state, h = lax.scan(_mlstm_chunk, init, (_chunks(q), _chunks(k), _chunks(v), _chunks(ig), _chunks(lf)))
    h = jnp.moveaxis(h, 0, 1).reshape(Bn, S, C_HEADS, C_HEAD_DIM)
    y = _layer1_out(x, h, o, z, mh_g, w_out, ln_g, ln_b)
    return y, state


def layer1_sample(x, c_mem, c_norm, c_max, w_in, b_gates, mh_g, w_out, ln_g, ln_b):
    q, k, v, ig, lf, o, z = _layer1_project(x, w_in, b_gates)
    state, h = _mlstm_chunk((c_mem.astype(F32), c_norm.astype(F32), c_max.astype(F32)), (q, k, v, ig, lf))
    y = _layer1_out(x, h, o, z, mh_g, w_out, ln_g, ln_b)
    return y, state


def setup_inputs(seed: int = 0) -> dict:
    key = jax.random.key(seed)
    ks = jax.random.split(key, 20)
    nrm = jax.random.normal
    x_prompt = nrm(ks[0], (BATCH, SEQ, D_MODEL), F32)
    x_sample = nrm(ks[1], (DEC_BATCH, DEC_SEQ, D_MODEL), F32)
    cache_a_kv = nrm(ks[2], (N_EVEN, DEC_BATCH, min(A_WINDOW, PAST_LEN), 2, A_KV_HEADS, HEAD_DIM), F32)
    cache_b0_kv = nrm(ks[3], (N_EVEN, DEC_BATCH, min(B_DILATIONS[0][0], PAST_LEN), 2, B_KV_HEADS, HEAD_DIM), F32)
    cache_b1_kv = nrm(ks[4], (N_EVEN, DEC_BATCH, min(B_DILATIONS[1][0], PAST_LEN), 2, B_KV_HEADS, HEAD_DIM), F32)
    cache_b2_kv = nrm(ks[5], (N_EVEN, DEC_BATCH, min(B_DILATIONS[2][0], PAST_LEN), 2, B_KV_HEADS, HEAD_DIM), F32)
    state_c_mem = 0.05 * nrm(ks[6], (N_ODD, DEC_BATCH, C_HEADS, C_HEAD_DIM, C_HEAD_DIM), F32)
    state_c_norm = 0.2 * nrm(ks[7], (N_ODD, DEC_BATCH, C_HEADS, C_HEAD_DIM), F32)
    state_c_max = nrm(ks[8], (N_ODD, DEC_BATCH, C_HEADS), F32)
    col0 = np.ones((IN0,), np.float32)
    va0 = A_QW + A_KW
    col0[va0:va0 + A_KW] = DN_BETA
    vb0 = A_QW + 2 * A_KW + B_QW + B_KW
    col0[vb0:vb0 + B_KW] = DN_BETA
    w_in0 = nrm(ks[9], (N_EVEN, D_MODEL, IN0), F32) * (D_MODEL ** -0.5) * jnp.asarray(col0)
    sinks0 = 0.5 * nrm(ks[10], (N_EVEN, A_Q_HEADS), F32)
    w_out0 = nrm(ks[11], (N_EVEN, MIX0, D_MODEL), F32) * (MIX0 ** -0.5 * DN_BETA)
    col1 = np.ones((IN1,), np.float32)
    col1[2 * MIX1:3 * MIX1] = DN_BETA
    w_in1 = nrm(ks[12], (N_ODD, D_MODEL, IN1), F32) * (D_MODEL ** -0.5) * jnp.asarray(col1)
    ig_bias = 0.1 * nrm(ks[13], (N_ODD, C_HEADS), F32)
    fg_bias = jnp.linspace(3.0, 6.0, C_HEADS, dtype=F32)[None] + 0.01 * nrm(ks[14], (N_ODD, C_HEADS), F32)
    b_gates1 = jnp.concatenate([ig_bias, fg_bias], axis=-1)
    mh_norm1 = 1.0 + 0.02 * nrm(ks[15], (N_ODD, MIX1), F32)
    w_out1 = nrm(ks[16], (N_ODD, MIX1, D_MODEL), F32) * (MIX1 ** -0.5 * DN_BETA)
    ln_g = 1.0 + 0.02 * nrm(ks[17], (DEPTH, D_MODEL), F32)
    ln_b = 0.02 * nrm(ks[18], (DEPTH, D_MODEL), F32)
    return {'x_prompt': x_prompt, 'x_sample': x_sample,
            'cache_a_kv': cache_a_kv, 'cache_b0_kv': cache_b0_kv, 'cache_b1_kv': cache_b1_kv,
            'cache_b2_kv': cache_b2_kv, 'state_c_mem': state_c_mem, 'state_c_norm': state_c_norm,
            'state_c_max': state_c_max, 'w_in0': w_in0, 'sinks0': sinks0, 'w_out0': w_out0,
            'w_in1': w_in1, 'b_gates1': b_gates1, 'mh_norm1': mh_norm1, 'w_out1': w_out1,
            'ln_g': ln_g, 'ln_b': ln_b}


def reference(x_prompt, x_sample, cache_a_kv, cache_b0_kv, cache_b1_kv, cache_b2_kv, state_c_mem,
              state_c_norm, state_c_max, w_in0, sinks0, w_out0, w_in1, b_gates1, mh_norm1, w_out1,
              ln_g, ln_b):
    yp, ys = x_prompt, x_sample
    ap, bp, cp = [], [[], [], []], [[], [], []]
    asm, bsm, csm = [], [[], [], []], [[], [], []]
    for layer in range(DEPTH):
        e = layer // 2
        if layer % 2 == 0:
            yp, a_new, b_new = layer0_prompt(yp, w_in0[e], sinks0[e], w_out0[e], ln_g[layer], ln_b[layer])
            ap.append(a_new)
            for g in range(N_B_GROUPS):
                bp[g].append(b_new[g])
            ys, a_new, b_new = layer0_sample(ys, cache_a_kv[e], (cache_b0_kv[e], cache_b1_kv[e], cache_b2_kv[e]),
                                             w_in0[e], sinks0[e], w_out0[e], ln_g[layer], ln_b[layer])
            asm.append(a_new)
            for g in range(N_B_GROUPS):
                bsm[g].append(b_new[g])
        else:
            yp, c_new = layer1_prompt(yp, w_in1[e], b_gates1[e], mh_norm1[e], w_out1[e], ln_g[layer], ln_b[layer])
            for i in range(3):
                cp[i].append(c_new[i])
            ys, c_new = layer1_sample(ys, state_c_mem[e], state_c_norm[e], state_c_max[e], w_in1[e], b_gates1[e],
                                      mh_norm1[e], w_out1[e], ln_g[layer], ln_b[layer])
            for i in range(3):
                csm[i].append(c_new[i])
    return (yp, ys,
            jnp.stack(ap, 0), jnp.stack(bp[0], 0), jnp.stack(bp[1], 0), jnp.stack(bp[2], 0),
            jnp.stack(cp[0], 0), jnp.stack(cp[1], 0), jnp.stack(cp[2], 0),
            jnp.stack(asm, 0), jnp.stack(bsm[0], 0), jnp.stack(bsm[1], 0), jnp.stack(bsm[2], 0),
            jnp.stack(csm[0], 0), jnp.stack(csm[1], 0), jnp.stack(csm[2], 0))
```

```python
import contextlib
import numpy as np
import concourse.bass as bass
import concourse.mybir as mybir
from concourse.bass_utils import run_bass_kernel_spmd

F32 = mybir.dt.float32
BF = mybir.dt.bfloat16
AF = mybir.ActivationFunctionType
ALU = mybir.AluOpType
AX = mybir.AxisListType

ENGS = ("tensor", "scalar", "vector", "gpsimd", "sync")


class Op:
    __slots__ = ("eng", "fn", "deps", "is_dma", "idx", "signal", "sem", "val", "prev_same_sem")

    def __init__(self, eng, fn, is_dma):
        self.eng = eng
        self.fn = fn
        self.is_dma = is_dma
        self.deps = set()
        self.signal = False
        self.sem = None
        self.val = 0
        self.prev_same_sem = None


class Res:
    __slots__ = ("w", "r_eng", "r_dma")

    def __init__(self):
        self.w = None
        self.r_eng = {}
        self.r_dma = []


class Prog:
    def __init__(self, nc, ndma_sems=12):
        self.nc = nc
        self.ops = []
        self.res = {}
        self.ndma = ndma_sems

    def _res(self, k):
        r = self.res.get(k)
        if r is None:
            r = self.res[k] = Res()
        return r

    def add(self, eng, fn, reads=(), writes=(), dma=False):
        op = Op(eng, fn, dma)
        op.idx = len(self.ops)
        for k in reads:
            r = self._res(k)
            if r.w is not None:
                op.deps.add(r.w)
        for k in writes:
            r = self._res(k)
            if r.w is not None:
                op.deps.add(r.w)
            for o in r.r_eng.values():
                op.deps.add(o)
            for o in r.r_dma:
                op.deps.add(o)
        for k in reads:
            r = self._res(k)
            if dma:
                r.r_dma.append(op)
            else:
                r.r_eng[eng] = op
        for k in writes:
            r = self._res(k)
            r.w = op
            r.r_eng = {}
            r.r_dma = []
        op.deps.discard(op)
        self.ops.append(op)
        return op

    def fence(self, fns):
        prev = list(self.ops)
        last = {}
        dmas = []
        for o in prev:
            if o.is_dma:
                dmas.append(o)
            else:
                last[o.eng] = o
        for eng, fn in fns.items():
            op = Op(eng, fn, eng == "sync")
            op.idx = len(self.ops)
            op.deps = set(last.values()) | set(dmas)
            self.ops.append(op)

    def dma(self, q, out, in_, reads=(), writes=(), **kw):
        return self.add(q, lambda e: e.dma_start(out=out, in_=in_, **kw), reads, writes, dma=True)

    def emit(self, stack):
        nc = self.nc
        ops = self.ops
        for op in ops:
            for d in op.deps:
                if d.is_dma or d.eng != "tensor" or op.eng != "tensor" or op.is_dma:
                    d.signal = True
        for op in ops:
            if op.is_dma:
                op.signal = True
        esem = {e: stack.enter_context(nc.semaphore("s_" + e)) for e in ENGS}
        dsem = {e: [stack.enter_context(nc.semaphore("d_%s_%d" % (e, i))) for i in range(self.ndma)]
                for e in ENGS if any(o.is_dma and o.eng == e for o in ops)}
        ecount = {e: 0 for e in ENGS}
        dcount = {e: [0] * self.ndma for e in dsem}
        dlast = {e: [None] * self.ndma for e in dsem}
        drr = {e: 0 for e in dsem}
        for op in ops:
            if op.is_dma:
                i = drr[op.eng]
                drr[op.eng] = (i + 1) % self.ndma
                op.sem = dsem[op.eng][i]
                dcount[op.eng][i] += 16
                op.val = dcount[op.eng][i]
                op.prev_same_sem = dlast[op.eng][i]
                dlast[op.eng][i] = op
            elif op.signal:
                ecount[op.eng] += 1
                op.sem = esem[op.eng]
                op.val = ecount[op.eng]
        per_eng = {e: [o for o in ops if o.eng == e] for e in ENGS}
        all_dmas = [o for o in ops if o.is_dma]
        block = stack.enter_context(nc.Block())

        def make(ename):
            def body(e):
                waited = {}
                for op in per_eng[ename]:
                    need = {}
                    deps = set(op.deps)
                    if op.is_dma and op.prev_same_sem is not None:
                        deps.add(op.prev_same_sem)
                    for d in deps:
                        if (not d.is_dma) and d.eng == "tensor" and ename == "tensor" and not op.is_dma:
                            continue
                        key = id(d.sem)
                        if waited.get(key, 0) >= d.val:
                            continue
                        if key not in need or need[key][1] < d.val:
                            need[key] = (d.sem, d.val)
                    for key, (s, v) in need.items():
                        e.wait_ge(s, v)
                        waited[key] = v
                    ins = op.fn(e)
                    if op.signal:
                        ins.then_inc(op.sem, 16 if op.is_dma else 1)
                if ename == "sync":
                    fin = {}
                    for o in all_dmas:
                        k = id(o.sem)
                        if k not in fin or fin[k][1] < o.val:
                            fin[k] = (o.sem, o.val)
                    for en in ENGS:
                        if ecount[en] > 0:
                            fin[id(esem[en])] = (esem[en], ecount[en])
                    for k, (s, v) in fin.items():
                        if waited.get(k, 0) < v:
                            e.wait_ge(s, v)
            return body

        for ename in ENGS:
            if per_eng[ename] or ename == "sync":
                getattr(block, ename)(make(ename))


D = 1024
IN0 = 4096
IN1 = 5128
ALPHA = 4.0 ** 0.25
LN_EPS = 1e-5
MH_EPS = 1e-6
PAST = 16384
THETA = 500000.0
NEG = -1e30
GROUPS = ("A", "b0", "b1", "b2")
DIL = (1, 1, 4, 16)
RING = (3, 3, 6, 18)
QBASE = (0, 768, 1280, 1792)
KBASE = (512, 2304, 2432, 2560)
VBASE = (640, 2688, 2816, 2944)
CACHE_L = (128, 128, 512, 2048)
MIDX = {1: (0, None, 1), 4: (2, 3, 4), 16: (5, 6, 7)}


_LIMIT = [None]


def host_consts(NT, NS):
    TS = 4 * NS
    c = {}
    k = np.arange(128)[:, None]
    q = np.arange(128)[None, :]
    m = np.zeros((128, 8, 128), np.float32)
    m[:, 0] = (k <= q)
    m[:, 1] = (k >= q)
    for d, (a, b, l) in ((4, MIDX[4]), (16, MIDX[16])):
        cong = ((q - k) % d == 0)
        m[:, a] = cong & (k <= q)
        m[:, b] = cong
        m[:, l] = cong & (k >= q)
    c["maskt"] = m
    half = 8
    inv = THETA ** (-np.arange(half, dtype=np.float32) / half)
    pos = (np.arange(NT)[None, :] * 128 + np.arange(128)[:, None]).astype(np.float32)
    ang = pos[:, :, None] * inv[None, None, :]
    c["cs"] = np.concatenate([np.cos(ang), np.sin(ang)], -1).astype(np.float32)
    tok = np.arange(TS)
    pos_s = (PAST + tok % 4).astype(np.float32)
    ang = pos_s[:, None] * inv[None, :]
    c["cs_s"] = np.concatenate([np.cos(ang), np.sin(ang)], -1).astype(np.float32)[:, None, :]
    c["ident"] = np.eye(128, dtype=np.float32)
    same = (tok[:, None] // 4 == tok[None, :] // 4)
    mn = np.zeros((TS, 2, TS), np.float32)
    mn[:, 0] = same & (tok[:, None] <= tok[None, :])
    mn[:, 1] = (tok[:, None] == tok[None, :])
    c["masknew"] = mn
    mc = np.zeros((128, 10, 4, 4), np.float32)
    rows = np.arange(128)[:, None, None]
    ii = np.arange(4)[None, None, :]
    mc[:, 0] = (rows >= ii)
    mc[:, 1] = (rows >= ii)
    for r in range(4):
        mc[:, 2 + r] = (ii == r)
        mc[:, 6 + r] = (ii == r)
    c["maskc"] = mc.reshape(128, 10, 16)
    t = np.arange(128)
    c["U128"] = (t[:, None] <= t[None, :]).astype(np.float32)
    c["cmask128"] = np.where(t[None, :] <= t[:, None], 0.0, NEG).astype(np.float32)
    c["sellast128"] = np.tile((t[:, None] == 127).astype(np.float32), (1, 128))
    ts_ = np.arange(TS)
    sames = (ts_[:, None] // 4 == ts_[None, :] // 4)
    c["U_s"] = (sames & (ts_[:, None] <= ts_[None, :])).astype(np.float32)
    c["cmask_s"] = np.where(sames & (ts_[None, :] <= ts_[:, None]), 0.0, NEG).astype(np.float32)
    c["selend_s"] = (sames & (ts_[:, None] % 4 == 3)).astype(np.float32)
    seln = np.zeros((TS, NS, 128), np.float32)
    for n in range(NS):
        seln[4 * n, n, :] = 1.0
    oh = np.zeros((TS, NS), np.float32)
    oh[ts_, ts_ // 4] = 1.0
    c["onehot_s"] = oh
    c["rowmask_s"] = np.tile(oh.T[None, :, :], (128, 1, 1)).astype(np.float32)
    c["G_s"] = oh.copy()
    e4 = np.zeros((4, 4, 128), np.float32)
    for h in range(4):
        e4[h, h, :] = 1.0
    c["E4"] = e4
    return c


def build(NT, NS):
    TS = 4 * NS
    nc = bass.Bass("TRN2", target_bir_lowering=False)
    consts = host_consts(NT, NS)
    dr = {}

    def din(name, shape, dt=F32):
        dr[name] = nc.dram_tensor(name, list(shape), dt, kind="ExternalInput").ap()
        return dr[name]

    def dout(name, shape):
        dr[name] = nc.dram_tensor(name, list(shape), F32, kind="ExternalOutput").ap()
        return dr[name]

    S = NT * 128
    xp = din("xp", [S, D])
    xs = din("xs", [TS, D])
    caches = [din("c%d" % g, [NS, CACHE_L[g], 256]) for g in range(4)]
    cmem = din("cmem", [NS, 4, 256, 256])
    cnorm = din("cnorm", [NS, 1024])
    cmax = din("cmax", [NS, 4])
    w_in0 = din("w_in0", [D, IN0])
    w_out0 = din("w_out0", [D, D])
    w_in1 = din("w_in1", [D, IN1])
    w_out1 = din("w_out1", [D, D])
    sink_bc = din("sink_bc", [128, 8])
    bg_bc = din("bg_bc", [128, 8])
    mh_bc = din("mh_bc", [128, 1024])
    lng_bc = din("lng_bc", [128, 2, 1024])
    lnb_bc = din("lnb_bc", [128, 2, 1024])
    cd = {k: din("k_" + k, v.shape) for k, v in consts.items()}

    yp = dout("yp", [S, D])
    ys = dout("ys", [TS, D])
    kvp = [dout("kvp%d" % g, [min(CACHE_L[g], S), 256]) for g in range(4)]
    cmem_p = dout("cmem_p", [4, 256, 256])
    cnorm_p = dout("cnorm_p", [4, 256])
    cmax_p = dout("cmax_p", [1, 4])
    kvs = [dout("kvs%d" % g, [NS, CACHE_L[g], 256]) for g in range(4)]
    cmem_s = dout("cmem_s", [NS, 4, 256, 256])
    cnorm_s = dout("cnorm_s", [NS, 1024])
    cmax_s = dout("cmax_s", [TS, 4])
    x1d = dout("x1d", [S + TS, D])

    st = contextlib.ExitStack()
    P = Prog(nc)

    cur = {"st": st}

    def sb(name, shape, dt=F32):
        return cur["st"].enter_context(nc.sbuf_tensor(name, list(shape), dt))

    fsc = st.enter_context(nc.sbuf_tensor("fsc", [128, 8], F32))

    P.add("gpsimd", lambda e: e.memset(fsc[:], 0.0), writes=["fsc"])

    def fence():
        P.fence({"scalar": lambda e: e.activation(out=fsc[0:1, 0:1], in_=fsc[0:1, 1:2], func=AF.Copy),
                 "vector": lambda e: e.memset(fsc[0:1, 2:3], 0.0),
                 "gpsimd": lambda e: e.memset(fsc[0:1, 4:5], 0.0),
                 "sync": lambda e: e.dma_start(out=fsc[0:1, 6:7], in_=fsc[0:1, 7:8])})

    ps = st.enter_context(nc.psum_tensor("ps", [128, 4096], F32))
    psb = ps.bitcast(BF)

    def PB(i, n=512, off=0, T=128):
        return ps[0:T, i * 512 + off: i * 512 + off + n]

    def PBb(i, n=1024, off=0, T=128):
        return psb[0:T, i * 1024 + off: i * 1024 + off + n]

    Warena = sb("Warena", [128, 8, IN1], BF)
    Woarena = sb("Woarena", [128, 8, D], BF)
    ident_f = sb("ident_f", [128, 128])
    ident_b = sb("ident_b", [128, 128], BF)
    lng = sb("lng", [128, 1024])
    lnb = sb("lnb", [128, 1024])
    sinkexp = sb("sinkexp", [128, 8])
    P.dma("sync", ident_f[:], cd["ident"], writes=["ident_f"])
    P.dma("gpsimd", ident_b[:], cd["ident"], writes=["ident_b"])
    P.dma("sync", lng[:], lng_bc[:, 0, :], writes=["lng"])
    P.dma("sync", lnb[:], lnb_bc[:, 0, :], writes=["lnb"])
    P.dma("sync", sinkexp[:], sink_bc, writes=["sinkexp"])
    P.add("scalar", lambda e: e.activation(out=sinkexp[:], in_=sinkexp[:], func=AF.Exp), reads=["sinkexp"], writes=["sinkexp"])

    def load_W(w_in, ncols):
        wv = w_in.rearrange("(c p) n -> p c n", p=128)
        ncg = (ncols + 511) // 512
        for cg in range(ncg):
            n = min(512, ncols - cg * 512)
            P.dma("gpsimd", Warena[:, :, cg * 512:cg * 512 + n], wv[:, :, cg * 512:cg * 512 + n], reads=[], writes=[("W", cg)])

    def load_Wo(w_out):
        wo = w_out.rearrange("(c p) n -> p c n", p=128)
        for c in range(8):
            P.dma("gpsimd", Woarena[:, c, :], wo[:, c, :], reads=[], writes=["Wo"])

    xin = [sb("xin0", [128, D])] * 2
    xin = list(xin)
    xb = sb("xb", [128, D], BF)
    xT = sb("xT", [128, 8, 128], BF)
    H = sb("H", [128, 3072])
    Hb = sb("Hb", [128, 3072], BF)
    sz0 = sb("sz0", [128, D])
    sz = [sz0, sz0]
    qT = sb("qT", [128, 16, 128], BF)
    Eb = [sb("Eb%d" % i, [128, 512], BF) for i in range(2)]
    Pm = [sb("Pm%d" % i, [128, 512], BF) for i in range(2)]
    dn = sb("dn", [128, 8])
    rd = sb("rd", [128, 8])
    mixpre = sb("mixpre", [128, D])
    mixb = sb("mixb", [128, D], BF)
    mixT = sb("mixT", [128, 8, 128], BF)
    rbuf = sb("rbuf", [128, D])
    stt = sb("stt", [128, 2, 6])
    mv = sb("mv", [128, 2])
    rstd = sb("rstd", [128, 1])
    stA = contextlib.ExitStack()
    cur["st"] = stA
    maskt = sb("maskt", [128, 8, 128], BF)
    cs = sb("cs", [128, NT, 16])
    kT = [sb("kT%d" % g, [128, RING[g], 128], BF) for g in range(4)]
    Vr = [sb("Vr%d" % g, [128, RING[g], 2, 65], BF) for g in range(4)]
    rtmp = sb("rtmpA", [128, 4, 30, 8])
    xin[1] = sb("xinA1", [128, D])
    sz[1] = sb("szA1", [128, D])
    for _i in (2, 3, 4):
        Eb.append(sb("EbA%d" % _i, [128, 512], BF))
        Pm.append(sb("PmA%d" % _i, [128, 512], BF))
    P.dma("gpsimd", maskt[:], cd["maskt"], writes=["maskt"])
    P.dma("sync", cs[:], cd["cs"], writes=["cs"])
    for g in range(4):
        P.add("gpsimd", lambda e, g=g: e.memset(Vr[g][:].rearrange("p a b c -> p (a b c)"), 1.0), writes=[("V", g, s) for s in range(RING[g])])

    cnt = {"s": 0, "m": 0}

    def bc_mid(t, off, pstep, T, n_mid, n_in):
        return bass.AP(t[:].tensor, off, [[pstep, T], [0, n_mid], [1, n_in]])

    def bc_last(t, off, pstep, T, n_mid, mid_step, n_in):
        return bass.AP(t[:].tensor, off, [[pstep, T], [mid_step, n_mid], [0, n_in]])

    def layer_norm(T, layer, out_ap):
        for i in range(2):
            P.add("vector", lambda e, i=i: e.bn_stats(out=stt[0:T, i, :], in_=rbuf[0:T, i * 512:(i + 1) * 512]), reads=["rbuf"], writes=["stt"])
        P.add("vector", lambda e: e.bn_aggr(out=mv[0:T, :], in_=stt[0:T].rearrange("p a b -> p (a b)")), reads=["stt"], writes=["mv"])
        P.add("vector", lambda e: e.tensor_scalar(out=rstd[0:T, :], in0=mv[0:T, 1:2], scalar1=LN_EPS, scalar2=None, op0=ALU.add), reads=["mv"], writes=["rstd"])
        P.add("scalar", lambda e: e.activation(out=rstd[0:T, :], in_=rstd[0:T, :], func=AF.Sqrt), reads=["rstd"], writes=["rstd"])
        P.add("vector", lambda e: e.reciprocal(out=rstd[0:T, :], in_=rstd[0:T, :]), reads=["rstd"], writes=["rstd"])
        P.add("vector", lambda e: e.tensor_scalar(out=rbuf[0:T, :], in0=rbuf[0:T, :], scalar1=mv[0:T, 0:1], scalar2=rstd[0:T, 0:1], op0=ALU.subtract, op1=ALU.mult), reads=["rbuf", "mv", "rstd"], writes=["rbuf"])
        P.add("vector", lambda e: e.tensor_tensor(out=rbuf[0:T, :], in0=rbuf[0:T, :], in1=lng[0:T, :], op=ALU.mult), reads=["rbuf", "lng"], writes=["rbuf"])
        return P.add("vector", lambda e: e.tensor_tensor(out=out_ap, in0=rbuf[0:T, :], in1=lnb[0:T, :], op=ALU.add), reads=["rbuf", "lnb"], writes=["rbuf"])

    def transpose_in(T, src_key, src_ap_fn, n, dst_ap, dst_key, bank=2, W=128):
        for c in range(n):
            P.add("tensor", lambda e, c=c: e.transpose(out=psb[0:W, bank * 1024 + c * T: bank * 1024 + (c + 1) * T], in_=src_ap_fn(c), identity=ident_b[0:T, 0:T]),
                  reads=[src_key, "ident_b"], writes=[("pb", bank)])
        eng = "vector" if cnt["m"] % 2 == 0 else "scalar"
        cnt["m"] += 1
        src = psb[0:W, bank * 1024: bank * 1024 + n * T]
        if eng == "vector":
            P.add("vector", lambda e: e.tensor_copy(out=dst_ap, in_=src), reads=[("pb", bank)], writes=[dst_key])
        else:
            P.add("scalar", lambda e: e.activation(out=dst_ap, in_=src, func=AF.Copy), reads=[("pb", bank)], writes=[dst_key])

    bg = []

    def drain(k=None):
        n = len(bg) if k is None else min(k, len(bg))
        for _ in range(n):
            bg.pop(0)()

    def load_and_project(T, x_ap, s2, W_cols, evac, defer=False, atomic=False):
        th = []
        xi = xin[s2]
        th.append(lambda: P.dma("sync", xi[0:T, :], x_ap, writes=[("xin", s2)]))
        th.append(lambda: P.add("gpsimd", lambda e: e.tensor_copy(out=xb[0:T, :], in_=xi[0:T, :]), reads=[("xin", s2)], writes=["xb"]))
        th.append(lambda: transpose_in(T, "xb", lambda c: xb[0:T, c * 128:(c + 1) * 128], 8, xT[:, :, 0:T], "xT"))
        ncg = (W_cols + 511) // 512
        for cg in range(ncg):
            n = min(512, W_cols - cg * 512)
            bank = cg % 2
            for c0 in range(0, 8, 4):
                def mm4(cg=cg, n=n, bank=bank, c0=c0):
                    for c in range(c0, c0 + 4):
                        P.add("tensor", lambda e, c=c: e.matmul(PB(bank, n, 0, T), lhsT=xT[:, c, 0:T], rhs=Warena[:, c, cg * 512: cg * 512 + n], start=(c == 0), stop=(c == 7)),
                              reads=["xT", ("W", cg)], writes=[("pb", bank)])
                th.append(mm4)
            th.append(lambda cg=cg, bank=bank, n=n: evac(cg, bank, n))
            if atomic:
                grp = th[-3:]
                del th[-3:]
                th.append(lambda grp=grp: [f() for f in grp])
        if defer:
            bg.extend(th)
        else:
            for f in th:
                f()

    def rope(T, base, nh, cs_ap_c, cs_ap_s):
        Hr = H[0:T, base: base + nh * 64].rearrange("p (h d) -> p h d", d=64)
        x1 = Hr[:, :, 0:8]
        x2 = Hr[:, :, 8:16]
        tm = [rtmp[0:T, i, 0:nh, :] for i in range(4)]
        P.add("vector", lambda e: e.tensor_tensor(out=tm[0], in0=x1, in1=cs_ap_c, op=ALU.mult), reads=["H", "cs"], writes=["rtmp"])
        P.add("vector", lambda e: e.tensor_tensor(out=tm[1], in0=x2, in1=cs_ap_s, op=ALU.mult), reads=["H", "cs"], writes=["rtmp"])
        P.add("vector", lambda e: e.tensor_tensor(out=tm[2], in0=x2, in1=cs_ap_c, op=ALU.mult), reads=["H", "cs"], writes=["rtmp2"])
        P.add("vector", lambda e: e.tensor_tensor(out=tm[3], in0=x1, in1=cs_ap_s, op=ALU.mult), reads=["H", "cs"], writes=["rtmp2"])
        P.add("vector", lambda e: e.tensor_tensor(out=x1, in0=tm[0], in1=tm[1], op=ALU.subtract), reads=["rtmp", "rtmp2"], writes=["H"])
        P.add("vector", lambda e: e.tensor_tensor(out=x2, in0=tm[2], in1=tm[3], op=ALU.add), reads=["rtmp", "rtmp2"], writes=["H"])

    def l0_front_a(T, x_ap, s2, defer=False):
        szb = sz[s2]

        def evac(cg, bank, n):
            if cg < 6:
                if cg % 2 == 0:
                    P.add("scalar", lambda e: e.activation(out=H[0:T, cg * 512:(cg + 1) * 512], in_=PB(bank, 512, 0, T), func=AF.Copy), reads=[("pb", bank)], writes=["H"])
                else:
                    P.add("vector", lambda e: e.tensor_copy(out=H[0:T, cg * 512:(cg + 1) * 512], in_=PB(bank, 512, 0, T)), reads=[("pb", bank)], writes=["H"])
            else:
                P.add("scalar", lambda e: e.activation(out=szb[0:T, (cg - 6) * 512:(cg - 5) * 512], in_=PB(bank, 512, 0, T), func=AF.Silu), reads=[("pb", bank)], writes=[("sz", s2)])
        load_and_project(T, x_ap, s2, IN0, evac, defer)

    def l0_front_b(T, csc, css):
        rope(T, 0, 10, csc(10), css(10))
        rope(T, 768, 30, csc(30), css(30))
        for G in range(4):
            qo = Hb[0:T, QBASE[G]:QBASE[G] + 512].rearrange("p (j hk d) -> p hk j d", hk=2, d=64)
            qi = H[0:T, QBASE[G]:QBASE[G] + 512].rearrange("p (hk j d) -> p hk j d", hk=2, d=64)
            if G % 2 == 0:
                P.add("gpsimd", lambda e, qo=qo, qi=qi: e.tensor_copy(out=qo, in_=qi), reads=["H"], writes=["Hb"])
            else:
                P.add("scalar", lambda e, qo=qo, qi=qi: e.activation(out=qo, in_=qi, func=AF.Copy), reads=["H"], writes=["Hb"])
        P.add("scalar", lambda e: e.activation(out=Hb[0:T, 512:640], in_=H[0:T, 512:640], func=AF.Copy), reads=["H"], writes=["Hb"])
        P.add("vector", lambda e: e.tensor_copy(out=Hb[0:T, 2304:2688], in_=H[0:T, 2304:2688]), reads=["H"], writes=["Hb"])

    def l0_front(T, x_ap, s2, csc, css):
        l0_front_a(T, x_ap, s2)
        l0_front_b(T, csc, css)

    def q_src(T, G, j):
        b = QBASE[G] + 128 * j
        return Hb[0:T, b:b + 128]

    def l0_back(T, s2, row0, out_dram, acc_phase, defer_tail=False):
        for ph in range(2):
            acc_phase(ph)
            den_ap = bass.AP(ps[:].tensor, 5 * 512 + 64, [[4096, T], [128, 8], [1, 1]])
            if ph == 0:
                P.add("vector", lambda e: e.tensor_tensor(out=dn[0:T, :], in0=den_ap, in1=sinkexp[0:T, :], op=ALU.add), reads=[("pb", 5), ("pb", 6), "sinkexp"], writes=["dn"])
            else:
                P.add("vector", lambda e: e.tensor_copy(out=dn[0:T, :], in_=den_ap), reads=[("pb", 5), ("pb", 6)], writes=["dn"])
            P.add("vector", lambda e: e.reciprocal(out=rd[0:T, :], in_=dn[0:T, :]), reads=["dn"], writes=["rd"])
            num_ap = bass.AP(ps[:].tensor, 5 * 512, [[4096, T], [128, 8], [1, 64]])
            rd_bc = bc_last(rd, 0, 8, T, 8, 1, 64)
            mp = mixpre[0:T, ph * 512:(ph + 1) * 512].rearrange("p (h d) -> p h d", d=64)
            P.add("vector", lambda e, mp=mp: e.tensor_tensor(out=mp, in0=num_ap, in1=rd_bc, op=ALU.mult), reads=[("pb", 5), ("pb", 6), "rd"], writes=["mixpre"])
        szb = sz[s2]
        P.add("gpsimd", lambda e: e.tensor_tensor(out=mixb[0:T, :], in0=mixpre[0:T, :], in1=szb[0:T, :], op=ALU.mult), reads=["mixpre", ("sz", s2)], writes=["mixb"])
        tail(T, s2, 0, out_dram, defer_tail)

    def tail(T, s2, layer, out_dram, defer=False, atomic=False):
        xi = xin[s2]
        th = []
        th.append(lambda: transpose_in(T, "mixb", lambda c: mixb[0:T, c * 128:(c + 1) * 128], 8, mixT[:, :, 0:T], "mixT"))
        for n in range(2):
            for c0 in (0, 4):
                def mm4(n=n, c0=c0):
                    for c in range(c0, c0 + 4):
                        P.add("tensor", lambda e, c=c: e.matmul(PB(n, 512, 0, T), lhsT=mixT[:, c, 0:T], rhs=Woarena[:, c, n * 512:(n + 1) * 512], start=(c == 0), stop=(c == 7)),
                              reads=["mixT", "Wo"], writes=[("pb", n)])
                th.append(mm4)
            th.append(lambda n=n: P.add("vector", lambda e: e.scalar_tensor_tensor(out=rbuf[0:T, n * 512:(n + 1) * 512], in0=xi[0:T, n * 512:(n + 1) * 512], scalar=ALPHA, in1=PB(n, 512, 0, T), op0=ALU.mult, op1=ALU.add),
                                        reads=[("xin", s2), ("pb", n)], writes=["rbuf"]))
            if atomic:
                grp = th[-3:]
                del th[-3:]
                th.append(lambda grp=grp: [f() for f in grp])
        th.append(lambda: layer_norm(T, layer, rbuf[0:T, :]))
        th.append(lambda: P.dma("gpsimd", out_dram, rbuf[0:T, :], reads=["rbuf"], writes=["x1d"] if layer == 0 else []))
        if defer:
            bg.extend(th)
        else:
            for f in th:
                f()

    load_W(w_in0, IN0)
    load_Wo(w_out0)

    SB = (3, 4, 7)

    def front_b_prompt(t):
        csc = lambda nh, t=t: bc_mid(cs, t * 16, NT * 16, 128, nh, 8)
        css = lambda nh, t=t: bc_mid(cs, t * 16 + 8, NT * 16, 128, nh, 8)
        l0_front_b(128, csc, css)
        for g in range(4):
            slot = t % RING[g]
            P.add("scalar", lambda e, g=g, slot=slot: e.activation(out=Vr[g][:, slot, :, 0:64], in_=H[:, VBASE[g]: VBASE[g] + 128].rearrange("p (a b) -> p a b", b=64), func=AF.Copy),
                  reads=["H"], writes=[("V", g, slot)])
            L = min(CACHE_L[g], S)
            r0 = t * 128 - (S - L)
            if r0 >= 0:
                kv_out = kvp[g][r0:r0 + 128, :]
                P.dma("gpsimd", kv_out[:, 0:128], H[:, KBASE[g]:KBASE[g] + 128], reads=["H"])
                P.dma("gpsimd", kv_out[:, 128:256], H[:, VBASE[g]:VBASE[g] + 128], reads=["H"])
        for b0 in range(0, 16, 8):
            transpose_in(128, "Hb", lambda c, b0=b0: q_src(128, (b0 + c) // 4, (b0 + c) % 4), 8,
                         qT[:, b0:b0 + 8, :], "qT")
        for g in range(4):
            slot = t % RING[g]
            transpose_in(128, "Hb", lambda c, g=g: Hb[:, KBASE[g]:KBASE[g] + 128], 1, kT[g][:, slot, :], ("kT", g, slot))

    l0_front_a(128, xp[0:128, :], 0)
    front_b_prompt(0)
    for t in range(NT):
        s2 = t % 2
        if t + 1 < NT:
            l0_front_a(128, xp[(t + 1) * 128:(t + 2) * 128, :], (t + 1) % 2, defer=True)

        def acc_phase(ph, t=t):
            groups = (0,) if ph == 0 else (1, 2, 3)
            plan = []
            for G in groups:
                d = DIL[G]
                for hk in range(2):
                    for o in range(d + 1):
                        if t - o < 0:
                            continue
                        mi = MIDX[d][0] if o == 0 else (MIDX[d][2] if o == d else MIDX[d][1])
                        plan.append((G, hk, o, mi))
            first = {}
            for i, (G, hk, o, mi) in enumerate(plan):
                first.setdefault(hk, i)
            bufs = {}

            def issue_S(i):
                G, hk, o, mi = plan[i]
                k3 = cnt["s"] % 5
                sbk = SB[cnt["s"] % 3]
                cnt["s"] += 1
                bufs[i] = k3
                Ek = Eb[k3]
                Pk = Pm[k3]
                slot = (t - o) % RING[G]
                P.add("tensor", lambda e: e.matmul(PB(sbk), lhsT=kT[G][64 * hk:64 * hk + 64, slot, :], rhs=qT[64 * hk:64 * hk + 64, 4 * G:4 * G + 4, :], start=True, stop=True),
                      reads=[("kT", G, slot), "qT"], writes=[("pb", sbk)])
                P.add("scalar", lambda e: e.activation(out=Ek[:], in_=PB(sbk), func=AF.Exp, scale=0.125), reads=[("pb", sbk)], writes=[("E", k3)])
                meng = "vector"
                mk = bc_mid(maskt, mi * 128, 8 * 128, 128, 4, 128)
                P.add(meng, lambda e: e.tensor_tensor(out=Pk[:], in0=Ek[:], in1=mk, op=ALU.mult), reads=[("E", k3), "maskt"], writes=[("P", k3)])

            def issue_PV(i):
                G, hk, o, mi = plan[i]
                k3 = bufs[i]
                Pk = Pm[k3]
                slot = (t - o) % RING[G]
                for j in range(4):
                    P.add("tensor", lambda e, j=j: e.matmul(PB(5 + hk, 65, j * 128), lhsT=Pk[:, j * 128:(j + 1) * 128], rhs=Vr[G][:, slot, hk, :], start=(i == first[hk] and j == 0), stop=False, skip_group_check=True),
                          reads=[("P", k3), ("V", G, slot)], writes=[("pb", 5 + hk)])

            LA = 3
            for i in range(min(LA, len(plan))):
                issue_S(i)
            for i in range(len(plan)):
                issue_PV(i)
                if i + LA < len(plan):
                    issue_S(i + LA)
                drain(2)
            if ph == 1:
                drain()

        l0_back(128, s2, t * 128, x1d[t * 128:(t + 1) * 128, :], acc_phase, defer_tail=True)
        if t + 1 < NT:
            front_b_prompt(t + 1)
    drain()
    xin[1] = xin[0]
    sz[1] = sz[0]
    del Eb[2:]
    del Pm[2:]

    fence()
    stA.close()
    stB = contextlib.ExitStack()
    cur["st"] = stB
    rtmp = sb("rtmpB", [TS, 4, 30, 8])
    cs_s = sb("cs_s", [TS, 1, 16])
    P.dma("sync", cs_s[:], cd["cs_s"], writes=["cs_s", "cs"])
    masknew = sb("masknew", [TS, 2, TS], BF)
    maskc = sb("maskc", [128, 10, 16], BF)
    qTs = sb("qTs", [128, 16, TS], BF)
    kTs = sb("kTs", [128, 4, TS], BF)
    Vs = sb("Vs", [TS, 8, 65], BF)
    ctile = [sb("ctile%d" % i, [128, 10, 256]) for i in range(2)]
    kcb = sb("kcb", [128, 10, 128], BF)
    vca = sb("vca", [128, 10, 2, 65], BF)
    kcT = sb("kcT", [128, 10, 128], BF)
    Pc = sb("Pc", [128, 320], BF)
    oT = sb("oT", [65, 4, 4 * TS])
    P.dma("gpsimd", masknew[:], cd["masknew"], writes=["masknew"])
    P.dma("gpsimd", maskc[:], cd["maskc"], writes=["maskc"])
    P.add("gpsimd", lambda e: e.memset(Vs[:].rearrange("p a b -> p (a b)"), 1.0), writes=["Vs"])
    P.add("gpsimd", lambda e: e.memset(vca[:].rearrange("p a b c -> p (a b c)"), 1.0), writes=[("vca", 0)])

    csc_s = lambda nh: bc_mid(cs_s, 0, 16, TS, nh, 8)
    css_s = lambda nh: bc_mid(cs_s, 8, 16, TS, nh, 8)
    l0_front(TS, xs[:, :], 0, csc_s, css_s)
    load_W(w_in1, IN1)
    for g in range(4):
        P.add("gpsimd", lambda e, g=g: e.tensor_copy(out=Vs[:, 2 * g:2 * g + 2, 0:64], in_=H[0:TS, VBASE[g]: VBASE[g] + 128].rearrange("p (a b) -> p a b", b=64)),
              reads=["H"], writes=["Vs"])
        Lg = CACHE_L[g]
        dst = bass.AP(kvs[g].tensor, (Lg - 4) * 256, [[Lg * 256, NS], [256, 4], [1, 128]])
        P.dma("gpsimd", dst, H[0:TS, KBASE[g]:KBASE[g] + 128], reads=["H"])
        dst2 = bass.AP(kvs[g].tensor, (Lg - 4) * 256 + 128, [[Lg * 256, NS], [256, 4], [1, 128]])
        P.dma("gpsimd", dst2, H[0:TS, VBASE[g]:VBASE[g] + 128], reads=["H"])
        for n in range(NS):
            P.dma("sync", kvs[g][n, 0:Lg - 4, :], caches[g][n, 4:Lg, :])
    for b0 in range(0, 16, 8):
        transpose_in(TS, "Hb", lambda c, b0=b0: q_src(TS, (b0 + c) // 4, (b0 + c) % 4), 8, qTs[:, b0:b0 + 8, :], "qTs")
    transpose_in(TS, "Hb", lambda c: Hb[0:TS, KBASE[c]:KBASE[c] + 128], 4, kTs[:, :, :], "kTs")

    def accT(ph, hk, n=4 * TS, off=0):
        return ps[0:65, (5 + ph) * 512 + hk * 256 + off: (5 + ph) * 512 + hk * 256 + off + n]

    firstb = {0: True, 1: True}
    for G in range(4):
        ph = 0 if G == 0 else 1
        for hk in range(2):
            sbk = 3 + cnt["s"] % 2
            ei = cnt["s"] % 2
            cnt["s"] += 1
            P.add("tensor", lambda e, G=G, hk=hk, sbk=sbk: e.matmul(PB(sbk, 4 * TS, 0, TS), lhsT=kTs[64 * hk:64 * hk + 64, G, :], rhs=qTs[64 * hk:64 * hk + 64, 4 * G:4 * G + 4, :], start=True, stop=True),
                  reads=["kTs", "qTs"], writes=[("pb", sbk)])
            P.add("scalar", lambda e, sbk=sbk, ei=ei: e.activation(out=Eb[ei][0:TS, 0:4 * TS], in_=PB(sbk, 4 * TS, 0, TS), func=AF.Exp, scale=0.125), reads=[("pb", sbk)], writes=[("E", ei)])
            mk = bc_mid(masknew, (0 if DIL[G] == 1 else 1) * TS, 2 * TS, TS, 4, TS)
            P.add("vector", lambda e, ei=ei, mk=mk: e.tensor_tensor(out=Pm[ei][0:TS, 0:4 * TS], in0=Eb[ei][0:TS, 0:4 * TS], in1=mk, op=ALU.mult), reads=[("E", ei), "masknew"], writes=[("P", ei)])
            st_flag = firstb[ph]
            firstb[ph] = False
            P.add("tensor", lambda e, ei=ei, G=G, hk=hk, ph=ph, st_flag=st_flag: e.matmul(accT(ph, hk), lhsT=Vs[:, 2 * G + hk, :], rhs=Pm[ei][0:TS, 0:4 * TS], start=st_flag, stop=False, skip_group_check=True),
                  reads=[("P", ei), "Vs"], writes=[("pb", 5 + ph)])
    TILES = [(0, 0, 1), (1, 0, 1)] + [(2, r, 4) for r in range(4)] + [(3, r, 16) for r in range(4)]
    kcbs = [kcb, sb("kcb2", [128, 10, 128], BF)]
    vcas = [vca, sb("vca2", [128, 10, 2, 65], BF)]
    kcTs = [kcT, sb("kcT2", [128, 10, 128], BF)]
    Pcs = [Pc, sb("Pc2", [128, 320], BF)]
    vca2_ = vcas[1]
    P.add("gpsimd", lambda e: e.memset(vca2_[:].rearrange("p a b c -> p (a b c)"), 1.0), writes=[("vca", 1)])
    mkc = bass.AP(maskc[:].tensor, 0, [[160, 128], [16, 10], [0, 2], [1, 16]])

    def s_stage1(n):
        cb = n % 2
        kcb_, vca_, kcT_, Pc_, E_ = kcbs[cb], vcas[cb], kcTs[cb], Pcs[cb], Eb[cb]
        ct_ = ctile[cb]
        sbk = 3 + cb
        for ti, (G, r, stp) in enumerate(TILES):
            src = bass.AP(caches[G].tensor, n * CACHE_L[G] * 256 + r * 256, [[stp * 256, 128], [1, 256]])
            P.dma("sync", ct_[:, ti, :], src, writes=[("ctile", cb)])
        P.add("gpsimd", lambda e: e.tensor_copy(out=kcb_[:], in_=ct_[:, :, 0:128]), reads=[("ctile", cb)], writes=[("kcb", cb)])
        P.add("vector", lambda e: e.tensor_copy(out=vca_[:, :, :, 0:64], in_=ct_[:, :, 128:256].rearrange("p t (a b) -> p t a b", b=64)), reads=[("ctile", cb)], writes=[("vca", cb)])
        for b0 in (0, 5):
            transpose_in(128, ("kcb", cb), lambda c, b0=b0: kcb_[:, b0 + c, :], 5, kcT_[:, b0:b0 + 5, :], ("kcT", cb))
        for ti, (G, r, stp) in enumerate(TILES):
            for hk in range(2):
                for j in range(4):
                    P.add("tensor", lambda e, ti=ti, hk=hk, j=j, G=G: e.matmul(PB(sbk, 4, (ti * 2 + hk) * 16 + 4 * j), lhsT=kcT_[64 * hk:64 * hk + 64, ti, :], rhs=qTs[64 * hk:64 * hk + 64, 4 * G + j, 4 * n:4 * n + 4], start=(ti == 0 and hk == 0 and j == 0), stop=False, skip_group_check=True),
                          reads=[("kcT", cb), "qTs"], writes=[("pb", sbk)])
        P.add("scalar", lambda e: e.activation(out=E_[:, 0:320], in_=PB(sbk, 320), func=AF.Exp, scale=0.125), reads=[("pb", sbk)], writes=[("E", cb)])
        P.add("vector", lambda e: e.tensor_tensor(out=Pc_[:].rearrange("p (t h c) -> p t h c", h=2, c=16), in0=E_[:, 0:320].rearrange("p (t h c) -> p t h c", h=2, c=16), in1=mkc, op=ALU.mult), reads=[("E", cb), "maskc"], writes=[("Pc", cb)])

    def s_stage2(n):
        cb = n % 2
        vca_, Pc_ = vcas[cb], Pcs[cb]
        for ti, (G, r, stp) in enumerate(TILES):
            ph = 0 if G == 0 else 1
            for hk in range(2):
                for j in range(4):
                    c0 = (5 + ph) * 512 + hk * 256 + j * TS + 4 * n
                    P.add("tensor", lambda e, ti=ti, hk=hk, j=j, c0=c0: e.matmul(ps[0:65, c0:c0 + 4], lhsT=vca_[:, ti, hk, :], rhs=Pc_[:, (ti * 2 + hk) * 16 + 4 * j:(ti * 2 + hk) * 16 + 4 * j + 4], start=False, stop=False, skip_group_check=True),
                          reads=[("Pc", cb), ("vca", cb)], writes=[("pb", 5 + ph)])

    s_stage1(0)
    for n in range(NS):
        if n + 1 < NS:
            s_stage1(n + 1)
        s_stage2(n)
    accT_all = bass.AP(ps[:].tensor, 5 * 512, [[4096, 65], [512, 2], [256, 2], [1, 4 * TS]])
    P.add("vector", lambda e: e.tensor_copy(out=oT[:].rearrange("p (a b) c -> p a b c", b=2), in_=accT_all), reads=[("pb", 5), ("pb", 6)], writes=["oT"])

    def acc_phase_s(ph):
        for hk in range(2):
            for j in range(4):
                P.add("tensor", lambda e, hk=hk, j=j, ph=ph: e.transpose(out=PB(5 + hk, 65, j * 128, TS), in_=oT[0:65, ph * 2 + hk, j * TS:(j + 1) * TS], identity=ident_f[0:65, 0:65]),
                      reads=["oT", "ident_f"], writes=[("pb", 5 + hk)])

    l0_back(TS, 0, 0, x1d[S:S + TS, :], acc_phase_s)

    fence()
    stB.close()
    cur["st"] = st
    load_Wo(w_out1)
    U128 = sb("U128", [128, 128]); cmask128 = sb("cmask128", [128, 128]); sellast = sb("sellast", [128, 128])
    U_s = sb("U_s", [TS, TS]); cmask_s = sb("cmask_s", [TS, TS]); selend_s = sb("selend_s", [TS, TS])
    E4 = sb("E4", [4, 4, 128]); bgb = sb("bgb", [128, 8]); mhb = sb("mhb", [128, D])
    onehot = sb("onehot", [TS, NS])
    for tl, nm in ((U128, "U128"), (cmask128, "cmask128"), (sellast, "sellast128"), (U_s, "U_s"), (cmask_s, "cmask_s"),
                   (selend_s, "selend_s"), (E4, "E4"), (onehot, "onehot_s")):
        P.dma("sync", tl[:], cd[nm], writes=[nm])
    P.dma("sync", bgb[:], bg_bc, writes=["bgb"])
    P.dma("sync", mhb[:], mh_bc, writes=["mhb"])
    vaug = sb("vaug", [128, 4, 257], BF)
    gt = sb("gt", [128, 8]); sp = sb("sp", [128, 4]); a_t = sb("a_t", [128, 4]); aT = sb("aT", [4, 128])
    Dm = sb("Dm", [128, 4, 128]); pw = sb("pw", [128, 4, 128]); cm = sb("cm", [128, 4]); gg = sb("gg", [128, 4]); negg = sb("negg", [128, 4])
    mpv = sb("mpv", [128, 4]); cw = sb("cw", [128, 4]); mtok = sb("mtok", [128, 4]); emm = sb("emm", [128, 4]); wsb = sb("wsb", [128, 4]); dec = sb("dec", [128, 4])
    tmp4 = sb("tmp4", [128, 4]); dint = sb("dint", [128, 4]); den = sb("den", [128, 4]); rdd = sb("rdd", [128, 4])
    onec = sb("onec", [128, 1]); scb = sb("scb", [128, 4, 128], BF)
    Cst = sb("Cst", [128, 4, 2, 257]); Cb = sb("Cb", [128, 4, 2, 257], BF)
    sth = sb("sth", [128, 4, 6]); mvh = sb("mvh", [128, 4, 2]); rsh = sb("rsh", [128, 4])
    tmpq = rbuf
    P.add("gpsimd", lambda e: e.memset(onec[:], 1.0), writes=["onec"])
    P.add("gpsimd", lambda e: e.memset(vaug[:].rearrange("p a b -> p (a b)"), 1.0), writes=["vaug", ("vaug1", 0)])
    P.add("gpsimd", lambda e: e.memset(Cst[:].rearrange("p a b c -> p (a b c)"), 0.0), writes=["Cst"])
    P.add("gpsimd", lambda e: e.memset(Cb[:].rearrange("p a b c -> p (a b c)"), 0.0), writes=["Cb"])
    P.add("gpsimd", lambda e: e.memset(mtok[:], 0.0), writes=["mtok"])
    hnum = mixpre
    Hs = [H, H]; Hbs = [Hb, Hb]; vaugs = [vaug, vaug]; gts = [gt, gt]

    def mlstm(T, x_ap, s2, out_dram, sample, p=0, mode="all", deftail=False):
        H_ = Hs[p]; Hb_ = Hbs[p]; sz_ = sz[p]; vaug_ = vaugs[p]; gt_ = gts[p]
        if sample:
            kH, kHb, ksz, kva, kgt = "H", "Hb", ("sz", 0), "vaug", "gt"
        else:
            kH, kHb, ksz, kva, kgt = ("H1", p), ("Hb1", p), ("sz", p), ("vaug1", p), ("gt1", p)
        DR = 4
        def evac(cg, bank, n):
            src = PB(bank, n, 0, T)
            k_ = ("pb", bank)
            if cg < 2:
                if sample:
                    P.add("scalar", lambda e: e.activation(out=Hb_[0:T, cg * 512:(cg + 1) * 512], in_=src, func=AF.Copy), reads=[k_], writes=[kHb])
                else:
                    P.add("vector", lambda e: e.tensor_copy(out=Hb_[0:T, cg * 512:(cg + 1) * 512], in_=src), reads=[k_], writes=[kHb])
                if sample:
                    P.add("scalar", lambda e: e.activation(out=H_[0:T, 2048 + cg * 512: 2048 + (cg + 1) * 512], in_=src, func=AF.Copy), reads=[k_], writes=[kH])
            elif cg < 4:
                c0 = (cg - 2) * 512
                P.add("scalar", lambda e: e.activation(out=H_[0:T, c0:c0 + 512], in_=src, func=AF.Copy, scale=1.0 / 16.0), reads=[k_], writes=[kH])
                P.add("gpsimd", lambda e: e.tensor_copy(out=Hb_[0:T, 1024 + c0:1024 + c0 + 512], in_=H_[0:T, c0:c0 + 512]), reads=[kH], writes=[kHb])
            elif cg < 6:
                h0 = 2 * (cg - 4)
                if sample:
                    P.add("scalar", lambda e: e.activation(out=vaug_[0:T, h0:h0 + 2, 0:256], in_=src.rearrange("p (a b) -> p a b", b=256), func=AF.Copy), reads=[k_], writes=[kva])
                else:
                    P.add("vector", lambda e: e.tensor_copy(out=vaug_[0:T, h0:h0 + 2, 0:256], in_=src.rearrange("p (a b) -> p a b", b=256)), reads=[k_], writes=[kva])
                if sample:
                    P.add("scalar", lambda e: e.activation(out=vf[0:T, h0:h0 + 2, :], in_=src.rearrange("p (a b) -> p a b", b=256), func=AF.Copy), reads=[k_], writes=["vf"])
            elif cg < 8:
                c0 = (cg - 6) * 512
                P.add("scalar", lambda e: e.activation(out=H_[0:T, 1024 + c0:1024 + c0 + 512], in_=src, func=AF.Sigmoid), reads=[k_], writes=[kH])
            elif cg < 10:
                c0 = (cg - 8) * 512
                P.add("scalar", lambda e: e.activation(out=sz_[0:T, c0:c0 + 512], in_=src, func=AF.Silu), reads=[k_], writes=[ksz])
            else:
                P.add("vector", lambda e: e.tensor_tensor(out=gt_[0:T, :], in0=src, in1=bgb[0:T, :], op=ALU.add), reads=[k_, "bgb"], writes=[kgt])
        if mode in ("all", "front"):
            load_and_project(T, x_ap, s2, IN1, evac, defer=(mode == "front"), atomic=True)
        if mode == "front":
            return
        Um, cmk, sel = (U_s, cmask_s, selend_s) if sample else (U128, cmask128, sellast)
        Uk, cmkk, selk = ("U_s", "cmask_s", "selend_s") if sample else ("U128", "cmask128", "sellast128")
        G7 = lambda off, n=4: PB(7, n, off, T)
        P.add("scalar", lambda e: e.activation(out=sp[0:T, :], in_=gt_[0:T, 4:8], func=AF.Exp, scale=-1.0), reads=[kgt], writes=["sp"])
        P.add("scalar", lambda e: e.activation(out=sp[0:T, :], in_=sp[0:T, :], func=AF.Ln, bias=onec[0:T, :], scale=1.0), reads=["sp", "onec"], writes=["sp"])
        P.add("tensor", lambda e: e.matmul(G7(0), lhsT=Um[0:T, 0:T], rhs=sp[0:T, :], start=True, stop=True, skip_group_check=True), reads=["sp", Uk], writes=[("pb", 7)])
        P.add("vector", lambda e: e.tensor_tensor(out=a_t[0:T, :], in0=G7(0), in1=gt_[0:T, 0:4], op=ALU.add), reads=[("pb", 7), kgt], writes=["a_t"])
        if sample:
            P.dma("sync", mpv[0:T, :], dr["cmax_tok"], writes=["mpv"])
        else:
            P.add("tensor", lambda e: e.matmul(G7(8), lhsT=sel[0:T, 0:T], rhs=mtok[0:T, :], start=False, stop=False, skip_group_check=True), reads=["mtok", selk], writes=[("pb", 7)])
            P.add("vector", lambda e: e.tensor_copy(out=mpv[0:T, :], in_=G7(8)), reads=[("pb", 7)], writes=["mpv"])
        P.add("tensor", lambda e: e.transpose(out=ps[0:4, 7 * 512 + 128: 7 * 512 + 128 + T], in_=a_t[0:T, :], identity=ident_f[0:T, 0:T]), reads=["a_t", "ident_f"], writes=[("pb", 7)])
        P.add("vector", lambda e: e.tensor_copy(out=aT[:, 0:T], in_=ps[0:4, 7 * 512 + 128: 7 * 512 + 128 + T]), reads=[("pb", 7)], writes=["aT"])
        drain(DR)
        for h in range(4):
            P.add("tensor", lambda e, h=h: e.matmul(PB(4, T, h * 128, T), lhsT=E4[0:4, h, 0:T], rhs=aT[0:4, 0:T], start=(h == 0), stop=False, skip_group_check=True), reads=["aT", "E4"], writes=[("pb", 4)])
        cmk_bc = bc_mid(cmk, 0, T, T, 4, T)
        A_ps = bass.AP(ps[:].tensor, 4 * 512, [[4096, T], [128, 4], [1, T]])
        P.add("vector", lambda e: e.tensor_tensor(out=Dm[0:T, :, 0:T], in0=A_ps, in1=cmk_bc, op=ALU.add), reads=[("pb", 4), cmkk], writes=["Dm"])
        P.add("vector", lambda e: e.tensor_reduce(out=cm[0:T, :], in_=Dm[0:T, :, 0:T], axis=AX.X, op=ALU.max), reads=["Dm"], writes=["cm"])
        P.add("vector", lambda e: e.tensor_tensor(out=gg[0:T, :], in0=cm[0:T, :], in1=mpv[0:T, :], op=ALU.max), reads=["cm", "mpv"], writes=["gg"])
        drain(DR)
        P.add("vector", lambda e: e.tensor_scalar(out=negg[0:T, :], in0=gg[0:T, :], scalar1=-1.0, scalar2=None, op0=ALU.mult), reads=["gg"], writes=["negg"])
        for h in range(4):
            P.add("scalar", lambda e, h=h: e.activation(out=pw[0:T, h, 0:T], in_=Dm[0:T, h, 0:T], func=AF.Exp, bias=negg[0:T, h:h + 1], scale=1.0), reads=["Dm", "negg"], writes=["pw"])
        drain(DR)
        P.add("vector", lambda e: e.tensor_tensor(out=tmp4[0:T, :], in0=mpv[0:T, :], in1=gg[0:T, :], op=ALU.subtract), reads=["mpv", "gg"], writes=["tmp4"])
        P.add("scalar", lambda e: e.activation(out=cw[0:T, :], in_=tmp4[0:T, :], func=AF.Exp), reads=["tmp4"], writes=["cw"])
        drain(DR)
        P.add("tensor", lambda e: e.matmul(G7(12), lhsT=sel[0:T, 0:T], rhs=gg[0:T, :], start=False, stop=False, skip_group_check=True), reads=["gg", selk], writes=[("pb", 7)])
        P.add("vector", lambda e: e.tensor_tensor(out=mtok[0:T, :], in0=gg[0:T, :], in1=G7(0), op=ALU.subtract), reads=["gg", ("pb", 7)], writes=["mtok"])
        P.add("scalar", lambda e: e.activation(out=emm[0:T, :], in_=mtok[0:T, :], func=AF.Exp, scale=-1.0), reads=["mtok"], writes=["emm"])
        P.add("vector", lambda e: e.tensor_tensor(out=tmp4[0:T, :], in0=a_t[0:T, :], in1=G7(12), op=ALU.subtract), reads=["a_t", ("pb", 7), "cw"], writes=["tmp4"])
        P.add("scalar", lambda e: e.activation(out=wsb[0:T, :], in_=tmp4[0:T, :], func=AF.Exp), reads=["tmp4"], writes=["wsb"])
        P.add("vector", lambda e: e.tensor_tensor(out=tmp4[0:T, :], in0=mpv[0:T, :], in1=G7(12), op=ALU.subtract), reads=["mpv", ("pb", 7), "wsb"], writes=["tmp4"])
        P.add("scalar", lambda e: e.activation(out=dec[0:T, :], in_=tmp4[0:T, :], func=AF.Exp), reads=["tmp4"], writes=["dec"])
        drain(DR)
        transpose_in(T, kHb, lambda c: Hb_[0:T, c * 128:(c + 1) * 128], 8, qT[:, 0:8, 0:T], "qT")
        transpose_in(T, kHb, lambda c: Hb_[0:T, 1024 + c * 128:1024 + (c + 1) * 128], 8, qT[:, 8:16, 0:T], "qT")
        for h in range(4):
            for c in range(2):
                P.add("tensor", lambda e, h=h, c=c: e.matmul(PB(3, T, h * 128, T), lhsT=qT[:, 2 * h + c, 0:T], rhs=qT[:, 8 + 2 * h + c, 0:T], start=(h == 0 and c == 0), stop=False, skip_group_check=True),
                      reads=["qT"], writes=[("pb", 3)])
        drain(DR)
        QK_ps = bass.AP(ps[:].tensor, 3 * 512, [[4096, T], [128, 4], [1, T]])
        P.add("vector", lambda e: e.tensor_tensor(out=Dm[0:T, :, 0:T], in0=QK_ps, in1=pw[0:T, :, 0:T], op=ALU.mult), reads=[("pb", 3), "pw", "cm"], writes=["Dm"])
        P.add("vector", lambda e: e.tensor_reduce(out=dint[0:T, :], in_=Dm[0:T, :, 0:T], axis=AX.X, op=ALU.add), reads=["Dm"], writes=["dint"])
        drain(DR)
        P.add("gpsimd", lambda e: e.tensor_copy(out=scb[0:T, :, 0:T], in_=Dm[0:T, :, 0:T]), reads=["Dm"], writes=["scb"])
        transpose_in(T, "scb", lambda c: scb[0:T, c, 0:T], 4, Eb[0][0:T, 0:4 * T], ("E", 0), W=T)
        for h in range(4):
            P.add("tensor", lambda e, h=h: e.matmul(PB(5 + h // 2, 256, (h % 2) * 256, T), lhsT=Eb[0][0:T, h * T:(h + 1) * T], rhs=vaug_[0:T, h, 0:256], start=(h % 2 == 0), stop=False, skip_group_check=True),
                  reads=[("E", 0), kva], writes=[("pb", 5 + h // 2)])
        drain(DR)
        if not sample:
            for h in range(4):
                for c in range(2):
                    P.add("tensor", lambda e, h=h, c=c: e.matmul(PB(h // 2, 256, (h % 2) * 256), lhsT=qT[:, 2 * h + c, :], rhs=Cb[:, h, c, 0:256], start=(h % 2 == 0 and c == 0), stop=False, skip_group_check=True),
                          reads=["qT", "Cb"], writes=[("pb", h // 2)])
            for h in range(4):
                for c in range(2):
                    P.add("tensor", lambda e, h=h, c=c: e.matmul(PB(7, 1, 20 + h), lhsT=qT[:, 2 * h + c, :], rhs=Cb[:, h, c, 256:257], start=False, stop=False, skip_group_check=True),
                          reads=["qT", "Cb"], writes=[("pb", 7)])
            for h in range(4):
                P.add("scalar", lambda e, h=h: e.activation(out=tmpq[:, h * 256:(h + 1) * 256], in_=PB(h // 2, 256, (h % 2) * 256), func=AF.Copy, scale=cw[:, h:h + 1]), reads=[("pb", h // 2), "cw"], writes=["rbuf"])
            P.add("vector", lambda e: e.tensor_tensor(out=den[:, :], in0=G7(20), in1=cw[:, :], op=ALU.mult), reads=[("pb", 7), "cw"], writes=["den"])
        else:
            sample_inter(T)
        drain(DR)
        P.add("vector", lambda e: e.tensor_tensor(out=hnum[0:T, :], in0=ps[0:T, 5 * 512:7 * 512], in1=tmpq[0:T, :], op=ALU.add), reads=[("pb", 5), ("pb", 6), "rbuf"], writes=["mixpre"])
        P.add("vector", lambda e: e.tensor_tensor(out=den[0:T, :], in0=den[0:T, :], in1=dint[0:T, :], op=ALU.add), reads=["den", "dint"], writes=["den"])
        P.add("vector", lambda e: e.tensor_scalar(out=tmp4[0:T, :], in0=den[0:T, :], scalar1=-1.0, scalar2=None, op0=ALU.mult), reads=["den", "dec"], writes=["tmp4"])
        P.add("vector", lambda e: e.tensor_tensor(out=den[0:T, :], in0=den[0:T, :], in1=tmp4[0:T, :], op=ALU.max), reads=["den", "tmp4"], writes=["den"])
        P.add("vector", lambda e: e.tensor_tensor(out=den[0:T, :], in0=den[0:T, :], in1=emm[0:T, :], op=ALU.max), reads=["den", "emm"], writes=["den"])
        P.add("vector", lambda e: e.reciprocal(out=rdd[0:T, :], in_=den[0:T, :]), reads=["den"], writes=["rdd"])
        for h in range(4):
            P.add("vector", lambda e, h=h: e.scalar_tensor_tensor(out=hnum[0:T, h * 256:(h + 1) * 256], in0=hnum[0:T, h * 256:(h + 1) * 256], scalar=rdd[0:T, h:h + 1], in1=H_[0:T, 1024 + h * 256:1024 + (h + 1) * 256], op0=ALU.mult, op1=ALU.mult),
                  reads=["mixpre", "rdd", kH], writes=["mixpre"])
            P.add("vector", lambda e, h=h: e.bn_stats(out=sth[0:T, h, :], in_=hnum[0:T, h * 256:(h + 1) * 256]), reads=["mixpre"], writes=["sth"])
            P.add("vector", lambda e, h=h: e.bn_aggr(out=mvh[0:T, h, :], in_=sth[0:T, h, :]), reads=["sth"], writes=["mvh"])
        drain(DR)
        P.add("vector", lambda e: e.tensor_scalar(out=rsh[0:T, :], in0=mvh[0:T, :, 1], scalar1=MH_EPS, scalar2=None, op0=ALU.add), reads=["mvh"], writes=["rsh"])
        P.add("scalar", lambda e: e.activation(out=rsh[0:T, :], in_=rsh[0:T, :], func=AF.Sqrt), reads=["rsh"], writes=["rsh"])
        P.add("vector", lambda e: e.reciprocal(out=rsh[0:T, :], in_=rsh[0:T, :]), reads=["rsh"], writes=["rsh"])
        for h in range(4):
            P.add("vector", lambda e, h=h: e.tensor_scalar(out=hnum[0:T, h * 256:(h + 1) * 256], in0=hnum[0:T, h * 256:(h + 1) * 256], scalar1=mvh[0:T, h, 0:1], scalar2=rsh[0:T, h:h + 1], op0=ALU.subtract, op1=ALU.mult),
                  reads=["mixpre", "mvh", "rsh"], writes=["mixpre"])
        drain(DR)
        P.add("gpsimd", lambda e: e.tensor_tensor(out=hnum[0:T, :], in0=hnum[0:T, :], in1=mhb[0:T, :], op=ALU.mult), reads=["mixpre", "mhb"], writes=["mixpre"])
        P.add("gpsimd", lambda e: e.tensor_tensor(out=mixb[0:T, :], in0=hnum[0:T, :], in1=sz_[0:T, :], op=ALU.mult), reads=["mixpre", ksz], writes=["mixb"])
        tail(T, s2, 1, out_dram, defer=deftail, atomic=True)
        drain(DR)
        if not sample:
            for h in range(4):
                P.add("gpsimd", lambda e, h=h: e.tensor_scalar(out=Hb_[:, 2048 + h * 256:2048 + (h + 1) * 256], in0=H_[:, h * 256:(h + 1) * 256], scalar1=wsb[:, h:h + 1], scalar2=None, op0=ALU.mult), reads=[kH, "wsb"], writes=[kHb])
            for h in range(4):
                for c in range(2):
                    bank = 3 + (h * 2 + c) % 2
                    P.add("tensor", lambda e, h=h, c=c, bank=bank: e.matmul(PB(bank, 257), lhsT=Hb_[:, 2048 + h * 256 + c * 128: 2048 + h * 256 + (c + 1) * 128], rhs=vaug_[:, h, :], start=True, stop=True),
                          reads=[kHb, kva], writes=[("pb", bank)])
                    P.add("vector", lambda e, h=h, c=c, bank=bank: e.scalar_tensor_tensor(out=Cst[:, h, c, :], in0=Cst[:, h, c, :], scalar=dec[:, h:h + 1], in1=PB(bank, 257), op0=ALU.mult, op1=ALU.add),
                          reads=["Cst", "dec", ("pb", bank)], writes=["Cst"])
            P.add("gpsimd", lambda e: e.tensor_copy(out=Cb[:].rearrange("p a b c -> p (a b c)"), in_=Cst[:].rearrange("p a b c -> p (a b c)")), reads=["Cst"], writes=["Cb"])
        else:
            sample_state(T)

    cmax_tok = din("cmax_tok", [TS, 4])
    cnorm_tok = din("cnorm_tok", [TS, 1024])

    def sample_inter(T):
        for c in range(8):
            P.add("tensor", lambda e, c=c: e.transpose(out=PB(4, T, c * T), in_=H[0:T, 2048 + c * 128: 2048 + (c + 1) * 128], identity=ident_f[0:T, 0:T]), reads=["H", "ident_f"], writes=[("pb", 4)])
        P.add("vector", lambda e: e.tensor_copy(out=qTf[:].rearrange("p a b -> p (a b)"), in_=PB(4, 8 * T)), reads=[("pb", 4)], writes=["qTf"])
        P.add("gpsimd", lambda e: e.memset(tmpq[0:T, :], 0.0), writes=["rbuf"])
        for n in range(NS):
            for h in range(4):
                cb = (n * 4 + h) % 2
                P.dma("sync", Cs[cb][:], cmem[n, h].rearrange("(c p) v -> p c v", p=128), writes=[("Cs", cb)])
                for c in range(2):
                    P.add("tensor", lambda e, h=h, c=c, cb=cb: e.matmul(PB(h // 2, 256, (h % 2) * 256, T), lhsT=qTf[:, 2 * h + c, :], rhs=Cs[cb][:, c, :], start=(h % 2 == 0 and c == 0), stop=False, skip_group_check=True),
                          reads=["qTf", ("Cs", cb)], writes=[("pb", h // 2)])
            P.add("vector", lambda e, n=n: e.scalar_tensor_tensor(out=tmpq[0:T, :], in0=ps[0:T, 0:1024], scalar=onehot[0:T, n:n + 1], in1=tmpq[0:T, :], op0=ALU.mult, op1=ALU.add), reads=[("pb", 0), ("pb", 1), "onehot_s", "rbuf"], writes=["rbuf"])
        for h in range(4):
            P.add("vector", lambda e, h=h: e.tensor_scalar(out=tmpq[0:T, h * 256:(h + 1) * 256], in0=tmpq[0:T, h * 256:(h + 1) * 256], scalar1=cw[0:T, h:h + 1], scalar2=None, op0=ALU.mult), reads=["rbuf", "cw"], writes=["rbuf"])
        P.dma("sync", kwf[0:TS, :], cnorm_tok, writes=["mixpre"])
        P.add("gpsimd", lambda e: e.tensor_tensor(out=kwn[0:T, :], in0=H[0:T, 2048:3072], in1=kwf[0:T, :], op=ALU.mult), reads=["H", "mixpre"], writes=["kwn"])
        P.add("vector", lambda e: e.tensor_reduce(out=den[0:T, :], in_=kwn[0:T, :].rearrange("p (h d) -> p h d", d=256), axis=AX.X, op=ALU.add), reads=["kwn"], writes=["den"])
        P.add("vector", lambda e: e.tensor_tensor(out=den[0:T, :], in0=den[0:T, :], in1=cw[0:T, :], op=ALU.mult), reads=["den", "cw"], writes=["den"])

    def sample_state(T):
        for h in range(4):
            P.add("gpsimd", lambda e, h=h: e.tensor_scalar(out=kwf[0:T, h * 256:(h + 1) * 256], in0=H[0:T, h * 256:(h + 1) * 256], scalar1=wsb[0:T, h:h + 1], scalar2=None, op0=ALU.mult), reads=["H", "wsb"], writes=["mixpre"])
        P.add("gpsimd", lambda e: e.memset(quart[:], 0.25), writes=["quart"])
        P.add("vector", lambda e: e.tensor_tensor(out=Xd[0:T, :, :], in0=bc_last(onehot, 0, NS, T, NS, 1, 4), in1=bc_mid(dec, 0, 4, T, NS, 4), op=ALU.mult), reads=["dec", "onehot_s"], writes=["Xd"])
        P.add("tensor", lambda e: e.matmul(PB(7, 4 * NS, 64), lhsT=quart[0:T, :], rhs=Xd[0:T].rearrange("p a b -> p (a b)"), start=False, stop=False, skip_group_check=True), reads=["Xd", "quart"], writes=[("pb", 7)])
        P.add("vector", lambda e: e.tensor_copy(out=decb[:].rearrange("p a b -> p (a b)"), in_=PB(7, 4 * NS, 64)), reads=[("pb", 7)], writes=["decb"])
        P.add("tensor", lambda e: e.matmul(ps[0:NS, 7 * 512 + 256: 7 * 512 + 260], lhsT=onehot[0:T, :], rhs=dec[0:T, :], start=False, stop=False, skip_group_check=True), reads=["dec", "onehot_s"], writes=[("pb", 7)])
        P.add("vector", lambda e: e.tensor_scalar(out=decn[:], in0=ps[0:NS, 7 * 512 + 256: 7 * 512 + 260], scalar1=0.25, scalar2=None, op0=ALU.mult), reads=[("pb", 7)], writes=["decn"])
        P.dma("sync", kwn[0:NS, :], cnorm, writes=["kwn"])
        for i in range(2):
            P.add("tensor", lambda e, i=i: e.matmul(ps[0:NS, (3 + i) * 512:(4 + i) * 512], lhsT=onehot[0:T, :], rhs=kwf[0:T, i * 512:(i + 1) * 512], start=True, stop=True), reads=["mixpre", "onehot_s"], writes=[("pb", 3 + i)])
        for h in range(4):
            P.add("vector", lambda e, h=h: e.scalar_tensor_tensor(out=kwn[0:NS, h * 256:(h + 1) * 256], in0=kwn[0:NS, h * 256:(h + 1) * 256], scalar=decn[:, h:h + 1], in1=ps[0:NS, 3 * 512 + h * 256: 3 * 512 + (h + 1) * 256], op0=ALU.mult, op1=ALU.add),
                  reads=["kwn", "decn", ("pb", 3), ("pb", 4)], writes=["kwn"])
        P.dma("gpsimd", cnorm_s, kwn[0:NS, :], reads=["kwn"])
        P.dma("gpsimd", cmax_s, mtok[0:T, :], reads=["mtok"])
        for n in range(NS):
            P.add("gpsimd", lambda e, n=n: e.tensor_scalar(out=kwn[0:T, :], in0=kwf[0:T, :], scalar1=onehot[0:T, n:n + 1], scalar2=None, op0=ALU.mult), reads=["mixpre", "onehot_s"], writes=["kwn"])
            for h in range(4):
                cb = (n * 4 + h) % 2
                P.dma("sync", Cs[cb][:], cmem[n, h].rearrange("(c p) v -> p c v", p=128), writes=[("Cs", cb)])
                for c in range(2):
                    bank = 3 + c
                    P.add("tensor", lambda e, h=h, c=c, bank=bank: e.matmul(PB(bank, 256), lhsT=kwn[0:T, h * 256 + c * 128: h * 256 + (c + 1) * 128], rhs=vf[0:T, h, :], start=True, stop=True), reads=["kwn", "vf"], writes=[("pb", bank)])
                    P.add("vector", lambda e, h=h, c=c, bank=bank, cb=cb, n=n: e.scalar_tensor_tensor(out=Cn[cb][:, c, :], in0=Cs[cb][:, c, :], scalar=decb[:, n, h:h + 1], in1=PB(bank, 256), op0=ALU.mult, op1=ALU.add),
                          reads=[("Cs", cb), "decb", ("pb", bank)], writes=[("Cn", cb)])
                P.dma("gpsimd", cmem_s[n, h].rearrange("(c p) v -> p c v", p=128), Cn[cb][:], reads=[("Cn", cb)])

    P.dma("sync", lng[:], lng_bc[:, 1, :], writes=["lng"])
    P.dma("sync", lnb[:], lnb_bc[:, 1, :], writes=["lnb"])
    stC = contextlib.ExitStack()
    cur["st"] = stC
    Hs[1] = sb("H2", [128, 2048]); Hbs[1] = sb("Hb2", [128, 3072], BF); vaugs[1] = sb("vaug2", [128, 4, 257], BF); gts[1] = sb("gt2", [128, 8])
    sz[1] = sb("sz2", [128, D]); xin[1] = sb("xin2", [128, D])
    cur["st"] = st
    va2 = vaugs[1]
    P.add("gpsimd", lambda e: e.memset(va2[:].rearrange("p a b -> p (a b)"), 1.0), writes=[("vaug1", 1)])
    mlstm(128, x1d[0:128, :], 0, None, False, p=0, mode="front")
    drain()
    for c in range(NT):
        if c + 1 < NT:
            mlstm(128, x1d[(c + 1) * 128:(c + 2) * 128, :], (c + 1) % 2, None, False, p=(c + 1) % 2, mode="front")
        mlstm(128, None, c % 2, yp[c * 128:(c + 1) * 128, :], False, p=c % 2, mode="rest", deftail=True)
    drain()
    for h in range(4):
        P.dma("gpsimd", cmem_p[h].rearrange("(c p) v -> p c v", p=128), Cst[:, h, :, 0:256], reads=["Cst"])
        P.dma("gpsimd", cnorm_p[h].rearrange("(c p) -> p c", p=128), Cst[:, h, :, 256], reads=["Cst"], allow_slow_non_contiguous=True)
    P.dma("gpsimd", cmax_p, mtok[127:128, :], reads=["mtok"])
    fence()
    stC.close()
    Hs[1] = Hs[0]; Hbs[1] = Hbs[0]; vaugs[1] = vaugs[0]; gts[1] = gts[0]; sz[1] = sz[0]; xin[1] = xin[0]
    Cs = [sb("Cs%d" % i, [128, 2, 256]) for i in range(2)]
    Cn = [sb("Cn%d" % i, [128, 2, 256]) for i in range(2)]
    qTf = sb("qTf", [128, 8, TS])
    vf = sb("vf", [TS, 4, 256])
    kwf = mixpre; kwn = sb("kwn", [TS, 1024])
    decb = sb("decb", [128, NS, 4]); decn = sb("decn", [NS, 4]); quart = sb("quart", [TS, 128]); Xd = sb("Xd", [TS, NS, 4])

    mlstm(TS, x1d[S:S + TS, :], 0, ys[:, :], True)

    if _LIMIT[0] is not None:
        print('total ops', len(P.ops))
        P.ops = P.ops[:_LIMIT[0]]
    P.emit(st)
    return nc, st, consts


def core_inputs(inp, core, NT, NS, consts, xp=None):
    TS = 4 * NS
    b = core % 4
    s0 = core * NS
    f = {}
    f["xp"] = np.ascontiguousarray(inp["x_prompt"][b] if xp is None else xp)
    f["xs"] = np.ascontiguousarray(inp["x_sample"][s0:s0 + NS].reshape(TS, D))
    for g, nm in enumerate(("cache_a_kv", "cache_b0_kv", "cache_b1_kv", "cache_b2_kv")):
        f["c%d" % g] = np.ascontiguousarray(inp[nm][0, s0:s0 + NS].reshape(NS, CACHE_L[g], 256))
    f["cmem"] = np.ascontiguousarray(inp["state_c_mem"][0, s0:s0 + NS])
    f["cnorm"] = np.ascontiguousarray(inp["state_c_norm"][0, s0:s0 + NS].reshape(NS, 1024))
    f["cmax"] = np.ascontiguousarray(inp["state_c_max"][0, s0:s0 + NS])
    f["w_in0"] = np.ascontiguousarray(inp["w_in0"][0])
    f["w_out0"] = np.ascontiguousarray(inp["w_out0"][0])
    f["w_in1"] = np.ascontiguousarray(inp["w_in1"][0])
    f["w_out1"] = np.ascontiguousarray(inp["w_out1"][0])
    f["sink_bc"] = np.ascontiguousarray(np.broadcast_to(inp["sinks0"][0][None, :], (128, 8)))
    f["bg_bc"] = np.ascontiguousarray(np.broadcast_to(inp["b_gates1"][0][None, :], (128, 8)))
    f["mh_bc"] = np.ascontiguousarray(np.broadcast_to(inp["mh_norm1"][0][None, :], (128, 1024)))
    f["lng_bc"] = np.ascontiguousarray(np.broadcast_to(inp["ln_g"][None, :, :], (128, 2, 1024)))
    f["lnb_bc"] = np.ascontiguousarray(np.broadcast_to(inp["ln_b"][None, :, :], (128, 2, 1024)))
    f["cmax_tok"] = np.repeat(f["cmax"], 4, axis=0)
    f["cnorm_tok"] = np.repeat(f["cnorm"], 4, axis=0)
    for k, v in consts.items():
        f["k_" + k] = np.ascontiguousarray(v)
    return {k: np.asarray(v, np.float32) for k, v in f.items()}


NT_FULL = 32
NS_FULL = 16
_CACHE = {}


def kernel(**inp):
    inp = {k: np.asarray(v) for k, v in inp.items()}
    if "nc" not in _CACHE:
        nc, st, consts = build(NT_FULL, NS_FULL)
        _CACHE["nc"] = (nc, consts)
    nc, consts = _CACHE["nc"]
    in_maps = [core_inputs(inp, c, NT_FULL, NS_FULL, consts) for c in range(8)]
    res = run_bass_kernel_spmd(nc, in_maps, core_ids=list(range(8)))
    R = res.results
    f32 = np.float32
    yp = np.stack([R[b]["yp"] for b in range(4)], 0).astype(f32)
    ys = np.concatenate([R[c]["ys"].reshape(NS_FULL, 4, D) for c in range(8)], 0).astype(f32)
    outs = [yp, ys]
    for g in range(4):
        L = CACHE_L[g]
        outs.append(np.stack([R[b]["kvp%d" % g].reshape(L, 2, 2, 64) for b in range(4)], 0)[None].astype(f32))
    outs.append(np.stack([R[b]["cmem_p"] for b in range(4)], 0)[None].astype(f32))
    outs.append(np.stack([R[b]["cnorm_p"] for b in range(4)], 0)[None].astype(f32))
    outs.append(np.stack([R[b]["cmax_p"].reshape(4) for b in range(4)], 0)[None].astype(f32))
    for g in range(4):
        L = CACHE_L[g]
        outs.append(np.concatenate([R[c]["kvs%d" % g].reshape(NS_FULL, L, 2, 2, 64) for c in range(8)], 0)[None].astype(f32))
    outs.append(np.concatenate([R[c]["cmem_s"] for c in range(8)], 0)[None].astype(f32))
    outs.append(np.concatenate([R[c]["cnorm_s"].reshape(NS_FULL, 4, 256) for c in range(8)], 0)[None].astype(f32))
    outs.append(np.concatenate([R[c]["cmax_s"].reshape(NS_FULL, 4, 4)[:, 3, :] for c in range(8)], 0)[None].astype(f32))
    return tuple(outs)
```

```python
import contextlib
import numpy as np
import concourse.bass as bass
import concourse.mybir as mybir
from concourse.bass_utils import run_bass_kernel_spmd

F32 = mybir.dt.float32
BF = mybir.dt.bfloat16
AF = mybir.ActivationFunctionType
ALU = mybir.AluOpType
AX = mybir.AxisListType

ENGS = ("tensor", "scalar", "vector", "gpsimd", "sync")


class Op:
    __slots__ = ("eng", "fn", "deps", "is_dma", "idx", "signal", "sem", "val", "prev_same_sem")

    def __init__(self, eng, fn, is_dma):
        self.eng = eng
        self.fn = fn
        self.is_dma = is_dma
        self.deps = set()
        self.signal = False
        self.sem = None
        self.val = 0
        self.prev_same_sem = None


class Res:
    __slots__ = ("w", "r_eng", "r_dma")

    def __init__(self):
        self.w = None
        self.r_eng = {}
        self.r_dma = []


class Prog:
    def __init__(self, nc, ndma_sems=12):
        self.nc = nc
        self.ops = []
        self.res = {}
        self.ndma = ndma_sems

    def _res(self, k):
        r = self.res.get(k)
        if r is None:
            r = self.res[k] = Res()
        return r

    def add(self, eng, fn, reads=(), writes=(), dma=False):
        op = Op(eng, fn, dma)
        op.idx = len(self.ops)
        for k in reads:
            r = self._res(k)
            if r.w is not None:
                op.deps.add(r.w)
        for k in writes:
            r = self._res(k)
            if r.w is not None:
                op.deps.add(r.w)
            for o in r.r_eng.values():
                op.deps.add(o)
            for o in r.r_dma:
                op.deps.add(o)
        for k in reads:
            r = self._res(k)
            if dma:
                r.r_dma.append(op)
            else:
                r.r_eng[eng] = op
        for k in writes:
            r = self._res(k)
            r.w = op
            r.r_eng = {}
            r.r_dma = []
        op.deps.discard(op)
        self.ops.append(op)
        return op

    def fence(self, fns):
        prev = list(self.ops)
        last = {}
        dmas = []
        for o in prev:
            if o.is_dma:
                dmas.append(o)
            else:
                last[o.eng] = o
        for eng, fn in fns.items():
            op = Op(eng, fn, eng == "sync")
            op.idx = len(self.ops)
            op.deps = set(last.values()) | set(dmas)
            self.ops.append(op)

    def dma(self, q, out, in_, reads=(), writes=(), **kw):
        return self.add(q, lambda e: e.dma_start(out=out, in_=in_, **kw), reads, writes, dma=True)

    def emit(self, stack):
        nc = self.nc
        ops = self.ops
        for op in ops:
            for d in op.deps:
                if d.is_dma or d.eng != "tensor" or op.eng != "tensor" or op.is_dma:
                    d.signal = True
        for op in ops:
            if op.is_dma:
                op.signal = True
        esem = {e: stack.enter_context(nc.semaphore("s_" + e)) for e in ENGS}
        dsem = {e: [stack.enter_context(nc.semaphore("d_%s_%d" % (e, i))) for i in range(self.ndma)]
                for e in ENGS if any(o.is_dma and o.eng == e for o in ops)}
        ecount = {e: 0 for e in ENGS}
        dcount = {e: [0] * self.ndma for e in dsem}
        dlast = {e: [None] * self.ndma for e in dsem}
        drr = {e: 0 for e in dsem}
        for op in ops:
            if op.is_dma:
                i = drr[op.eng]
                drr[op.eng] = (i + 1) % self.ndma
                op.sem = dsem[op.eng][i]
                dcount[op.eng][i] += 16
                op.val = dcount[op.eng][i]
                op.prev_same_sem = dlast[op.eng][i]
                dlast[op.eng][i] = op
            elif op.signal:
                ecount[op.eng] += 1
                op.sem = esem[op.eng]
                op.val = ecount[op.eng]
        per_eng = {e: [o for o in ops if o.eng == e] for e in ENGS}
        all_dmas = [o for o in ops if o.is_dma]
        block = stack.enter_context(nc.Block())

        def make(ename):
            def body(e):
                waited = {}
                for op in per_eng[ename]:
                    need = {}
                    deps = set(op.deps)
                    if op.is_dma and op.prev_same_sem is not None:
                        deps.add(op.prev_same_sem)
                    for d in deps:
                        if (not d.is_dma) and d.eng == "tensor" and ename == "tensor" and not op.is_dma:
                            continue
                        key = id(d.sem)
                        if waited.get(key, 0) >= d.val:
                            continue
                        if key not in need or need[key][1] < d.val:
                            need[key] = (d.sem, d.val)
                    for key, (s, v) in need.items():
                        e.wait_ge(s, v)
                        waited[key] = v
                    ins = op.fn(e)
                    if op.signal:
                        ins.then_inc(op.sem, 16 if op.is_dma else 1)
                if ename == "sync":
                    fin = {}
                    for o in all_dmas:
                        k = id(o.sem)
                        if k not in fin or fin[k][1] < o.val:
                            fin[k] = (o.sem, o.val)
                    for en in ENGS:
                        if ecount[en] > 0:
                            fin[id(esem[en])] = (esem[en], ecount[en])
                    for k, (s, v) in fin.items():
                        if waited.get(k, 0) < v:
                            e.wait_ge(s, v)
            return body

        for ename in ENGS:
            if per_eng[ename] or ename == "sync":
                getattr(block, ename)(make(ename))


D = 1024
IN0 = 4096
IN1 = 5128
ALPHA = 4.0 ** 0.25
LN_EPS = 1e-5
MH_EPS = 1e-6
PAST = 16384
THETA = 500000.0
NEG = -1e30
GROUPS = ("A", "b0", "b1", "b2")
DIL = (1, 1, 4, 16)
RING = (3, 3, 6, 18)
QBASE = (0, 768, 1280, 1792)
KBASE = (512, 2304, 2432, 2560)
VBASE = (640, 2688, 2816, 2944)
CACHE_L = (128, 128, 512, 2048)
MIDX = {1: (0, None, 1), 4: (2, 3, 4), 16: (5, 6, 7)}


_LIMIT = [None]


def host_consts(NT, NS):
    TS = 4 * NS
    c = {}
    k = np.arange(128)[:, None]
    q = np.arange(128)[None, :]
    m = np.zeros((128, 8, 128), np.float32)
    m[:, 0] = (k <= q)
    m[:, 1] = (k >= q)
    for d, (a, b, l) in ((4, MIDX[4]), (16, MIDX[16])):
        cong = ((q - k) % d == 0)
        m[:, a] = cong & (k <= q)
        m[:, b] = cong
        m[:, l] = cong & (k >= q)
    c["maskt"] = m
    half = 8
    inv = THETA ** (-np.arange(half, dtype=np.float32) / half)
    pos = (np.arange(NT)[None, :] * 128 + np.arange(128)[:, None]).astype(np.float32)
    ang = pos[:, :, None] * inv[None, None, :]
    c["cs"] = np.concatenate([np.cos(ang), np.sin(ang)], -1).astype(np.float32)
    tok = np.arange(TS)
    pos_s = (PAST + tok % 4).astype(np.float32)
    ang = pos_s[:, None] * inv[None, :]
    c["cs_s"] = np.concatenate([np.cos(ang), np.sin(ang)], -1).astype(np.float32)[:, None, :]
    c["ident"] = np.eye(128, dtype=np.float32)
    same = (tok[:, None] // 4 == tok[None, :] // 4)
    mn = np.zeros((TS, 2, TS), np.float32)
    mn[:, 0] = same & (tok[:, None] <= tok[None, :])
    mn[:, 1] = (tok[:, None] == tok[None, :])
    c["masknew"] = mn
    mc = np.zeros((128, 10, 4, 4), np.float32)
    rows = np.arange(128)[:, None, None]
    ii = np.arange(4)[None, None, :]
    mc[:, 0] = (rows >= ii)
    mc[:, 1] = (rows >= ii)
    for r in range(4):
        mc[:, 2 + r] = (ii == r)
        mc[:, 6 + r] = (ii == r)
    c["maskc"] = mc.reshape(128, 10, 16)
    t = np.arange(128)
    c["U128"] = (t[:, None] <= t[None, :]).astype(np.float32)
    c["cmask128"] = np.where(t[None, :] <= t[:, None], 0.0, NEG).astype(np.float32)
    c["sellast128"] = np.tile((t[:, None] == 127).astype(np.float32), (1, 128))
    ts_ = np.arange(TS)
    sames = (ts_[:, None] // 4 == ts_[None, :] // 4)
    c["U_s"] = (sames & (ts_[:, None] <= ts_[None, :])).astype(np.float32)
    c["cmask_s"] = np.where(sames & (ts_[None, :] <= ts_[:, None]), 0.0, NEG).astype(np.float32)
    c["selend_s"] = (sames & (ts_[:, None] % 4 == 3)).astype(np.float32)
    seln = np.zeros((TS, NS, 128), np.float32)
    for n in range(NS):
        seln[4 * n, n, :] = 1.0
    oh = np.zeros((TS, NS), np.float32)
    oh[ts_, ts_ // 4] = 1.0
    c["onehot_s"] = oh
    c["rowmask_s"] = np.tile(oh.T[None, :, :], (128, 1, 1)).astype(np.float32)
    c["G_s"] = oh.copy()
    e4 = np.zeros((4, 4, 128), np.float32)
    for h in range(4):
        e4[h, h, :] = 1.0
    c["E4"] = e4
    return c


def build(NT, NS):
    TS = 4 * NS
    nc = bass.Bass("TRN2", target_bir_lowering=False)
    consts = host_consts(NT, NS)
    dr = {}

    def din(name, shape, dt=F32):
        dr[name] = nc.dram_tensor(name, list(shape), dt, kind="ExternalInput").ap()
        return dr[name]

    def dout(name, shape):
        dr[name] = nc.dram_tensor(name, list(shape), F32, kind="ExternalOutput").ap()
        return dr[name]

    S = NT * 128
    xp = din("xp", [S, D])
    xs = din("xs", [TS, D])
    caches = [din("c%d" % g, [NS, CACHE_L[g], 256]) for g in range(4)]
    cmem = din("cmem", [NS, 4, 256, 256])
    cnorm = din("cnorm", [NS, 1024])
    cmax = din("cmax", [NS, 4])
    w_in0 = din("w_in0", [D, IN0])
    w_out0 = din("w_out0", [D, D])
    w_in1 = din("w_in1", [D, IN1])
    w_out1 = din("w_out1", [D, D])
    sink_bc = din("sink_bc", [128, 8])
    bg_bc = din("bg_bc", [128, 8])
    mh_bc = din("mh_bc", [128, 1024])
    lng_bc = din("lng_bc", [128, 2, 1024])
    lnb_bc = din("lnb_bc", [128, 2, 1024])
    cd = {k: din("k_" + k, v.shape) for k, v in consts.items()}

    yp = dout("yp", [S, D])
    ys = dout("ys", [TS, D])
    kvp = [dout("kvp%d" % g, [min(CACHE_L[g], S), 256]) for g in range(4)]
    cmem_p = dout("cmem_p", [4, 256, 256])
    cnorm_p = dout("cnorm_p", [4, 256])
    cmax_p = dout("cmax_p", [1, 4])
    kvs = [dout("kvs%d" % g, [NS, CACHE_L[g], 256]) for g in range(4)]
    cmem_s = dout("cmem_s", [NS, 4, 256, 256])
    cnorm_s = dout("cnorm_s", [NS, 1024])
    cmax_s = dout("cmax_s", [TS, 4])
    x1d = dout("x1d", [S + TS, D])

    st = contextlib.ExitStack()
    P = Prog(nc)

    cur = {"st": st}

    def sb(name, shape, dt=F32):
        return cur["st"].enter_context(nc.sbuf_tensor(name, list(shape), dt))

    fsc = st.enter_context(nc.sbuf_tensor("fsc", [128, 8], F32))

    P.add("gpsimd", lambda e: e.memset(fsc[:], 0.0), writes=["fsc"])

    def fence():
        P.fence({"scalar": lambda e: e.activation(out=fsc[0:1, 0:1], in_=fsc[0:1, 1:2], func=AF.Copy),
                 "vector": lambda e: e.memset(fsc[0:1, 2:3], 0.0),
                 "gpsimd": lambda e: e.memset(fsc[0:1, 4:5], 0.0),
                 "sync": lambda e: e.dma_start(out=fsc[0:1, 6:7], in_=fsc[0:1, 7:8])})

    ps = st.enter_context(nc.psum_tensor("ps", [128, 4096], F32))
    psb = ps.bitcast(BF)

    def PB(i, n=512, off=0, T=128):
        return ps[0:T, i * 512 + off: i * 512 + off + n]

    def PBb(i, n=1024, off=0, T=128):
        return psb[0:T, i * 1024 + off: i * 1024 + off + n]

    Warena = sb("Warena", [128, 8, IN1], BF)
    Woarena = sb("Woarena", [128, 8, D], BF)
    ident_f = sb("ident_f", [128, 128])
    ident_b = sb("ident_b", [128, 128], BF)
    lng = sb("lng", [128, 1024])
    lnb = sb("lnb", [128, 1024])
    sinkexp = sb("sinkexp", [128, 8])
    P.dma("sync", ident_f[:], cd["ident"], writes=["ident_f"])
    P.dma("gpsimd", ident_b[:], cd["ident"], writes=["ident_b"])
    P.dma("sync", lng[:], lng_bc[:, 0, :], writes=["lng"])
    P.dma("sync", lnb[:], lnb_bc[:, 0, :], writes=["lnb"])
    P.dma("sync", sinkexp[:], sink_bc, writes=["sinkexp"])
    P.add("scalar", lambda e: e.activation(out=sinkexp[:], in_=sinkexp[:], func=AF.Exp), reads=["sinkexp"], writes=["sinkexp"])

    def load_W(w_in, ncols):
        wv = w_in.rearrange("(c p) n -> p c n", p=128)
        ncg = (ncols + 511) // 512
        for cg in range(ncg):
            n = min(512, ncols - cg * 512)
            P.dma("gpsimd", Warena[:, :, cg * 512:cg * 512 + n], wv[:, :, cg * 512:cg * 512 + n], reads=[], writes=[("W", cg)])

    def load_Wo(w_out):
        wo = w_out.rearrange("(c p) n -> p c n", p=128)
        for c in range(8):
            P.dma("gpsimd", Woarena[:, c, :], wo[:, c, :], reads=[], writes=["Wo"])

    xin = [sb("xin0", [128, D])] * 2
    xin = list(xin)
    xb = sb("xb", [128, D], BF)
    xT = sb("xT", [128, 8, 128], BF)
    H = sb("H", [128, 3072])
    Hb = sb("Hb", [128, 3072], BF)
    sz0 = sb("sz0", [128, D])
    sz = [sz0, sz0]
    qT = sb("qT", [128, 16, 128], BF)
    Eb = [sb("Eb%d" % i, [128, 512], BF) for i in range(2)]
    Pm = [sb("Pm%d" % i, [128, 512], BF) for i in range(2)]
    dn = sb("dn", [128, 8])
    rd = sb("rd", [128, 8])
    mixpre = sb("mixpre", [128, D])
    mixb = sb("mixb", [128, D], BF)
    mixT = sb("mixT", [128, 8, 128], BF)
    rbuf = sb("rbuf", [128, D])
    stt = sb("stt", [128, 2, 6])
    mv = sb("mv", [128, 2])
    rstd = sb("rstd", [128, 1])
    stA = contextlib.ExitStack()
    cur["st"] = stA
    maskt = sb("maskt", [128, 8, 128], BF)
    cs = sb("cs", [128, NT, 16])
    kT = [sb("kT%d" % g, [128, RING[g], 128], BF) for g in range(4)]
    Vr = [sb("Vr%d" % g, [128, RING[g], 2, 65], BF) for g in range(4)]
    rtmp = sb("rtmpA", [128, 4, 30, 8])
    qTd = [qT, sb("qTA2", [128, 16, 128], BF)]
    xin[1] = sb("xinA1", [128, D])
    sz[1] = sb("szA1", [128, D])
    for _i in (2, 3, 4):
        Eb.append(sb("EbA%d" % _i, [128, 512], BF))
        Pm.append(sb("PmA%d" % _i, [128, 512], BF))
    P.dma("gpsimd", maskt[:], cd["maskt"], writes=["maskt"])
    P.dma("sync", cs[:], cd["cs"], writes=["cs"])
    for g in range(4):
        P.add("gpsimd", lambda e, g=g: e.memset(Vr[g][:].rearrange("p a b c -> p (a b c)"), 1.0), writes=[("V", g, s) for s in range(RING[g])])

    cnt = {"s": 0, "m": 0}

    def bc_mid(t, off, pstep, T, n_mid, n_in):
        return bass.AP(t[:].tensor, off, [[pstep, T], [0, n_mid], [1, n_in]])

    def bc_last(t, off, pstep, T, n_mid, mid_step, n_in):
        return bass.AP(t[:].tensor, off, [[pstep, T], [mid_step, n_mid], [0, n_in]])

    def layer_norm(T, layer, out_ap):
        for i in range(2):
            P.add("vector", lambda e, i=i: e.bn_stats(out=stt[0:T, i, :], in_=rbuf[0:T, i * 512:(i + 1) * 512]), reads=["rbuf"], writes=["stt"])
        P.add("vector", lambda e: e.bn_aggr(out=mv[0:T, :], in_=stt[0:T].rearrange("p a b -> p (a b)")), reads=["stt"], writes=["mv"])
        P.add("vector", lambda e: e.tensor_scalar(out=rstd[0:T, :], in0=mv[0:T, 1:2], scalar1=LN_EPS, scalar2=None, op0=ALU.add), reads=["mv"], writes=["rstd"])
        P.add("scalar", lambda e: e.activation(out=rstd[0:T, :], in_=rstd[0:T, :], func=AF.Sqrt), reads=["rstd"], writes=["rstd"])
        P.add("vector", lambda e: e.reciprocal(out=rstd[0:T, :], in_=rstd[0:T, :]), reads=["rstd"], writes=["rstd"])
        P.add("vector", lambda e: e.tensor_scalar(out=rbuf[0:T, :], in0=rbuf[0:T, :], scalar1=mv[0:T, 0:1], scalar2=rstd[0:T, 0:1], op0=ALU.subtract, op1=ALU.mult), reads=["rbuf", "mv", "rstd"], writes=["rbuf"])
        P.add("vector", lambda e: e.tensor_tensor(out=rbuf[0:T, :], in0=rbuf[0:T, :], in1=lng[0:T, :], op=ALU.mult), reads=["rbuf", "lng"], writes=["rbuf"])
        return P.add("vector", lambda e: e.tensor_tensor(out=out_ap, in0=rbuf[0:T, :], in1=lnb[0:T, :], op=ALU.add), reads=["rbuf", "lnb"], writes=["rbuf"])

    def transpose_in(T, src_key, src_ap_fn, n, dst_ap, dst_key, bank=2, W=128):
        for c in range(n):
            P.add("tensor", lambda e, c=c: e.transpose(out=psb[0:W, bank * 1024 + c * T: bank * 1024 + (c + 1) * T], in_=src_ap_fn(c), identity=ident_b[0:T, 0:T]),
                  reads=[src_key, "ident_b"], writes=[("pb", bank)])
        eng = "vector" if cnt["m"] % 2 == 0 else "scalar"
        cnt["m"] += 1
        src = psb[0:W, bank * 1024: bank * 1024 + n * T]
        if eng == "vector":
            P.add("vector", lambda e: e.tensor_copy(out=dst_ap, in_=src), reads=[("pb", bank)], writes=[dst_key])
        else:
            P.add("scalar", lambda e: e.activation(out=dst_ap, in_=src, func=AF.Copy), reads=[("pb", bank)], writes=[dst_key])

    bg = []

    def drain(k=None):
        n = len(bg) if k is None else min(k, len(bg))
        for _ in range(n):
            bg.pop(0)()

    def load_and_project(T, x_ap, s2, W_cols, evac, defer=False, atomic=False):
        th = []
        xi = xin[s2]
        th.append(lambda: P.dma("sync", xi[0:T, :], x_ap, writes=[("xin", s2)]))
        th.append(lambda: P.add("gpsimd", lambda e: e.tensor_copy(out=xb[0:T, :], in_=xi[0:T, :]), reads=[("xin", s2)], writes=["xb"]))
        th.append(lambda: transpose_in(T, "xb", lambda c: xb[0:T, c * 128:(c + 1) * 128], 8, xT[:, :, 0:T], "xT"))
        ncg = (W_cols + 511) // 512
        for cg in range(ncg):
            n = min(512, W_cols - cg * 512)
            bank = cg % 2
            for c0 in range(0, 8, 4):
                def mm4(cg=cg, n=n, bank=bank, c0=c0):
                    for c in range(c0, c0 + 4):
                        P.add("tensor", lambda e, c=c: e.matmul(PB(bank, n, 0, T), lhsT=xT[:, c, 0:T], rhs=Warena[:, c, cg * 512: cg * 512 + n], start=(c == 0), stop=(c == 7)),
                              reads=["xT", ("W", cg)], writes=[("pb", bank)])
                th.append(mm4)
            th.append(lambda cg=cg, bank=bank, n=n: evac(cg, bank, n))
            if atomic:
                grp = th[-3:]
                del th[-3:]
                th.append(lambda grp=grp: [f() for f in grp])
        if defer:
            bg.extend(th)
        else:
            for f in th:
                f()

    def rope(T, base, nh, cs_ap_c, cs_ap_s):
        Hr = H[0:T, base: base + nh * 64].rearrange("p (h d) -> p h d", d=64)
        x1 = Hr[:, :, 0:8]
        x2 = Hr[:, :, 8:16]
        tm = [rtmp[0:T, i, 0:nh, :] for i in range(4)]
        P.add("vector", lambda e: e.tensor_tensor(out=tm[0], in0=x1, in1=cs_ap_c, op=ALU.mult), reads=["H", "cs"], writes=["rtmp"])
        P.add("vector", lambda e: e.tensor_tensor(out=tm[1], in0=x2, in1=cs_ap_s, op=ALU.mult), reads=["H", "cs"], writes=["rtmp"])
        P.add("vector", lambda e: e.tensor_tensor(out=tm[2], in0=x2, in1=cs_ap_c, op=ALU.mult), reads=["H", "cs"], writes=["rtmp2"])
        P.add("vector", lambda e: e.tensor_tensor(out=tm[3], in0=x1, in1=cs_ap_s, op=ALU.mult), reads=["H", "cs"], writes=["rtmp2"])
        P.add("vector", lambda e: e.tensor_tensor(out=x1, in0=tm[0], in1=tm[1], op=ALU.subtract), reads=["rtmp", "rtmp2"], writes=["H"])
        P.add("vector", lambda e: e.tensor_tensor(out=x2, in0=tm[2], in1=tm[3], op=ALU.add), reads=["rtmp", "rtmp2"], writes=["H"])

    def l0_front_a(T, x_ap, s2, defer=False):
        szb = sz[s2]

        def evac(cg, bank, n):
            if cg < 6:
                if cg % 2 == 0:
                    P.add("scalar", lambda e: e.activation(out=H[0:T, cg * 512:(cg + 1) * 512], in_=PB(bank, 512, 0, T), func=AF.Copy), reads=[("pb", bank)], writes=["H"])
                else:
                    P.add("vector", lambda e: e.tensor_copy(out=H[0:T, cg * 512:(cg + 1) * 512], in_=PB(bank, 512, 0, T)), reads=[("pb", bank)], writes=["H"])
            else:
                P.add("scalar", lambda e: e.activation(out=szb[0:T, (cg - 6) * 512:(cg - 5) * 512], in_=PB(bank, 512, 0, T), func=AF.Silu), reads=[("pb", bank)], writes=[("sz", s2)])
        load_and_project(T, x_ap, s2, IN0, evac, defer)

    def l0_casts(T):
        for G in range(4):
            qo = Hb[0:T, QBASE[G]:QBASE[G] + 512].rearrange("p (j hk d) -> p hk j d", hk=2, d=64)
            qi = H[0:T, QBASE[G]:QBASE[G] + 512].rearrange("p (hk j d) -> p hk j d", hk=2, d=64)
            if G % 2 == 0:
                P.add("gpsimd", lambda e, qo=qo, qi=qi: e.tensor_copy(out=qo, in_=qi), reads=["H"], writes=["Hb"])
            else:
                P.add("scalar", lambda e, qo=qo, qi=qi: e.activation(out=qo, in_=qi, func=AF.Copy), reads=["H"], writes=["Hb"])
        P.add("scalar", lambda e: e.activation(out=Hb[0:T, 512:640], in_=H[0:T, 512:640], func=AF.Copy), reads=["H"], writes=["Hb"])
        P.add("vector", lambda e: e.tensor_copy(out=Hb[0:T, 2304:2688], in_=H[0:T, 2304:2688]), reads=["H"], writes=["Hb"])

    def l0_front_b(T, csc, css):
        rope(T, 0, 10, csc(10), css(10))
        rope(T, 768, 30, csc(30), css(30))
        l0_casts(T)

    def l0_front(T, x_ap, s2, csc, css):
        l0_front_a(T, x_ap, s2)
        l0_front_b(T, csc, css)

    def q_src(T, G, j):
        b = QBASE[G] + 128 * j
        return Hb[0:T, b:b + 128]

    def l0_back(T, s2, row0, out_dram, acc_phase, defer_tail=False):
        for ph in range(2):
            acc_phase(ph)
            den_ap = bass.AP(ps[:].tensor, 5 * 512 + 64, [[4096, T], [128, 8], [1, 1]])
            if ph == 0:
                P.add("vector", lambda e: e.tensor_tensor(out=dn[0:T, :], in0=den_ap, in1=sinkexp[0:T, :], op=ALU.add), reads=[("pb", 5), ("pb", 6), "sinkexp"], writes=["dn"])
            else:
                P.add("vector", lambda e: e.tensor_copy(out=dn[0:T, :], in_=den_ap), reads=[("pb", 5), ("pb", 6)], writes=["dn"])
            P.add("vector", lambda e: e.reciprocal(out=rd[0:T, :], in_=dn[0:T, :]), reads=["dn"], writes=["rd"])
            num_ap = bass.AP(ps[:].tensor, 5 * 512, [[4096, T], [128, 8], [1, 64]])
            rd_bc = bc_last(rd, 0, 8, T, 8, 1, 64)
            mp = mixpre[0:T, ph * 512:(ph + 1) * 512].rearrange("p (h d) -> p h d", d=64)
            P.add("vector", lambda e, mp=mp: e.tensor_tensor(out=mp, in0=num_ap, in1=rd_bc, op=ALU.mult), reads=[("pb", 5), ("pb", 6), "rd"], writes=["mixpre"])
        szb = sz[s2]
        P.add("gpsimd", lambda e: e.tensor_tensor(out=mixb[0:T, :], in0=mixpre[0:T, :], in1=szb[0:T, :], op=ALU.mult), reads=["mixpre", ("sz", s2)], writes=["mixb"])
        tail(T, s2, 0, out_dram, defer_tail)

    def tail(T, s2, layer, out_dram, defer=False, atomic=False):
        xi = xin[s2]
        th = []
        th.append(lambda: transpose_in(T, "mixb", lambda c: mixb[0:T, c * 128:(c + 1) * 128], 8, mixT[:, :, 0:T], "mixT"))
        for n in range(2):
            for c0 in (0, 4):
                def mm4(n=n, c0=c0):
                    for c in range(c0, c0 + 4):
                        P.add("tensor", lambda e, c=c: e.matmul(PB(n, 512, 0, T), lhsT=mixT[:, c, 0:T], rhs=Woarena[:, c, n * 512:(n + 1) * 512], start=(c == 0), stop=(c == 7)),
                              reads=["mixT", "Wo"], writes=[("pb", n)])
                th.append(mm4)
            th.append(lambda n=n: P.add("vector", lambda e: e.scalar_tensor_tensor(out=rbuf[0:T, n * 512:(n + 1) * 512], in0=xi[0:T, n * 512:(n + 1) * 512], scalar=ALPHA, in1=PB(n, 512, 0, T), op0=ALU.mult, op1=ALU.add),
                                        reads=[("xin", s2), ("pb", n)], writes=["rbuf"]))
            if atomic:
                grp = th[-3:]
                del th[-3:]
                th.append(lambda grp=grp: [f() for f in grp])
        th.append(lambda: layer_norm(T, layer, rbuf[0:T, :]))
        th.append(lambda: P.dma("gpsimd", out_dram, rbuf[0:T, :], reads=["rbuf"], writes=["x1d"] if layer == 0 else []))
        if defer:
            bg.extend(th)
        else:
            for f in th:
                f()

    load_W(w_in0, IN0)
    load_Wo(w_out0)

    SB = (3, 4, 7)

    def front_b_prompt(t, defer=False):
        th = []
        csc = lambda nh, t=t: bc_mid(cs, t * 16, NT * 16, 128, nh, 8)
        css = lambda nh, t=t: bc_mid(cs, t * 16 + 8, NT * 16, 128, nh, 8)
        qTt = qTd[t % 2]
        qk = ("qT", t % 2)
        th.append(lambda: rope(128, 0, 10, csc(10), css(10)))
        th.append(lambda: rope(128, 768, 30, csc(30), css(30)))
        th.append(lambda: l0_casts(128))

        def kvpart(g):
            slot = t % RING[g]
            P.add("scalar", lambda e: e.activation(out=Vr[g][:, slot, :, 0:64], in_=H[:, VBASE[g]: VBASE[g] + 128].rearrange("p (a b) -> p a b", b=64), func=AF.Copy),
                  reads=["H"], writes=[("V", g, slot)])
            L = min(CACHE_L[g], S)
            r0 = t * 128 - (S - L)
            if r0 >= 0:
                kv_out = kvp[g][r0:r0 + 128, :]
                P.dma("gpsimd", kv_out[:, 0:128], H[:, KBASE[g]:KBASE[g] + 128], reads=["H"])
                P.dma("gpsimd", kv_out[:, 128:256], H[:, VBASE[g]:VBASE[g] + 128], reads=["H"])
        for g in range(4):
            th.append(lambda g=g: kvpart(g))
        for b0 in range(0, 16, 8):
            th.append(lambda b0=b0: transpose_in(128, "Hb", lambda c: q_src(128, (b0 + c) // 4, (b0 + c) % 4), 8, qTt[:, b0:b0 + 8, :], qk))
        for g in range(4):
            th.append(lambda g=g: transpose_in(128, "Hb", lambda c: Hb[:, KBASE[g]:KBASE[g] + 128], 1, kT[g][:, t % RING[g], :], ("kT", g, t % RING[g])))
        if defer:
            bg.extend(th)
        else:
            for f in th:
                f()

    l0_front_a(128, xp[0:128, :], 0)
    front_b_prompt(0)
    for t in range(NT):
        s2 = t % 2
        if t + 1 < NT:
            l0_front_a(128, xp[(t + 1) * 128:(t + 2) * 128, :], (t + 1) % 2, defer=True)
            front_b_prompt(t + 1, defer=True)
        qTt = qTd[t % 2]
        qk = ("qT", t % 2)

        def acc_phase(ph, t=t, qTt=qTt, qk=qk):
            groups = (0,) if ph == 0 else (1, 2, 3)
            plan = []
            for G in groups:
                d = DIL[G]
                for hk in range(2):
                    for o in range(d + 1):
                        if t - o < 0:
                            continue
                        mi = MIDX[d][0] if o == 0 else (MIDX[d][2] if o == d else MIDX[d][1])
                        plan.append((G, hk, o, mi))
            first = {}
            for i, (G, hk, o, mi) in enumerate(plan):
                first.setdefault(hk, i)
            bufs = {}

            def issue_S(i):
                G, hk, o, mi = plan[i]
                k3 = cnt["s"] % 5
                sbk = SB[cnt["s"] % 3]
                cnt["s"] += 1
                bufs[i] = k3
                Ek = Eb[k3]
                Pk = Pm[k3]
                slot = (t - o) % RING[G]
                P.add("tensor", lambda e: e.matmul(PB(sbk), lhsT=kT[G][64 * hk:64 * hk + 64, slot, :], rhs=qTt[64 * hk:64 * hk + 64, 4 * G:4 * G + 4, :], start=True, stop=True),
                      reads=[("kT", G, slot), qk], writes=[("pb", sbk)])
                P.add("scalar", lambda e: e.activation(out=Ek[:], in_=PB(sbk), func=AF.Exp, scale=0.125), reads=[("pb", sbk)], writes=[("E", k3)])
                meng = "vector"
                mk = bc_mid(maskt, mi * 128, 8 * 128, 128, 4, 128)
                P.add(meng, lambda e: e.tensor_tensor(out=Pk[:], in0=Ek[:], in1=mk, op=ALU.mult), reads=[("E", k3), "maskt"], writes=[("P", k3)])

            def issue_PV(i):
                G, hk, o, mi = plan[i]
                k3 = bufs[i]
                Pk = Pm[k3]
                slot = (t - o) % RING[G]
                for j in range(4):
                    P.add("tensor", lambda e, j=j: e.matmul(PB(5 + hk, 65, j * 128), lhsT=Pk[:, j * 128:(j + 1) * 128], rhs=Vr[G][:, slot, hk, :], start=(i == first[hk] and j == 0), stop=False, skip_group_check=True),
                          reads=[("P", k3), ("V", G, slot)], writes=[("pb", 5 + hk)])

            LA = 3
            for i in range(min(LA, len(plan))):
                issue_S(i)
            for i in range(len(plan)):
                issue_PV(i)
                if i + LA < len(plan):
                    issue_S(i + LA)
                drain(2)
            if ph == 1:
                drain()

        l0_back(128, s2, t * 128, x1d[t * 128:(t + 1) * 128, :], acc_phase, defer_tail=True)
    drain()
    xin[1] = xin[0]
    sz[1] = sz[0]
    del Eb[2:]
    del Pm[2:]

    fence()
    stA.close()
    stB = contextlib.ExitStack()
    cur["st"] = stB
    rtmp = sb("rtmpB", [TS, 4, 30, 8])
    cs_s = sb("cs_s", [TS, 1, 16])
    P.dma("sync", cs_s[:], cd["cs_s"], writes=["cs_s", "cs"])
    masknew = sb("masknew", [TS, 2, TS], BF)
    maskc = sb("maskc", [128, 10, 16], BF)
    qTs = sb("qTs", [128, 16, TS], BF)
    kTs = sb("kTs", [128, 4, TS], BF)
    Vs = sb("Vs", [TS, 8, 65], BF)
    ctile = [sb("ctile%d" % i, [128, 10, 256]) for i in range(2)]
    kcb = sb("kcb", [128, 10, 128], BF)
    vca = sb("vca", [128, 10, 2, 65], BF)
    kcT = sb("kcT", [128, 10, 128], BF)
    Pc = sb("Pc", [128, 320], BF)
    oT = sb("oT", [65, 4, 4 * TS])
    P.dma("gpsimd", masknew[:], cd["masknew"], writes=["masknew"])
    P.dma("gpsimd", maskc[:], cd["maskc"], writes=["maskc"])
    P.add("gpsimd", lambda e: e.memset(Vs[:].rearrange("p a b -> p (a b)"), 1.0), writes=["Vs"])
    P.add("gpsimd", lambda e: e.memset(vca[:].rearrange("p a b c -> p (a b c)"), 1.0), writes=[("vca", 0)])

    csc_s = lambda nh: bc_mid(cs_s, 0, 16, TS, nh, 8)
    css_s = lambda nh: bc_mid(cs_s, 8, 16, TS, nh, 8)
    l0_front(TS, xs[:, :], 0, csc_s, css_s)
    load_W(w_in1, IN1)
    for g in range(4):
        P.add("gpsimd", lambda e, g=g: e.tensor_copy(out=Vs[:, 2 * g:2 * g + 2, 0:64], in_=H[0:TS, VBASE[g]: VBASE[g] + 128].rearrange("p (a b) -> p a b", b=64)),
              reads=["H"], writes=["Vs"])
        Lg = CACHE_L[g]
        dst = bass.AP(kvs[g].tensor, (Lg - 4) * 256, [[Lg * 256, NS], [256, 4], [1, 128]])
        P.dma("gpsimd", dst, H[0:TS, KBASE[g]:KBASE[g] + 128], reads=["H"])
        dst2 = bass.AP(kvs[g].tensor, (Lg - 4) * 256 + 128, [[Lg * 256, NS], [256, 4], [1, 128]])
        P.dma("gpsimd", dst2, H[0:TS, VBASE[g]:VBASE[g] + 128], reads=["H"])
        for n in range(NS):
            P.dma("sync", kvs[g][n, 0:Lg - 4, :], caches[g][n, 4:Lg, :])
    for b0 in range(0, 16, 8):
        transpose_in(TS, "Hb", lambda c, b0=b0: q_src(TS, (b0 + c) // 4, (b0 + c) % 4), 8, qTs[:, b0:b0 + 8, :], "qTs")
    transpose_in(TS, "Hb", lambda c: Hb[0:TS, KBASE[c]:KBASE[c] + 128], 4, kTs[:, :, :], "kTs")

    def accT(ph, hk, n=4 * TS, off=0):
        return ps[0:65, (5 + ph) * 512 + hk * 256 + off: (5 + ph) * 512 + hk * 256 + off + n]

    firstb = {0: True, 1: True}
    for G in range(4):
        ph = 0 if G == 0 else 1
        for hk in range(2):
            sbk = 3 + cnt["s"] % 2
            ei = cnt["s"] % 2
            cnt["s"] += 1
            P.add("tensor", lambda e, G=G, hk=hk, sbk=sbk: e.matmul(PB(sbk, 4 * TS, 0, TS), lhsT=kTs[64 * hk:64 * hk + 64, G, :], rhs=qTs[64 * hk:64 * hk + 64, 4 * G:4 * G + 4, :], start=True, stop=True),
                  reads=["kTs", "qTs"], writes=[("pb", sbk)])
            P.add("scalar", lambda e, sbk=sbk, ei=ei: e.activation(out=Eb[ei][0:TS, 0:4 * TS], in_=PB(sbk, 4 * TS, 0, TS), func=AF.Exp, scale=0.125), reads=[("pb", sbk)], writes=[("E", ei)])
            mk = bc_mid(masknew, (0 if DIL[G] == 1 else 1) * TS, 2 * TS, TS, 4, TS)
            P.add("vector", lambda e, ei=ei, mk=mk: e.tensor_tensor(out=Pm[ei][0:TS, 0:4 * TS], in0=Eb[ei][0:TS, 0:4 * TS], in1=mk, op=ALU.mult), reads=[("E", ei), "masknew"], writes=[("P", ei)])
            st_flag = firstb[ph]
            firstb[ph] = False
            P.add("tensor", lambda e, ei=ei, G=G, hk=hk, ph=ph, st_flag=st_flag: e.matmul(accT(ph, hk), lhsT=Vs[:, 2 * G + hk, :], rhs=Pm[ei][0:TS, 0:4 * TS], start=st_flag, stop=False, skip_group_check=True),
                  reads=[("P", ei), "Vs"], writes=[("pb", 5 + ph)])
    TILES = [(0, 0, 1), (1, 0, 1)] + [(2, r, 4) for r in range(4)] + [(3, r, 16) for r in range(4)]
    kcbs = [kcb, sb("kcb2", [128, 10, 128], BF)]
    vcas = [vca, sb("vca2", [128, 10, 2, 65], BF)]
    kcTs = [kcT, sb("kcT2", [128, 10, 128], BF)]
    Pcs = [Pc, sb("Pc2", [128, 320], BF)]
    vca2_ = vcas[1]
    P.add("gpsimd", lambda e: e.memset(vca2_[:].rearrange("p a b c -> p (a b c)"), 1.0), writes=[("vca", 1)])
    mkc = bass.AP(maskc[:].tensor, 0, [[160, 128], [16, 10], [0, 2], [1, 16]])

    def s_stage1(n):
        cb = n % 2
        kcb_, vca_, kcT_, Pc_, E_ = kcbs[cb], vcas[cb], kcTs[cb], Pcs[cb], Eb[cb]
        ct_ = ctile[cb]
        sbk = 3 + cb
        for ti, (G, r, stp) in enumerate(TILES):
            src = bass.AP(caches[G].tensor, n * CACHE_L[G] * 256 + r * 256, [[stp * 256, 128], [1, 256]])
            P.dma("sync", ct_[:, ti, :], src, writes=[("ctile", cb)])
        P.add("gpsimd", lambda e: e.tensor_copy(out=kcb_[:], in_=ct_[:, :, 0:128]), reads=[("ctile", cb)], writes=[("kcb", cb)])
        P.add("vector", lambda e: e.tensor_copy(out=vca_[:, :, :, 0:64], in_=ct_[:, :, 128:256].rearrange("p t (a b) -> p t a b", b=64)), reads=[("ctile", cb)], writes=[("vca", cb)])
        for b0 in (0, 5):
            transpose_in(128, ("kcb", cb), lambda c, b0=b0: kcb_[:, b0 + c, :], 5, kcT_[:, b0:b0 + 5, :], ("kcT", cb))
        for ti, (G, r, stp) in enumerate(TILES):
            for hk in range(2):
                for j in range(4):
                    P.add("tensor", lambda e, ti=ti, hk=hk, j=j, G=G: e.matmul(PB(sbk, 4, (ti * 2 + hk) * 16 + 4 * j), lhsT=kcT_[64 * hk:64 * hk + 64, ti, :], rhs=qTs[64 * hk:64 * hk + 64, 4 * G + j, 4 * n:4 * n + 4], start=(ti == 0 and hk == 0 and j == 0), stop=False, skip_group_check=True),
                          reads=[("kcT", cb), "qTs"], writes=[("pb", sbk)])
        P.add("scalar", lambda e: e.activation(out=E_[:, 0:320], in_=PB(sbk, 320), func=AF.Exp, scale=0.125), reads=[("pb", sbk)], writes=[("E", cb)])
        P.add("vector", lambda e: e.tensor_tensor(out=Pc_[:].rearrange("p (t h c) -> p t h c", h=2, c=16), in0=E_[:, 0:320].rearrange("p (t h c) -> p t h c", h=2, c=16), in1=mkc, op=ALU.mult), reads=[("E", cb), "maskc"], writes=[("Pc", cb)])

    def s_stage2(n):
        cb = n % 2
        vca_, Pc_ = vcas[cb], Pcs[cb]
        for ti, (G, r, stp) in enumerate(TILES):
            ph = 0 if G == 0 else 1
            for hk in range(2):
                for j in range(4):
                    c0 = (5 + ph) * 512 + hk * 256 + j * TS + 4 * n
                    P.add("tensor", lambda e, ti=ti, hk=hk, j=j, c0=c0: e.matmul(ps[0:65, c0:c0 + 4], lhsT=vca_[:, ti, hk, :], rhs=Pc_[:, (ti * 2 + hk) * 16 + 4 * j:(ti * 2 + hk) * 16 + 4 * j + 4], start=False, stop=False, skip_group_check=True),
                          reads=[("Pc", cb), ("vca", cb)], writes=[("pb", 5 + ph)])

    s_stage1(0)
    for n in range(NS):
        if n + 1 < NS:
            s_stage1(n + 1)
        s_stage2(n)
    accT_all = bass.AP(ps[:].tensor, 5 * 512, [[4096, 65], [512, 2], [256, 2], [1, 4 * TS]])
    P.add("vector", lambda e: e.tensor_copy(out=oT[:].rearrange("p (a b) c -> p a b c", b=2), in_=accT_all), reads=[("pb", 5), ("pb", 6)], writes=["oT"])

    def acc_phase_s(ph):
        for hk in range(2):
            for j in range(4):
                P.add("tensor", lambda e, hk=hk, j=j, ph=ph: e.transpose(out=PB(5 + hk, 65, j * 128, TS), in_=oT[0:65, ph * 2 + hk, j * TS:(j + 1) * TS], identity=ident_f[0:65, 0:65]),
                      reads=["oT", "ident_f"], writes=[("pb", 5 + hk)])

    l0_back(TS, 0, 0, x1d[S:S + TS, :], acc_phase_s)

    fence()
    stB.close()
    cur["st"] = st
    load_Wo(w_out1)
    U128 = sb("U128", [128, 128]); cmask128 = sb("cmask128", [128, 128]); sellast = sb("sellast", [128, 128])
    U_s = sb("U_s", [TS, TS]); cmask_s = sb("cmask_s", [TS, TS]); selend_s = sb("selend_s", [TS, TS])
    E4 = sb("E4", [4, 4, 128]); bgb = sb("bgb", [128, 8]); mhb = sb("mhb", [128, D])
    onehot = sb("onehot", [TS, NS])
    for tl, nm in ((U128, "U128"), (cmask128, "cmask128"), (sellast, "sellast128"), (U_s, "U_s"), (cmask_s, "cmask_s"),
                   (selend_s, "selend_s"), (E4, "E4"), (onehot, "onehot_s")):
        P.dma("sync", tl[:], cd[nm], writes=[nm])
    P.dma("sync", bgb[:], bg_bc, writes=["bgb"])
    P.dma("sync", mhb[:], mh_bc, writes=["mhb"])
    vaug = sb("vaug", [128, 4, 257], BF)
    gt = sb("gt", [128, 8]); sp = sb("sp", [128, 4]); a_t = sb("a_t", [128, 4]); aT = sb("aT", [4, 128])
    Dm = sb("Dm", [128, 4, 128]); pw = sb("pw", [128, 4, 128]); cm = sb("cm", [128, 4]); gg = sb("gg", [128, 4]); negg = sb("negg", [128, 4])
    mpv = sb("mpv", [128, 4]); cw = sb("cw", [128, 4]); mtok = sb("mtok", [128, 4]); emm = sb("emm", [128, 4]); wsb = sb("wsb", [128, 4]); dec = sb("dec", [128, 4])
    tmp4 = sb("tmp4", [128, 4]); dint = sb("dint", [128, 4]); den = sb("den", [128, 4]); rdd = sb("rdd", [128, 4])
    onec = sb("onec", [128, 1]); scb = sb("scb", [128, 4, 128], BF)
    Cst = sb("Cst", [128, 4, 2, 257]); Cb = sb("Cb", [128, 4, 2, 257], BF)
    sth = sb("sth", [128, 4, 6]); mvh = sb("mvh", [128, 4, 2]); rsh = sb("rsh", [128, 4])
    tmpq = rbuf
    P.add("gpsimd", lambda e: e.memset(onec[:], 1.0), writes=["onec"])
    P.add("gpsimd", lambda e: e.memset(vaug[:].rearrange("p a b -> p (a b)"), 1.0), writes=["vaug", ("vaug1", 0)])
    P.add("gpsimd", lambda e: e.memset(Cst[:].rearrange("p a b c -> p (a b c)"), 0.0), writes=["Cst"])
    P.add("gpsimd", lambda e: e.memset(Cb[:].rearrange("p a b c -> p (a b c)"), 0.0), writes=["Cb"])
    P.add("gpsimd", lambda e: e.memset(mtok[:], 0.0), writes=["mtok"])
    hnum = mixpre
    Hs = [H, H]; Hbs = [Hb, Hb]; vaugs = [vaug, vaug]; gts = [gt, gt]

    def mlstm(T, x_ap, s2, out_dram, sample, p=0, mode="all", deftail=False):
        H_ = Hs[p]; Hb_ = Hbs[p]; sz_ = sz[p]; vaug_ = vaugs[p]; gt_ = gts[p]
        if sample:
            kH, kHb, ksz, kva, kgt = "H", "Hb", ("sz", 0), "vaug", "gt"
        else:
            kH, kHb, ksz, kva, kgt = ("H1", p), ("Hb1", p), ("sz", p), ("vaug1", p), ("gt1", p)
        DR = 4
        def evac(cg, bank, n):
            src = PB(bank, n, 0, T)
            k_ = ("pb", bank)
            if cg < 2:
                if sample:
                    P.add("scalar", lambda e: e.activation(out=Hb_[0:T, cg * 512:(cg + 1) * 512], in_=src, func=AF.Copy), reads=[k_], writes=[kHb])
                else:
                    P.add("vector", lambda e: e.tensor_copy(out=Hb_[0:T, cg * 512:(cg + 1) * 512], in_=src), reads=[k_], writes=[kHb])
                if sample:
                    P.add("scalar", lambda e: e.activation(out=H_[0:T, 2048 + cg * 512: 2048 + (cg + 1) * 512], in_=src, func=AF.Copy), reads=[k_], writes=[kH])
            elif cg < 4:
                c0 = (cg - 2) * 512
                P.add("scalar", lambda e: e.activation(out=H_[0:T, c0:c0 + 512], in_=src, func=AF.Copy, scale=1.0 / 16.0), reads=[k_], writes=[kH])
                P.add("gpsimd", lambda e: e.tensor_copy(out=Hb_[0:T, 1024 + c0:1024 + c0 + 512], in_=H_[0:T, c0:c0 + 512]), reads=[kH], writes=[kHb])
            elif cg < 6:
                h0 = 2 * (cg - 4)
                if sample:
                    P.add("scalar", lambda e: e.activation(out=vaug_[0:T, h0:h0 + 2, 0:256], in_=src.rearrange("p (a b) -> p a b", b=256), func=AF.Copy), reads=[k_], writes=[kva])
                else:
                    P.add("vector", lambda e: e.tensor_copy(out=vaug_[0:T, h0:h0 + 2, 0:256], in_=src.rearrange("p (a b) -> p a b", b=256)), reads=[k_], writes=[kva])
                if sample:
                    P.add("scalar", lambda e: e.activation(out=vf[0:T, h0:h0 + 2, :], in_=src.rearrange("p (a b) -> p a b", b=256), func=AF.Copy), reads=[k_], writes=["vf"])
            elif cg < 8:
                c0 = (cg - 6) * 512
                P.add("scalar", lambda e: e.activation(out=H_[0:T, 1024 + c0:1024 + c0 + 512], in_=src, func=AF.Sigmoid), reads=[k_], writes=[kH])
            elif cg < 10:
                c0 = (cg - 8) * 512
                P.add("scalar", lambda e: e.activation(out=sz_[0:T, c0:c0 + 512], in_=src, func=AF.Silu), reads=[k_], writes=[ksz])
            else:
                P.add("vector", lambda e: e.tensor_tensor(out=gt_[0:T, :], in0=src, in1=bgb[0:T, :], op=ALU.add), reads=[k_, "bgb"], writes=[kgt])
        if mode in ("all", "front"):
            load_and_project(T, x_ap, s2, IN1, evac, defer=(mode == "front"), atomic=True)
        if mode == "front":
            return
        Um, cmk, sel = (U_s, cmask_s, selend_s) if sample else (U128, cmask128, sellast)
        Uk, cmkk, selk = ("U_s", "cmask_s", "selend_s") if sample else ("U128", "cmask128", "sellast128")
        G7 = lambda off, n=4: PB(7, n, off, T)
        P.add("scalar", lambda e: e.activation(out=sp[0:T, :], in_=gt_[0:T, 4:8], func=AF.Exp, scale=-1.0), reads=[kgt], writes=["sp"])
        P.add("scalar", lambda e: e.activation(out=sp[0:T, :], in_=sp[0:T, :], func=AF.Ln, bias=onec[0:T, :], scale=1.0), reads=["sp", "onec"], writes=["sp"])
        P.add("tensor", lambda e: e.matmul(G7(0), lhsT=Um[0:T, 0:T], rhs=sp[0:T, :], start=True, stop=True, skip_group_check=True), reads=["sp", Uk], writes=[("pb", 7)])
        P.add("vector", lambda e: e.tensor_tensor(out=a_t[0:T, :], in0=G7(0), in1=gt_[0:T, 0:4], op=ALU.add), reads=[("pb", 7), kgt], writes=["a_t"])
        if sample:
            P.dma("sync", mpv[0:T, :], dr["cmax_tok"], writes=["mpv"])
        else:
            P.add("tensor", lambda e: e.matmul(G7(8), lhsT=sel[0:T, 0:T], rhs=mtok[0:T, :], start=False, stop=False, skip_group_check=True), reads=["mtok", selk], writes=[("pb", 7)])
            P.add("vector", lambda e: e.tensor_copy(out=mpv[0:T, :], in_=G7(8)), reads=[("pb", 7)], writes=["mpv"])
        P.add("tensor", lambda e: e.transpose(out=ps[0:4, 7 * 512 + 128: 7 * 512 + 128 + T], in_=a_t[0:T, :], identity=ident_f[0:T, 0:T]), reads=["a_t", "ident_f"], writes=[("pb", 7)])
        P.add("vector", lambda e: e.tensor_copy(out=aT[:, 0:T], in_=ps[0:4, 7 * 512 + 128: 7 * 512 + 128 + T]), reads=[("pb", 7)], writes=["aT"])
        drain(DR)
        for h in range(4):
            P.add("tensor", lambda e, h=h: e.matmul(PB(4, T, h * 128, T), lhsT=E4[0:4, h, 0:T], rhs=aT[0:4, 0:T], start=(h == 0), stop=False, skip_group_check=True), reads=["aT", "E4"], writes=[("pb", 4)])
        cmk_bc = bc_mid(cmk, 0, T, T, 4, T)
        A_ps = bass.AP(ps[:].tensor, 4 * 512, [[4096, T], [128, 4], [1, T]])
        P.add("vector", lambda e: e.tensor_tensor(out=Dm[0:T, :, 0:T], in0=A_ps, in1=cmk_bc, op=ALU.add), reads=[("pb", 4), cmkk], writes=["Dm"])
        P.add("vector", lambda e: e.tensor_reduce(out=cm[0:T, :], in_=Dm[0:T, :, 0:T], axis=AX.X, op=ALU.max), reads=["Dm"], writes=["cm"])
        P.add("vector", lambda e: e.tensor_tensor(out=gg[0:T, :], in0=cm[0:T, :], in1=mpv[0:T, :], op=ALU.max), reads=["cm", "mpv"], writes=["gg"])
        drain(DR)
        P.add("vector", lambda e: e.tensor_scalar(out=negg[0:T, :], in0=gg[0:T, :], scalar1=-1.0, scalar2=None, op0=ALU.mult), reads=["gg"], writes=["negg"])
        for h in range(4):
            P.add("scalar", lambda e, h=h: e.activation(out=pw[0:T, h, 0:T], in_=Dm[0:T, h, 0:T], func=AF.Exp, bias=negg[0:T, h:h + 1], scale=1.0), reads=["Dm", "negg"], writes=["pw"])
        drain(DR)
        P.add("vector", lambda e: e.tensor_tensor(out=tmp4[0:T, :], in0=mpv[0:T, :], in1=gg[0:T, :], op=ALU.subtract), reads=["mpv", "gg"], writes=["tmp4"])
        P.add("scalar", lambda e: e.activation(out=cw[0:T, :], in_=tmp4[0:T, :], func=AF.Exp), reads=["tmp4"], writes=["cw"])
        drain(DR)
        P.add("tensor", lambda e: e.matmul(G7(12), lhsT=sel[0:T, 0:T], rhs=gg[0:T, :], start=False, stop=False, skip_group_check=True), reads=["gg", selk], writes=[("pb", 7)])
        P.add("vector", lambda e: e.tensor_tensor(out=mtok[0:T, :], in0=gg[0:T, :], in1=G7(0), op=ALU.subtract), reads=["gg", ("pb", 7)], writes=["mtok"])
        P.add("scalar", lambda e: e.activation(out=emm[0:T, :], in_=mtok[0:T, :], func=AF.Exp, scale=-1.0), reads=["mtok"], writes=["emm"])
        P.add("vector", lambda e: e.tensor_tensor(out=tmp4[0:T, :], in0=a_t[0:T, :], in1=G7(12), op=ALU.subtract), reads=["a_t", ("pb", 7), "cw"], writes=["tmp4"])
        P.add("scalar", lambda e: e.activation(out=wsb[0:T, :], in_=tmp4[0:T, :], func=AF.Exp), reads=["tmp4"], writes=["wsb"])
        P.add("vector", lambda e: e.tensor_tensor(out=tmp4[0:T, :], in0=mpv[0:T, :], in1=G7(12), op=ALU.subtract), reads=["mpv", ("pb", 7), "wsb"], writes=["tmp4"])
        P.add("scalar", lambda e: e.activation(out=dec[0:T, :], in_=tmp4[0:T, :], func=AF.Exp), reads=["tmp4"], writes=["dec"])
        drain(DR)
        transpose_in(T, kHb, lambda c: Hb_[0:T, c * 128:(c + 1) * 128], 8, qT[:, 0:8, 0:T], "qT")
        transpose_in(T, kHb, lambda c: Hb_[0:T, 1024 + c * 128:1024 + (c + 1) * 128], 8, qT[:, 8:16, 0:T], "qT")
        for h in range(4):
            for c in range(2):
                P.add("tensor", lambda e, h=h, c=c: e.matmul(PB(3, T, h * 128, T), lhsT=qT[:, 2 * h + c, 0:T], rhs=qT[:, 8 + 2 * h + c, 0:T], start=(h == 0 and c == 0), stop=False, skip_group_check=True),
                      reads=["qT"], writes=[("pb", 3)])
        drain(DR)
        QK_ps = bass.AP(ps[:].tensor, 3 * 512, [[4096, T], [128, 4], [1, T]])
        P.add("vector", lambda e: e.tensor_tensor(out=Dm[0:T, :, 0:T], in0=QK_ps, in1=pw[0:T, :, 0:T], op=ALU.mult), reads=[("pb", 3), "pw", "cm"], writes=["Dm"])
        P.add("vector", lambda e: e.tensor_reduce(out=dint[0:T, :], in_=Dm[0:T, :, 0:T], axis=AX.X, op=ALU.add), reads=["Dm"], writes=["dint"])
        drain(DR)
        P.add("gpsimd", lambda e: e.tensor_copy(out=scb[0:T, :, 0:T], in_=Dm[0:T, :, 0:T]), reads=["Dm"], writes=["scb"])
        transpose_in(T, "scb", lambda c: scb[0:T, c, 0:T], 4, Eb[0][0:T, 0:4 * T], ("E", 0), W=T)
        for h in range(4):
            P.add("tensor", lambda e, h=h: e.matmul(PB(5 + h // 2, 256, (h % 2) * 256, T), lhsT=Eb[0][0:T, h * T:(h + 1) * T], rhs=vaug_[0:T, h, 0:256], start=(h % 2 == 0), stop=False, skip_group_check=True),
                  reads=[("E", 0), kva], writes=[("pb", 5 + h // 2)])
        drain(DR)
        if not sample:
            for h in range(4):
                for c in range(2):
                    P.add("tensor", lambda e, h=h, c=c: e.matmul(PB(h // 2, 256, (h % 2) * 256), lhsT=qT[:, 2 * h + c, :], rhs=Cb[:, h, c, 0:256], start=(h % 2 == 0 and c == 0), stop=False, skip_group_check=True),
                          reads=["qT", "Cb"], writes=[("pb", h // 2)])
            for h in range(4):
                for c in range(2):
                    P.add("tensor", lambda e, h=h, c=c: e.matmul(PB(7, 1, 20 + h), lhsT=qT[:, 2 * h + c, :], rhs=Cb[:, h, c, 256:257], start=False, stop=False, skip_group_check=True),
                          reads=["qT", "Cb"], writes=[("pb", 7)])
            for h in range(4):
                P.add("scalar", lambda e, h=h: e.activation(out=tmpq[:, h * 256:(h + 1) * 256], in_=PB(h // 2, 256, (h % 2) * 256), func=AF.Copy, scale=cw[:, h:h + 1]), reads=[("pb", h // 2), "cw"], writes=["rbuf"])
            P.add("vector", lambda e: e.tensor_tensor(out=den[:, :], in0=G7(20), in1=cw[:, :], op=ALU.mult), reads=[("pb", 7), "cw"], writes=["den"])
        else:
            sample_inter(T)
        drain(DR)
        P.add("vector", lambda e: e.tensor_tensor(out=hnum[0:T, :], in0=ps[0:T, 5 * 512:7 * 512], in1=tmpq[0:T, :], op=ALU.add), reads=[("pb", 5), ("pb", 6), "rbuf"], writes=["mixpre"])
        P.add("vector", lambda e: e.tensor_tensor(out=den[0:T, :], in0=den[0:T, :], in1=dint[0:T, :], op=ALU.add), reads=["den", "dint"], writes=["den"])
        P.add("vector", lambda e: e.tensor_scalar(out=tmp4[0:T, :], in0=den[0:T, :], scalar1=-1.0, scalar2=None, op0=ALU.mult), reads=["den", "dec"], writes=["tmp4"])
        P.add("vector", lambda e: e.tensor_tensor(out=den[0:T, :], in0=den[0:T, :], in1=tmp4[0:T, :], op=ALU.max), reads=["den", "tmp4"], writes=["den"])
        P.add("vector", lambda e: e.tensor_tensor(out=den[0:T, :], in0=den[0:T, :], in1=emm[0:T, :], op=ALU.max), reads=["den", "emm"], writes=["den"])
        P.add("vector", lambda e: e.reciprocal(out=rdd[0:T, :], in_=den[0:T, :]), reads=["den"], writes=["rdd"])
        for h in range(4):
            P.add("vector", lambda e, h=h: e.scalar_tensor_tensor(out=hnum[0:T, h * 256:(h + 1) * 256], in0=hnum[0:T, h * 256:(h + 1) * 256], scalar=rdd[0:T, h:h + 1], in1=H_[0:T, 1024 + h * 256:1024 + (h + 1) * 256], op0=ALU.mult, op1=ALU.mult),
                  reads=["mixpre", "rdd", kH], writes=["mixpre"])
            P.add("vector", lambda e, h=h: e.bn_stats(out=sth[0:T, h, :], in_=hnum[0:T, h * 256:(h + 1) * 256]), reads=["mixpre"], writes=["sth"])
            P.add("vector", lambda e, h=h: e.bn_aggr(out=mvh[0:T, h, :], in_=sth[0:T, h, :]), reads=["sth"], writes=["mvh"])
        drain(DR)
        P.add("vector", lambda e: e.tensor_scalar(out=rsh[0:T, :], in0=mvh[0:T, :, 1], scalar1=MH_EPS, scalar2=None, op0=ALU.add), reads=["mvh"], writes=["rsh"])
        P.add("scalar", lambda e: e.activation(out=rsh[0:T, :], in_=rsh[0:T, :], func=AF.Sqrt), reads=["rsh"], writes=["rsh"])
        P.add("vector", lambda e: e.reciprocal(out=rsh[0:T, :], in_=rsh[0:T, :]), reads=["rsh"], writes=["rsh"])
        for h in range(4):
            P.add("vector", lambda e, h=h: e.tensor_scalar(out=hnum[0:T, h * 256:(h + 1) * 256], in0=hnum[0:T, h * 256:(h + 1) * 256], scalar1=mvh[0:T, h, 0:1], scalar2=rsh[0:T, h:h + 1], op0=ALU.subtract, op1=ALU.mult),
                  reads=["mixpre", "mvh", "rsh"], writes=["mixpre"])
        drain(DR)
        P.add("gpsimd", lambda e: e.tensor_tensor(out=hnum[0:T, :], in0=hnum[0:T, :], in1=mhb[0:T, :], op=ALU.mult), reads=["mixpre", "mhb"], writes=["mixpre"])
        P.add("gpsimd", lambda e: e.tensor_tensor(out=mixb[0:T, :], in0=hnum[0:T, :], in1=sz_[0:T, :], op=ALU.mult), reads=["mixpre", ksz], writes=["mixb"])
        tail(T, s2, 1, out_dram, defer=deftail, atomic=True)
        drain(DR)
        if not sample:
            for h in range(4):
                P.add("gpsimd", lambda e, h=h: e.tensor_scalar(out=Hb_[:, 2048 + h * 256:2048 + (h + 1) * 256], in0=H_[:, h * 256:(h + 1) * 256], scalar1=wsb[:, h:h + 1], scalar2=None, op0=ALU.mult), reads=[kH, "wsb"], writes=[kHb])
            for h in range(4):
                for c in range(2):
                    bank = 3 + (h * 2 + c) % 2
                    P.add("tensor", lambda e, h=h, c=c, bank=bank: e.matmul(PB(bank, 257), lhsT=Hb_[:, 2048 + h * 256 + c * 128: 2048 + h * 256 + (c + 1) * 128], rhs=vaug_[:, h, :], start=True, stop=True),
                          reads=[kHb, kva], writes=[("pb", bank)])
                    P.add("vector", lambda e, h=h, c=c, bank=bank: e.scalar_tensor_tensor(out=Cst[:, h, c, :], in0=Cst[:, h, c, :], scalar=dec[:, h:h + 1], in1=PB(bank, 257), op0=ALU.mult, op1=ALU.add),
                          reads=["Cst", "dec", ("pb", bank)], writes=["Cst"])
            P.add("gpsimd", lambda e: e.tensor_copy(out=Cb[:].rearrange("p a b c -> p (a b c)"), in_=Cst[:].rearrange("p a b c -> p (a b c)")), reads=["Cst"], writes=["Cb"])
        else:
            sample_state(T)

    cmax_tok = din("cmax_tok", [TS, 4])
    cnorm_tok = din("cnorm_tok", [TS, 1024])

    def sample_inter(T):
        for c in range(8):
            P.add("tensor", lambda e, c=c: e.transpose(out=PB(4, T, c * T), in_=H[0:T, 2048 + c * 128: 2048 + (c + 1) * 128], identity=ident_f[0:T, 0:T]), reads=["H", "ident_f"], writes=[("pb", 4)])
        P.add("vector", lambda e: e.tensor_copy(out=qTf[:].rearrange("p a b -> p (a b)"), in_=PB(4, 8 * T)), reads=[("pb", 4)], writes=["qTf"])
        P.add("gpsimd", lambda e: e.memset(tmpq[0:T, :], 0.0), writes=["rbuf"])
        for n in range(NS):
            for h in range(4):
                cb = (n * 4 + h) % 2
                P.dma("sync", Cs[cb][:], cmem[n, h].rearrange("(c p) v -> p c v", p=128), writes=[("Cs", cb)])
                for c in range(2):
                    P.add("tensor", lambda e, h=h, c=c, cb=cb: e.matmul(PB(h // 2, 256, (h % 2) * 256, T), lhsT=qTf[:, 2 * h + c, :], rhs=Cs[cb][:, c, :], start=(h % 2 == 0 and c == 0), stop=False, skip_group_check=True),
                          reads=["qTf", ("Cs", cb)], writes=[("pb", h // 2)])
            P.add("vector", lambda e, n=n: e.scalar_tensor_tensor(out=tmpq[0:T, :], in0=ps[0:T, 0:1024], scalar=onehot[0:T, n:n + 1], in1=tmpq[0:T, :], op0=ALU.mult, op1=ALU.add), reads=[("pb", 0), ("pb", 1), "onehot_s", "rbuf"], writes=["rbuf"])
        for h in range(4):
            P.add("vector", lambda e, h=h: e.tensor_scalar(out=tmpq[0:T, h * 256:(h + 1) * 256], in0=tmpq[0:T, h * 256:(h + 1) * 256], scalar1=cw[0:T, h:h + 1], scalar2=None, op0=ALU.mult), reads=["rbuf", "cw"], writes=["rbuf"])
        P.dma("sync", kwf[0:TS, :], cnorm_tok, writes=["mixpre"])
        P.add("gpsimd", lambda e: e.tensor_tensor(out=kwn[0:T, :], in0=H[0:T, 2048:3072], in1=kwf[0:T, :], op=ALU.mult), reads=["H", "mixpre"], writes=["kwn"])
        P.add("vector", lambda e: e.tensor_reduce(out=den[0:T, :], in_=kwn[0:T, :].rearrange("p (h d) -> p h d", d=256), axis=AX.X, op=ALU.add), reads=["kwn"], writes=["den"])
        P.add("vector", lambda e: e.tensor_tensor(out=den[0:T, :], in0=den[0:T, :], in1=cw[0:T, :], op=ALU.mult), reads=["den", "cw"], writes=["den"])

    def sample_state(T):
        for h in range(4):
            P.add("gpsimd", lambda e, h=h: e.tensor_scalar(out=kwf[0:T, h * 256:(h + 1) * 256], in0=H[0:T, h * 256:(h + 1) * 256], scalar1=wsb[0:T, h:h + 1], scalar2=None, op0=ALU.mult), reads=["H", "wsb"], writes=["mixpre"])
        P.add("gpsimd", lambda e: e.memset(quart[:], 0.25), writes=["quart"])
        P.add("vector", lambda e: e.tensor_tensor(out=Xd[0:T, :, :], in0=bc_last(onehot, 0, NS, T, NS, 1, 4), in1=bc_mid(dec, 0, 4, T, NS, 4), op=ALU.mult), reads=["dec", "onehot_s"], writes=["Xd"])
        P.add("tensor", lambda e: e.matmul(PB(7, 4 * NS, 64), lhsT=quart[0:T, :], rhs=Xd[0:T].rearrange("p a b -> p (a b)"), start=False, stop=False, skip_group_check=True), reads=["Xd", "quart"], writes=[("pb", 7)])
        P.add("vector", lambda e: e.tensor_copy(out=decb[:].rearrange("p a b -> p (a b)"), in_=PB(7, 4 * NS, 64)), reads=[("pb", 7)], writes=["decb"])
        P.add("tensor", lambda e: e.matmul(ps[0:NS, 7 * 512 + 256: 7 * 512 + 260], lhsT=onehot[0:T, :], rhs=dec[0:T, :], start=False, stop=False, skip_group_check=True), reads=["dec", "onehot_s"], writes=[("pb", 7)])
        P.add("vector", lambda e: e.tensor_scalar(out=decn[:], in0=ps[0:NS, 7 * 512 + 256: 7 * 512 + 260], scalar1=0.25, scalar2=None, op0=ALU.mult), reads=[("pb", 7)], writes=["decn"])
        P.dma("sync", kwn[0:NS, :], cnorm, writes=["kwn"])
        for i in range(2):
            P.add("tensor", lambda e, i=i: e.matmul(ps[0:NS, (3 + i) * 512:(4 + i) * 512], lhsT=onehot[0:T, :], rhs=kwf[0:T, i * 512:(i + 1) * 512], start=True, stop=True), reads=["mixpre", "onehot_s"], writes=[("pb", 3 + i)])
        for h in range(4):
            P.add("vector", lambda e, h=h: e.scalar_tensor_tensor(out=kwn[0:NS, h * 256:(h + 1) * 256], in0=kwn[0:NS, h * 256:(h + 1) * 256], scalar=decn[:, h:h + 1], in1=ps[0:NS, 3 * 512 + h * 256: 3 * 512 + (h + 1) * 256], op0=ALU.mult, op1=ALU.add),
                  reads=["kwn", "decn", ("pb", 3), ("pb", 4)], writes=["kwn"])
        P.dma("gpsimd", cnorm_s, kwn[0:NS, :], reads=["kwn"])
        P.dma("gpsimd", cmax_s, mtok[0:T, :], reads=["mtok"])
        for n in range(NS):
            P.add("gpsimd", lambda e, n=n: e.tensor_scalar(out=kwn[0:T, :], in0=kwf[0:T, :], scalar1=onehot[0:T, n:n + 1], scalar2=None, op0=ALU.mult), reads=["mixpre", "onehot_s"], writes=["kwn"])
            for h in range(4):
                cb = (n * 4 + h) % 2
                P.dma("sync", Cs[cb][:], cmem[n, h].rearrange("(c p) v -> p c v", p=128), writes=[("Cs", cb)])
                for c in range(2):
                    bank = 3 + c
                    P.add("tensor", lambda e, h=h, c=c, bank=bank: e.matmul(PB(bank, 256), lhsT=kwn[0:T, h * 256 + c * 128: h * 256 + (c + 1) * 128], rhs=vf[0:T, h, :], start=True, stop=True), reads=["kwn", "vf"], writes=[("pb", bank)])
                    P.add("vector", lambda e, h=h, c=c, bank=bank, cb=cb, n=n: e.scalar_tensor_tensor(out=Cn[cb][:, c, :], in0=Cs[cb][:, c, :], scalar=decb[:, n, h:h + 1], in1=PB(bank, 256), op0=ALU.mult, op1=ALU.add),
                          reads=[("Cs", cb), "decb", ("pb", bank)], writes=[("Cn", cb)])
                P.dma("gpsimd", cmem_s[n, h].rearrange("(c p) v -> p c v", p=128), Cn[cb][:], reads=[("Cn", cb)])

    P.dma("sync", lng[:], lng_bc[:, 1, :], writes=["lng"])
    P.dma("sync", lnb[:], lnb_bc[:, 1, :], writes=["lnb"])
    stC = contextlib.ExitStack()
    cur["st"] = stC
    Hs[1] = sb("H2", [128, 2048]); Hbs[1] = sb("Hb2", [128, 3072], BF); vaugs[1] = sb("vaug2", [128, 4, 257], BF); gts[1] = sb("gt2", [128, 8])
    sz[1] = sb("sz2", [128, D]); xin[1] = sb("xin2", [128, D])
    cur["st"] = st
    va2 = vaugs[1]
    P.add("gpsimd", lambda e: e.memset(va2[:].rearrange("p a b -> p (a b)"), 1.0), writes=[("vaug1", 1)])
    mlstm(128, x1d[0:128, :], 0, None, False, p=0, mode="front")
    drain()
    for c in range(NT):
        if c + 1 < NT:
            mlstm(128, x1d[(c + 1) * 128:(c + 2) * 128, :], (c + 1) % 2, None, False, p=(c + 1) % 2, mode="front")
        mlstm(128, None, c % 2, yp[c * 128:(c + 1) * 128, :], False, p=c % 2, mode="rest", deftail=True)
    drain()
    for h in range(4):
        P.dma("gpsimd", cmem_p[h].rearrange("(c p) v -> p c v", p=128), Cst[:, h, :, 0:256], reads=["Cst"])
        P.dma("gpsimd", cnorm_p[h].rearrange("(c p) -> p c", p=128), Cst[:, h, :, 256], reads=["Cst"], allow_slow_non_contiguous=True)
    P.dma("gpsimd", cmax_p, mtok[127:128, :], reads=["mtok"])
    fence()
    stC.close()
    Hs[1] = Hs[0]; Hbs[1] = Hbs[0]; vaugs[1] = vaugs[0]; gts[1] = gts[0]; sz[1] = sz[0]; xin[1] = xin[0]
    Cs = [sb("Cs%d" % i, [128, 2, 256]) for i in range(2)]
    Cn = [sb("Cn%d" % i, [128, 2, 256]) for i in range(2)]
    qTf = sb("qTf", [128, 8, TS])
    vf = sb("vf", [TS, 4, 256])
    kwf = mixpre; kwn = sb("kwn", [TS, 1024])
    decb = sb("decb", [128, NS, 4]); decn = sb("decn", [NS, 4]); quart = sb("quart", [TS, 128]); Xd = sb("Xd", [TS, NS, 4])

    mlstm(TS, x1d[S:S + TS, :], 0, ys[:, :], True)

    if _LIMIT[0] is not None:
        print('total ops', len(P.ops))
        P.ops = P.ops[:_LIMIT[0]]
    P.emit(st)
    return nc, st, consts


def core_inputs(inp, core, NT, NS, consts, xp=None):
    TS = 4 * NS
    b = core % 4
    s0 = core * NS
    f = {}
    f["xp"] = np.ascontiguousarray(inp["x_prompt"][b] if xp is None else xp)
    f["xs"] = np.ascontiguousarray(inp["x_sample"][s0:s0 + NS].reshape(TS, D))
    for g, nm in enumerate(("cache_a_kv", "cache_b0_kv", "cache_b1_kv", "cache_b2_kv")):
        f["c%d" % g] = np.ascontiguousarray(inp[nm][0, s0:s0 + NS].reshape(NS, CACHE_L[g], 256))
    f["cmem"] = np.ascontiguousarray(inp["state_c_mem"][0, s0:s0 + NS])
    f["cnorm"] = np.ascontiguousarray(inp["state_c_norm"][0, s0:s0 + NS].reshape(NS, 1024))
    f["cmax"] = np.ascontiguousarray(inp["state_c_max"][0, s0:s0 + NS])
    f["w_in0"] = np.ascontiguousarray(inp["w_in0"][0])
    f["w_out0"] = np.ascontiguousarray(inp["w_out0"][0])
    f["w_in1"] = np.ascontiguousarray(inp["w_in1"][0])
    f["w_out1"] = np.ascontiguousarray(inp["w_out1"][0])
    f["sink_bc"] = np.ascontiguousarray(np.broadcast_to(inp["sinks0"][0][None, :], (128, 8)))
    f["bg_bc"] = np.ascontiguousarray(np.broadcast_to(inp["b_gates1"][0][None, :], (128, 8)))
    f["mh_bc"] = np.ascontiguousarray(np.broadcast_to(inp["mh_norm1"][0][None, :], (128, 1024)))
    f["lng_bc"] = np.ascontiguousarray(np.broadcast_to(inp["ln_g"][None, :, :], (128, 2, 1024)))
    f["lnb_bc"] = np.ascontiguousarray(np.broadcast_to(inp["ln_b"][None, :, :], (128, 2, 1024)))
    f["cmax_tok"] = np.repeat(f["cmax"], 4, axis=0)
    f["cnorm_tok"] = np.repeat(f["cnorm"], 4, axis=0)
    for k, v in consts.items():
        f["k_" + k] = np.ascontiguousarray(v)
    return {k: np.asarray(v, np.float32) for k, v in f.items()}


NT_FULL = 32
NS_FULL = 16
_CACHE = {}


def kernel(**inp):
    inp = {k: np.asarray(v) for k, v in inp.items()}
    if "nc" not in _CACHE:
        nc, st, consts = build(NT_FULL, NS_FULL)
        _CACHE["nc"] = (nc, consts)
    nc, consts = _CACHE["nc"]
    in_maps = [core_inputs(inp, c, NT_FULL, NS_FULL, consts) for c in range(8)]
    res = run_bass_kernel_spmd(nc, in_maps, core_ids=list(range(8)))
    R = res.results
    f32 = np.float32
    yp = np.stack([R[b]["yp"] for b in range(4)], 0).astype(f32)
    ys = np.concatenate([R[c]["ys"].reshape(NS_FULL, 4, D) for c in range(8)], 0).astype(f32)
    outs = [yp, ys]
    for g in range(4):
        L = CACHE_L[g]
        outs.append(np.stack([R[b]["kvp%d" % g].reshape(L, 2, 2, 64) for b in range(4)], 0)[None].astype(f32))
    outs.append(np.stack([R[b]["cmem_p"] for b in range(4)], 0)[None].astype(f32))
    outs.append(np.stack([R[b]["cnorm_p"] for b in range(4)], 0)[None].astype(f32))
    outs.append(np.stack([R[b]["cmax_p"].reshape(4) for b in range(4)], 0)[None].astype(f32))
    for g in range(4):
        L = CACHE_L[g]
        outs.append(np.concatenate([R[c]["kvs%d" % g].reshape(NS_FULL, L, 2, 2, 64) for c in range(8)], 0)[None].astype(f32))
    outs.append(np.concatenate([R[c]["cmem_s"] for c in range(8)], 0)[None].astype(f32))
    outs.append(np.concatenate([R[c]["cnorm_s"].reshape(NS_FULL, 4, 256) for c in range(8)], 0)[None].astype(f32))
    outs.append(np.concatenate([R[c]["cmax_s"].reshape(NS_FULL, 4, 4)[:, 3, :] for c in range(8)], 0)[None].astype(f32))
    return tuple(outs)
```

```python
import contextlib
import numpy as np
import concourse.bass as bass
import concourse.mybir as mybir
from concourse.bass_utils import run_bass_kernel_spmd

F32 = mybir.dt.float32
BF = mybir.dt.bfloat16
AF = mybir.ActivationFunctionType
ALU = mybir.AluOpType
AX = mybir.AxisListType

ENGS = ("tensor", "scalar", "vector", "gpsimd", "sync")


class Op:
    __slots__ = ("eng", "fn", "deps", "is_dma", "idx", "signal", "sem", "val", "prev_same_sem")

    def __init__(self, eng, fn, is_dma):
        self.eng = eng
        self.fn = fn
        self.is_dma = is_dma
        self.deps = set()
        self.signal = False
        self.sem = None
        self.val = 0
        self.prev_same_sem = None


class Res:
    __slots__ = ("w", "r_eng", "r_dma")

    def __init__(self):
        self.w = None
        self.r_eng = {}
        self.r_dma = []


class Prog:
    def __init__(self, nc, ndma_sems=12):
        self.nc = nc
        self.ops = []
        self.res = {}
        self.ndma = ndma_sems

    def _res(self, k):
        r = self.res.get(k)
        if r is None:
            r = self.res[k] = Res()
        return r

    def add(self, eng, fn, reads=(), writes=(), dma=False):
        op = Op(eng, fn, dma)
        op.idx = len(self.ops)
        for k in reads:
            r = self._res(k)
            if r.w is not None:
                op.deps.add(r.w)
        for k in writes:
            r = self._res(k)
            if r.w is not None:
                op.deps.add(r.w)
            for o in r.r_eng.values():
                op.deps.add(o)
            for o in r.r_dma:
                op.deps.add(o)
        for k in reads:
            r = self._res(k)
            if dma:
                r.r_dma.append(op)
            else:
                r.r_eng[eng] = op
        for k in writes:
            r = self._res(k)
            r.w = op
            r.r_eng = {}
            r.r_dma = []
        op.deps.discard(op)
        self.ops.append(op)
        return op

    def fence(self, fns):
        prev = list(self.ops)
        last = {}
        dmas = []
        for o in prev:
            if o.is_dma:
                dmas.append(o)
            else:
                last[o.eng] = o
        for eng, fn in fns.items():
            op = Op(eng, fn, eng == "sync")
            op.idx = len(self.ops)
            op.deps = set(last.values()) | set(dmas)
            self.ops.append(op)

    def dma(self, q, out, in_, reads=(), writes=(), **kw):
        return self.add(q, lambda e: e.dma_start(out=out, in_=in_, **kw), reads, writes, dma=True)

    def emit(self, stack):
        nc = self.nc
        ops = self.ops
        for op in ops:
            for d in op.deps:
                if d.is_dma or d.eng != "tensor" or op.eng != "tensor" or op.is_dma:
                    d.signal = True
        for op in ops:
            if op.is_dma:
                op.signal = True
        esem = {e: stack.enter_context(nc.semaphore("s_" + e)) for e in ENGS}
        dsem = {e: [stack.enter_context(nc.semaphore("d_%s_%d" % (e, i))) for i in range(self.ndma)]
                for e in ENGS if any(o.is_dma and o.eng == e for o in ops)}
        ecount = {e: 0 for e in ENGS}
        dcount = {e: [0] * self.ndma for e in dsem}
        dlast = {e: [None] * self.ndma for e in dsem}
        drr = {e: 0 for e in dsem}
        for op in ops:
            if op.is_dma:
                i = drr[op.eng]
                drr[op.eng] = (i + 1) % self.ndma
                op.sem = dsem[op.eng][i]
                dcount[op.eng][i] += 16
                op.val = dcount[op.eng][i]
                op.prev_same_sem = dlast[op.eng][i]
                dlast[op.eng][i] = op
            elif op.signal:
                ecount[op.eng] += 1
                op.sem = esem[op.eng]
                op.val = ecount[op.eng]
        per_eng = {e: [o for o in ops if o.eng == e] for e in ENGS}
        all_dmas = [o for o in ops if o.is_dma]
        block = stack.enter_context(nc.Block())

        def make(ename):
            def body(e):
                waited = {}
                for op in per_eng[ename]:
                    need = {}
                    deps = set(op.deps)
                    if op.is_dma and op.prev_same_sem is not None:
                        deps.add(op.prev_same_sem)
                    for d in deps:
                        if (not d.is_dma) and d.eng == "tensor" and ename == "tensor" and not op.is_dma:
                            continue
                        key = id(d.sem)
                        if waited.get(key, 0) >= d.val:
                            continue
                        if key not in need or need[key][1] < d.val:
                            need[key] = (d.sem, d.val)
                    for key, (s, v) in need.items():
                        e.wait_ge(s, v)
                        waited[key] = v
                    ins = op.fn(e)
                    if op.signal:
                        ins.then_inc(op.sem, 16 if op.is_dma else 1)
                if ename == "sync":
                    fin = {}
                    for o in all_dmas:
                        k = id(o.sem)
                        if k not in fin or fin[k][1] < o.val:
                            fin[k] = (o.sem, o.val)
                    for en in ENGS:
                        if ecount[en] > 0:
                            fin[id(esem[en])] = (esem[en], ecount[en])
                    for k, (s, v) in fin.items():
                        if waited.get(k, 0) < v:
                            e.wait_ge(s, v)
            return body

        for ename in ENGS:
            if per_eng[ename] or ename == "sync":
                getattr(block, ename)(make(ename))


D = 1024
IN0 = 4096
IN1 = 5128
ALPHA = 4.0 ** 0.25
LN_EPS = 1e-5
MH_EPS = 1e-6
PAST = 16384
THETA = 500000.0
NEG = -1e30
GROUPS = ("A", "b0", "b1", "b2")
DIL = (1, 1, 4, 16)
RING = (3, 3, 6, 18)
QBASE = (0, 768, 1280, 1792)
KBASE = (512, 2304, 2432, 2560)
VBASE = (640, 2688, 2816, 2944)
CACHE_L = (128, 128, 512, 2048)
MIDX = {1: (0, None, 1), 4: (2, 3, 4), 16: (5, 6, 7)}


_LIMIT = [None]


def host_consts(NT, NS):
    TS = 4 * NS
    c = {}
    k = np.arange(128)[:, None]
    q = np.arange(128)[None, :]
    m = np.zeros((128, 8, 128), np.float32)
    m[:, 0] = (k <= q)
    m[:, 1] = (k >= q)
    for d, (a, b, l) in ((4, MIDX[4]), (16, MIDX[16])):
        cong = ((q - k) % d == 0)
        m[:, a] = cong & (k <= q)
        m[:, b] = cong
        m[:, l] = cong & (k >= q)
    c["maskt"] = m
    half = 8
    inv = THETA ** (-np.arange(half, dtype=np.float32) / half)
    pos = (np.arange(NT)[None, :] * 128 + np.arange(128)[:, None]).astype(np.float32)
    ang = pos[:, :, None] * inv[None, None, :]
    c["cs"] = np.concatenate([np.cos(ang), np.sin(ang)], -1).astype(np.float32)
    tok = np.arange(TS)
    pos_s = (PAST + tok % 4).astype(np.float32)
    ang = pos_s[:, None] * inv[None, :]
    c["cs_s"] = np.concatenate([np.cos(ang), np.sin(ang)], -1).astype(np.float32)[:, None, :]
    c["ident"] = np.eye(128, dtype=np.float32)
    same = (tok[:, None] // 4 == tok[None, :] // 4)
    mn = np.zeros((TS, 2, TS), np.float32)
    mn[:, 0] = same & (tok[:, None] <= tok[None, :])
    mn[:, 1] = (tok[:, None] == tok[None, :])
    c["masknew"] = mn
    mc = np.zeros((128, 10, 4, 4), np.float32)
    rows = np.arange(128)[:, None, None]
    ii = np.arange(4)[None, None, :]
    mc[:, 0] = (rows >= ii)
    mc[:, 1] = (rows >= ii)
    for r in range(4):
        mc[:, 2 + r] = (ii == r)
        mc[:, 6 + r] = (ii == r)
    c["maskc"] = mc.reshape(128, 10, 16)
    t = np.arange(128)
    c["U128"] = (t[:, None] <= t[None, :]).astype(np.float32)
    c["cmask128"] = np.where(t[None, :] <= t[:, None], 0.0, NEG).astype(np.float32)
    c["sellast128"] = np.tile((t[:, None] == 127).astype(np.float32), (1, 128))
    ts_ = np.arange(TS)
    sames = (ts_[:, None] // 4 == ts_[None, :] // 4)
    c["U_s"] = (sames & (ts_[:, None] <= ts_[None, :])).astype(np.float32)
    c["cmask_s"] = np.where(sames & (ts_[None, :] <= ts_[:, None]), 0.0, NEG).astype(np.float32)
    c["selend_s"] = (sames & (ts_[:, None] % 4 == 3)).astype(np.float32)
    seln = np.zeros((TS, NS, 128), np.float32)
    for n in range(NS):
        seln[4 * n, n, :] = 1.0
    oh = np.zeros((TS, NS), np.float32)
    oh[ts_, ts_ // 4] = 1.0
    c["onehot_s"] = oh
    c["rowmask_s"] = np.tile(oh.T[None, :, :], (128, 1, 1)).astype(np.float32)
    c["G_s"] = oh.copy()
    e4 = np.zeros((4, 4, 128), np.float32)
    for h in range(4):
        e4[h, h, :] = 1.0
    c["E4"] = e4
    return c


def build(NT, NS):
    TS = 4 * NS
    nc = bass.Bass("TRN2", target_bir_lowering=False)
    consts = host_consts(NT, NS)
    dr = {}

    def din(name, shape, dt=F32):
        dr[name] = nc.dram_tensor(name, list(shape), dt, kind="ExternalInput").ap()
        return dr[name]

    def dout(name, shape):
        dr[name] = nc.dram_tensor(name, list(shape), F32, kind="ExternalOutput").ap()
        return dr[name]

    S = NT * 128
    xp = din("xp", [S, D])
    xs = din("xs", [TS, D])
    caches = [din("c%d" % g, [NS, CACHE_L[g], 256]) for g in range(4)]
    cmem = din("cmem", [NS, 4, 256, 256])
    cnorm = din("cnorm", [NS, 1024])
    cmax = din("cmax", [NS, 4])
    w_in0 = din("w_in0", [D, IN0])
    w_out0 = din("w_out0", [D, D])
    w_in1 = din("w_in1", [D, IN1])
    w_out1 = din("w_out1", [D, D])
    sink_bc = din("sink_bc", [128, 8])
    bg_bc = din("bg_bc", [128, 8])
    mh_bc = din("mh_bc", [128, 1024])
    lng_bc = din("lng_bc", [128, 2, 1024])
    lnb_bc = din("lnb_bc", [128, 2, 1024])
    cd = {k: din("k_" + k, v.shape) for k, v in consts.items()}

    yp = dout("yp", [S, D])
    ys = dout("ys", [TS, D])
    kvp = [dout("kvp%d" % g, [min(CACHE_L[g], S), 256]) for g in range(4)]
    cmem_p = dout("cmem_p", [4, 256, 256])
    cnorm_p = dout("cnorm_p", [4, 256])
    cmax_p = dout("cmax_p", [1, 4])
    kvs = [dout("kvs%d" % g, [NS, CACHE_L[g], 256]) for g in range(4)]
    cmem_s = dout("cmem_s", [NS, 4, 256, 256])
    cnorm_s = dout("cnorm_s", [NS, 1024])
    cmax_s = dout("cmax_s", [TS, 4])
    x1d = dout("x1d", [S + TS, D])

    st = contextlib.ExitStack()
    P = Prog(nc)

    cur = {"st": st}

    def sb(name, shape, dt=F32):
        return cur["st"].enter_context(nc.sbuf_tensor(name, list(shape), dt))

    fsc = st.enter_context(nc.sbuf_tensor("fsc", [128, 8], F32))

    P.add("gpsimd", lambda e: e.memset(fsc[:], 0.0), writes=["fsc"])

    def fence():
        P.fence({"scalar": lambda e: e.activation(out=fsc[0:1, 0:1], in_=fsc[0:1, 1:2], func=AF.Copy),
                 "vector": lambda e: e.memset(fsc[0:1, 2:3], 0.0),
                 "gpsimd": lambda e: e.memset(fsc[0:1, 4:5], 0.0),
                 "sync": lambda e: e.dma_start(out=fsc[0:1, 6:7], in_=fsc[0:1, 7:8])})

    ps = st.enter_context(nc.psum_tensor("ps", [128, 4096], F32))
    psb = ps.bitcast(BF)

    def PB(i, n=512, off=0, T=128):
        return ps[0:T, i * 512 + off: i * 512 + off + n]

    def PBb(i, n=1024, off=0, T=128):
        return psb[0:T, i * 1024 + off: i * 1024 + off + n]

    Warena = sb("Warena", [128, 8, IN1], BF)
    Woarena = sb("Woarena", [128, 8, D], BF)
    ident_f = sb("ident_f", [128, 128])
    ident_b = sb("ident_b", [128, 128], BF)
    lng = sb("lng", [128, 1024])
    lnb = sb("lnb", [128, 1024])
    sinkexp = sb("sinkexp", [128, 8])
    P.dma("sync", ident_f[:], cd["ident"], writes=["ident_f"])
    P.dma("gpsimd", ident_b[:], cd["ident"], writes=["ident_b"])
    P.dma("sync", lng[:], lng_bc[:, 0, :], writes=["lng"])
    P.dma("sync", lnb[:], lnb_bc[:, 0, :], writes=["lnb"])
    P.dma("sync", sinkexp[:], sink_bc, writes=["sinkexp"])
    P.add("scalar", lambda e: e.activation(out=sinkexp[:], in_=sinkexp[:], func=AF.Exp), reads=["sinkexp"], writes=["sinkexp"])

    def load_W(w_in, ncols):
        wv = w_in.rearrange("(c p) n -> p c n", p=128)
        ncg = (ncols + 511) // 512
        for cg in range(ncg):
            n = min(512, ncols - cg * 512)
            P.dma("gpsimd", Warena[:, :, cg * 512:cg * 512 + n], wv[:, :, cg * 512:cg * 512 + n], reads=[], writes=[("W", cg)])

    def load_Wo(w_out):
        wo = w_out.rearrange("(c p) n -> p c n", p=128)
        for c in range(8):
            P.dma("gpsimd", Woarena[:, c, :], wo[:, c, :], reads=[], writes=["Wo"])

    xin = [sb("xin0", [128, D])] * 2
    xin = list(xin)
    xb = sb("xb", [128, D], BF)
    xT = sb("xT", [128, 8, 128], BF)
    H = sb("H", [128, 3072])
    Hb = sb("Hb", [128, 3072], BF)
    sz0 = sb("sz0", [128, D])
    sz = [sz0, sz0]
    qT = sb("qT", [128, 16, 128], BF)
    Eb = [sb("Eb%d" % i, [128, 512], BF) for i in range(2)]
    Pm = [sb("Pm%d" % i, [128, 512], BF) for i in range(2)]
    dn = sb("dn", [128, 8])
    rd = sb("rd", [128, 8])
    mixpre = sb("mixpre", [128, D])
    mixb = sb("mixb", [128, D], BF)
    mixT = sb("mixT", [128, 8, 128], BF)
    rbuf = sb("rbuf", [128, D])
    stt = sb("stt", [128, 2, 6])
    mv = sb("mv", [128, 2])
    rstd = sb("rstd", [128, 1])
    stA = contextlib.ExitStack()
    cur["st"] = stA
    maskt = sb("maskt", [128, 8, 128], BF)
    cs = sb("cs", [128, NT, 16])
    kT = [sb("kT%d" % g, [128, RING[g], 128], BF) for g in range(4)]
    Vr = [sb("Vr%d" % g, [128, RING[g], 2, 65], BF) for g in range(4)]
    rtmp = sb("rtmpA", [128, 4, 30, 8])
    qTd = [qT, sb("qTA2", [128, 16, 128], BF)]
    xin[1] = sb("xinA1", [128, D])
    sz[1] = sb("szA1", [128, D])
    for _i in (2, 3, 4):
        Eb.append(sb("EbA%d" % _i, [128, 512], BF))
        Pm.append(sb("PmA%d" % _i, [128, 512], BF))
    P.dma("gpsimd", maskt[:], cd["maskt"], writes=["maskt"])
    P.dma("sync", cs[:], cd["cs"], writes=["cs"])
    for g in range(4):
        P.add("gpsimd", lambda e, g=g: e.memset(Vr[g][:].rearrange("p a b c -> p (a b c)"), 1.0), writes=[("V", g, s) for s in range(RING[g])])

    cnt = {"s": 0, "m": 0}

    def bc_mid(t, off, pstep, T, n_mid, n_in):
        return bass.AP(t[:].tensor, off, [[pstep, T], [0, n_mid], [1, n_in]])

    def bc_last(t, off, pstep, T, n_mid, mid_step, n_in):
        return bass.AP(t[:].tensor, off, [[pstep, T], [mid_step, n_mid], [0, n_in]])

    def layer_norm(T, layer, out_ap):
        for i in range(2):
            P.add("vector", lambda e, i=i: e.bn_stats(out=stt[0:T, i, :], in_=rbuf[0:T, i * 512:(i + 1) * 512]), reads=["rbuf"], writes=["stt"])
        P.add("vector", lambda e: e.bn_aggr(out=mv[0:T, :], in_=stt[0:T].rearrange("p a b -> p (a b)")), reads=["stt"], writes=["mv"])
        P.add("vector", lambda e: e.tensor_scalar(out=rstd[0:T, :], in0=mv[0:T, 1:2], scalar1=LN_EPS, scalar2=None, op0=ALU.add), reads=["mv"], writes=["rstd"])
        P.add("scalar", lambda e: e.activation(out=rstd[0:T, :], in_=rstd[0:T, :], func=AF.Sqrt), reads=["rstd"], writes=["rstd"])
        P.add("vector", lambda e: e.reciprocal(out=rstd[0:T, :], in_=rstd[0:T, :]), reads=["rstd"], writes=["rstd"])
        P.add("vector", lambda e: e.tensor_scalar(out=rbuf[0:T, :], in0=rbuf[0:T, :], scalar1=mv[0:T, 0:1], scalar2=rstd[0:T, 0:1], op0=ALU.subtract, op1=ALU.mult), reads=["rbuf", "mv", "rstd"], writes=["rbuf"])
        P.add("vector", lambda e: e.tensor_tensor(out=rbuf[0:T, :], in0=rbuf[0:T, :], in1=lng[0:T, :], op=ALU.mult), reads=["rbuf", "lng"], writes=["rbuf"])
        return P.add("vector", lambda e: e.tensor_tensor(out=out_ap, in0=rbuf[0:T, :], in1=lnb[0:T, :], op=ALU.add), reads=["rbuf", "lnb"], writes=["rbuf"])

    def transpose_in(T, src_key, src_ap_fn, n, dst_ap, dst_key, bank=2, W=128):
        for c in range(n):
            P.add("tensor", lambda e, c=c: e.transpose(out=psb[0:W, bank * 1024 + c * T: bank * 1024 + (c + 1) * T], in_=src_ap_fn(c), identity=ident_b[0:T, 0:T]),
                  reads=[src_key, "ident_b"], writes=[("pb", bank)])
        eng = "vector" if cnt["m"] % 2 == 0 else "scalar"
        cnt["m"] += 1
        src = psb[0:W, bank * 1024: bank * 1024 + n * T]
        if eng == "vector":
            P.add("vector", lambda e: e.tensor_copy(out=dst_ap, in_=src), reads=[("pb", bank)], writes=[dst_key])
        else:
            P.add("scalar", lambda e: e.activation(out=dst_ap, in_=src, func=AF.Copy), reads=[("pb", bank)], writes=[dst_key])

    bg = []

    def drain(k=None):
        n = len(bg) if k is None else min(k, len(bg))
        for _ in range(n):
            bg.pop(0)()

    def load_and_project(T, x_ap, s2, W_cols, evac, defer=False, atomic=False, cast_eng="gpsimd"):
        th = []
        xi = xin[s2]
        th.append(lambda: P.dma("sync", xi[0:T, :], x_ap, writes=[("xin", s2)]))
        if cast_eng == "scalar":
            th.append(lambda: P.add("scalar", lambda e: e.activation(out=xb[0:T, :], in_=xi[0:T, :], func=AF.Copy), reads=[("xin", s2)], writes=["xb"]))
        else:
            th.append(lambda: P.add("gpsimd", lambda e: e.tensor_copy(out=xb[0:T, :], in_=xi[0:T, :]), reads=[("xin", s2)], writes=["xb"]))
        th.append(lambda: transpose_in(T, "xb", lambda c: xb[0:T, c * 128:(c + 1) * 128], 8, xT[:, :, 0:T], "xT"))
        ncg = (W_cols + 511) // 512
        for cg in range(ncg):
            n = min(512, W_cols - cg * 512)
            bank = cg % 2
            for c0 in range(0, 8, 4):
                def mm4(cg=cg, n=n, bank=bank, c0=c0):
                    for c in range(c0, c0 + 4):
                        P.add("tensor", lambda e, c=c: e.matmul(PB(bank, n, 0, T), lhsT=xT[:, c, 0:T], rhs=Warena[:, c, cg * 512: cg * 512 + n], start=(c == 0), stop=(c == 7)),
                              reads=["xT", ("W", cg)], writes=[("pb", bank)])
                th.append(mm4)
            th.append(lambda cg=cg, bank=bank, n=n: evac(cg, bank, n))
            if atomic:
                grp = th[-3:]
                del th[-3:]
                th.append(lambda grp=grp: [f() for f in grp])
        if defer:
            bg.extend(th)
        else:
            for f in th:
                f()

    def rope(T, base, nh, cs_ap_c, cs_ap_s):
        Hr = H[0:T, base: base + nh * 64].rearrange("p (h d) -> p h d", d=64)
        x1 = Hr[:, :, 0:8]
        x2 = Hr[:, :, 8:16]
        tm = [rtmp[0:T, i, 0:nh, :] for i in range(4)]
        P.add("vector", lambda e: e.tensor_tensor(out=tm[0], in0=x1, in1=cs_ap_c, op=ALU.mult), reads=["H", "cs"], writes=["rtmp"])
        P.add("vector", lambda e: e.tensor_tensor(out=tm[1], in0=x2, in1=cs_ap_s, op=ALU.mult), reads=["H", "cs"], writes=["rtmp"])
        P.add("vector", lambda e: e.tensor_tensor(out=tm[2], in0=x2, in1=cs_ap_c, op=ALU.mult), reads=["H", "cs"], writes=["rtmp2"])
        P.add("vector", lambda e: e.tensor_tensor(out=tm[3], in0=x1, in1=cs_ap_s, op=ALU.mult), reads=["H", "cs"], writes=["rtmp2"])
        P.add("vector", lambda e: e.tensor_tensor(out=x1, in0=tm[0], in1=tm[1], op=ALU.subtract), reads=["rtmp", "rtmp2"], writes=["H"])
        P.add("vector", lambda e: e.tensor_tensor(out=x2, in0=tm[2], in1=tm[3], op=ALU.add), reads=["rtmp", "rtmp2"], writes=["H"])

    def l0_front_a(T, x_ap, s2, defer=False, cast_eng="gpsimd"):
        szb = sz[s2]

        def evac(cg, bank, n):
            if cg < 6:
                if cg % 2 == 0:
                    P.add("scalar", lambda e: e.activation(out=H[0:T, cg * 512:(cg + 1) * 512], in_=PB(bank, 512, 0, T), func=AF.Copy), reads=[("pb", bank)], writes=["H"])
                else:
                    P.add("vector", lambda e: e.tensor_copy(out=H[0:T, cg * 512:(cg + 1) * 512], in_=PB(bank, 512, 0, T)), reads=[("pb", bank)], writes=["H"])
            else:
                P.add("scalar", lambda e: e.activation(out=szb[0:T, (cg - 6) * 512:(cg - 5) * 512], in_=PB(bank, 512, 0, T), func=AF.Silu), reads=[("pb", bank)], writes=[("sz", s2)])
        load_and_project(T, x_ap, s2, IN0, evac, defer, cast_eng=cast_eng)

    def l0_casts(T, pool_ok=True):
        for G in range(4):
            qo = Hb[0:T, QBASE[G]:QBASE[G] + 512].rearrange("p (j hk d) -> p hk j d", hk=2, d=64)
            qi = H[0:T, QBASE[G]:QBASE[G] + 512].rearrange("p (hk j d) -> p hk j d", hk=2, d=64)
            if G % 2 == 0 and pool_ok:
                P.add("gpsimd", lambda e, qo=qo, qi=qi: e.tensor_copy(out=qo, in_=qi), reads=["H"], writes=["Hb"])
            else:
                P.add("scalar", lambda e, qo=qo, qi=qi: e.activation(out=qo, in_=qi, func=AF.Copy), reads=["H"], writes=["Hb"])
        P.add("scalar", lambda e: e.activation(out=Hb[0:T, 512:640], in_=H[0:T, 512:640], func=AF.Copy), reads=["H"], writes=["Hb"])
        P.add("vector", lambda e: e.tensor_copy(out=Hb[0:T, 2304:2688], in_=H[0:T, 2304:2688]), reads=["H"], writes=["Hb"])

    def l0_front_b(T, csc, css):
        rope(T, 0, 10, csc(10), css(10))
        rope(T, 768, 30, csc(30), css(30))
        l0_casts(T, pool_ok=False)

    def l0_front(T, x_ap, s2, csc, css):
        l0_front_a(T, x_ap, s2, cast_eng="scalar")
        l0_front_b(T, csc, css)

    def q_src(T, G, j):
        b = QBASE[G] + 128 * j
        return Hb[0:T, b:b + 128]

    def l0_back(T, s2, row0, out_dram, acc_phase, defer_tail=False):
        for ph in range(2):
            acc_phase(ph)
            den_ap = bass.AP(ps[:].tensor, 5 * 512 + 64, [[4096, T], [128, 8], [1, 1]])
            if ph == 0:
                P.add("vector", lambda e: e.tensor_tensor(out=dn[0:T, :], in0=den_ap, in1=sinkexp[0:T, :], op=ALU.add), reads=[("pb", 5), ("pb", 6), "sinkexp"], writes=["dn"])
            else:
                P.add("vector", lambda e: e.tensor_copy(out=dn[0:T, :], in_=den_ap), reads=[("pb", 5), ("pb", 6)], writes=["dn"])
            P.add("vector", lambda e: e.reciprocal(out=rd[0:T, :], in_=dn[0:T, :]), reads=["dn"], writes=["rd"])
            num_ap = bass.AP(ps[:].tensor, 5 * 512, [[4096, T], [128, 8], [1, 64]])
            rd_bc = bc_last(rd, 0, 8, T, 8, 1, 64)
            mp = mixpre[0:T, ph * 512:(ph + 1) * 512].rearrange("p (h d) -> p h d", d=64)
            P.add("vector", lambda e, mp=mp: e.tensor_tensor(out=mp, in0=num_ap, in1=rd_bc, op=ALU.mult), reads=[("pb", 5), ("pb", 6), "rd"], writes=["mixpre"])
        szb = sz[s2]
        P.add("gpsimd" if T == 128 else "vector", lambda e: e.tensor_tensor(out=mixb[0:T, :], in0=mixpre[0:T, :], in1=szb[0:T, :], op=ALU.mult), reads=["mixpre", ("sz", s2)], writes=["mixb"])
        tail(T, s2, 0, out_dram, defer_tail)

    def tail(T, s2, layer, out_dram, defer=False, atomic=False):
        xi = xin[s2]
        th = []
        th.append(lambda: transpose_in(T, "mixb", lambda c: mixb[0:T, c * 128:(c + 1) * 128], 8, mixT[:, :, 0:T], "mixT"))
        for n in range(2):
            for c0 in (0, 4):
                def mm4(n=n, c0=c0):
                    for c in range(c0, c0 + 4):
                        P.add("tensor", lambda e, c=c: e.matmul(PB(n, 512, 0, T), lhsT=mixT[:, c, 0:T], rhs=Woarena[:, c, n * 512:(n + 1) * 512], start=(c == 0), stop=(c == 7)),
                              reads=["mixT", "Wo"], writes=[("pb", n)])
                th.append(mm4)
            th.append(lambda n=n: P.add("vector", lambda e: e.scalar_tensor_tensor(out=rbuf[0:T, n * 512:(n + 1) * 512], in0=xi[0:T, n * 512:(n + 1) * 512], scalar=ALPHA, in1=PB(n, 512, 0, T), op0=ALU.mult, op1=ALU.add),
                                        reads=[("xin", s2), ("pb", n)], writes=["rbuf"]))
            if atomic:
                grp = th[-3:]
                del th[-3:]
                th.append(lambda grp=grp: [f() for f in grp])
        th.append(lambda: layer_norm(T, layer, rbuf[0:T, :]))
        th.append(lambda: P.dma("gpsimd", out_dram, rbuf[0:T, :], reads=["rbuf"], writes=["x1d"] if layer == 0 else []))
        if defer:
            bg.extend(th)
        else:
            for f in th:
                f()

    load_W(w_in0, IN0)
    load_Wo(w_out0)

    SB = (3, 4, 7)

    def front_b_prompt(t, defer=False):
        th = []
        csc = lambda nh, t=t: bc_mid(cs, t * 16, NT * 16, 128, nh, 8)
        css = lambda nh, t=t: bc_mid(cs, t * 16 + 8, NT * 16, 128, nh, 8)
        qTt = qTd[t % 2]
        qk = ("qT", t % 2)
        th.append(lambda: rope(128, 0, 10, csc(10), css(10)))
        th.append(lambda: rope(128, 768, 30, csc(30), css(30)))
        th.append(lambda: l0_casts(128, pool_ok=(t > 0)))

        def kvpart(g):
            slot = t % RING[g]
            P.add("scalar", lambda e: e.activation(out=Vr[g][:, slot, :, 0:64], in_=H[:, VBASE[g]: VBASE[g] + 128].rearrange("p (a b) -> p a b", b=64), func=AF.Copy),
                  reads=["H"], writes=[("V", g, slot)])
            L = min(CACHE_L[g], S)
            r0 = t * 128 - (S - L)
            if r0 >= 0:
                kv_out = kvp[g][r0:r0 + 128, :]
                P.dma("gpsimd", kv_out[:, 0:128], H[:, KBASE[g]:KBASE[g] + 128], reads=["H"])
                P.dma("gpsimd", kv_out[:, 128:256], H[:, VBASE[g]:VBASE[g] + 128], reads=["H"])
        for g in range(4):
            th.append(lambda g=g: kvpart(g))
        for b0 in range(0, 16, 8):
            th.append(lambda b0=b0: transpose_in(128, "Hb", lambda c: q_src(128, (b0 + c) // 4, (b0 + c) % 4), 8, qTt[:, b0:b0 + 8, :], qk))
        for g in range(4):
            th.append(lambda g=g: transpose_in(128, "Hb", lambda c: Hb[:, KBASE[g]:KBASE[g] + 128], 1, kT[g][:, t % RING[g], :], ("kT", g, t % RING[g])))
        if defer:
            bg.extend(th)
        else:
            for f in th:
                f()

    l0_front_a(128, xp[0:128, :], 0, cast_eng="scalar")
    front_b_prompt(0)
    for t in range(NT):
        s2 = t % 2
        if t + 1 < NT:
            l0_front_a(128, xp[(t + 1) * 128:(t + 2) * 128, :], (t + 1) % 2, defer=True)
            front_b_prompt(t + 1, defer=True)
        qTt = qTd[t % 2]
        qk = ("qT", t % 2)

        def acc_phase(ph, t=t, qTt=qTt, qk=qk):
            groups = (0,) if ph == 0 else (1, 2, 3)
            plan = []
            for G in groups:
                d = DIL[G]
                for hk in range(2):
                    for o in range(d + 1):
                        if t - o < 0:
                            continue
                        mi = MIDX[d][0] if o == 0 else (MIDX[d][2] if o == d else MIDX[d][1])
                        plan.append((G, hk, o, mi))
            first = {}
            for i, (G, hk, o, mi) in enumerate(plan):
                first.setdefault(hk, i)
            bufs = {}

            def issue_S(i):
                G, hk, o, mi = plan[i]
                k3 = cnt["s"] % 5
                sbk = SB[cnt["s"] % 3]
                cnt["s"] += 1
                bufs[i] = k3
                Ek = Eb[k3]
                Pk = Pm[k3]
                slot = (t - o) % RING[G]
                P.add("tensor", lambda e: e.matmul(PB(sbk), lhsT=kT[G][64 * hk:64 * hk + 64, slot, :], rhs=qTt[64 * hk:64 * hk + 64, 4 * G:4 * G + 4, :], start=True, stop=True),
                      reads=[("kT", G, slot), qk], writes=[("pb", sbk)])
                P.add("scalar", lambda e: e.activation(out=Ek[:], in_=PB(sbk), func=AF.Exp, scale=0.125), reads=[("pb", sbk)], writes=[("E", k3)])
                meng = "vector"
                mk = bc_mid(maskt, mi * 128, 8 * 128, 128, 4, 128)
                P.add(meng, lambda e: e.tensor_tensor(out=Pk[:], in0=Ek[:], in1=mk, op=ALU.mult), reads=[("E", k3), "maskt"], writes=[("P", k3)])

            def issue_PV(i):
                G, hk, o, mi = plan[i]
                k3 = bufs[i]
                Pk = Pm[k3]
                slot = (t - o) % RING[G]
                for j in range(4):
                    P.add("tensor", lambda e, j=j: e.matmul(PB(5 + hk, 65, j * 128), lhsT=Pk[:, j * 128:(j + 1) * 128], rhs=Vr[G][:, slot, hk, :], start=(i == first[hk] and j == 0), stop=False, skip_group_check=True),
                          reads=[("P", k3), ("V", G, slot)], writes=[("pb", 5 + hk)])

            LA = 3
            for i in range(min(LA, len(plan))):
                issue_S(i)
            for i in range(len(plan)):
                issue_PV(i)
                if i + LA < len(plan):
                    issue_S(i + LA)
                drain(2)
            if ph == 1:
                drain()

        l0_back(128, s2, t * 128, x1d[t * 128:(t + 1) * 128, :], acc_phase, defer_tail=True)
    drain()
    xin[1] = xin[0]
    sz[1] = sz[0]
    del Eb[2:]
    del Pm[2:]

    fence()
    stA.close()
    stB = contextlib.ExitStack()
    cur["st"] = stB
    rtmp = sb("rtmpB", [TS, 4, 30, 8])
    cs_s = sb("cs_s", [TS, 1, 16])
    P.dma("sync", cs_s[:], cd["cs_s"], writes=["cs_s", "cs"])
    masknew = sb("masknew", [TS, 2, TS], BF)
    maskc = sb("maskc", [128, 10, 16], BF)
    qTs = sb("qTs", [128, 16, TS], BF)
    kTs = sb("kTs", [128, 4, TS], BF)
    Vs = sb("Vs", [TS, 8, 65], BF)
    ctile = [sb("ctile%d" % i, [128, 10, 256]) for i in range(2)]
    kcb = sb("kcb", [128, 10, 128], BF)
    vca = sb("vca", [128, 10, 2, 65], BF)
    kcT = sb("kcT", [128, 10, 128], BF)
    Pc = sb("Pc", [128, 320], BF)
    oT = sb("oT", [65, 4, 4 * TS])
    P.dma("gpsimd", masknew[:], cd["masknew"], writes=["masknew"])
    P.dma("gpsimd", maskc[:], cd["maskc"], writes=["maskc"])
    P.add("gpsimd", lambda e: e.memset(Vs[:].rearrange("p a b -> p (a b)"), 1.0), writes=["Vs"])
    P.add("gpsimd", lambda e: e.memset(vca[:].rearrange("p a b c -> p (a b c)"), 1.0), writes=[("vca", 0)])

    csc_s = lambda nh: bc_mid(cs_s, 0, 16, TS, nh, 8)
    css_s = lambda nh: bc_mid(cs_s, 8, 16, TS, nh, 8)
    l0_front(TS, xs[:, :], 0, csc_s, css_s)
    load_W(w_in1, IN1)
    for g in range(4):
        P.add("scalar", lambda e, g=g: e.activation(out=Vs[:, 2 * g:2 * g + 2, 0:64], in_=H[0:TS, VBASE[g]: VBASE[g] + 128].rearrange("p (a b) -> p a b", b=64), func=AF.Copy),
              reads=["H"], writes=["Vs"])
        Lg = CACHE_L[g]
        dst = bass.AP(kvs[g].tensor, (Lg - 4) * 256, [[Lg * 256, NS], [256, 4], [1, 128]])
        P.dma("gpsimd", dst, H[0:TS, KBASE[g]:KBASE[g] + 128], reads=["H"])
        dst2 = bass.AP(kvs[g].tensor, (Lg - 4) * 256 + 128, [[Lg * 256, NS], [256, 4], [1, 128]])
        P.dma("gpsimd", dst2, H[0:TS, VBASE[g]:VBASE[g] + 128], reads=["H"])
        for n in range(NS):
            P.dma("sync", kvs[g][n, 0:Lg - 4, :], caches[g][n, 4:Lg, :])
    for b0 in range(0, 16, 8):
        transpose_in(TS, "Hb", lambda c, b0=b0: q_src(TS, (b0 + c) // 4, (b0 + c) % 4), 8, qTs[:, b0:b0 + 8, :], "qTs")
    transpose_in(TS, "Hb", lambda c: Hb[0:TS, KBASE[c]:KBASE[c] + 128], 4, kTs[:, :, :], "kTs")

    def accT(ph, hk, n=4 * TS, off=0):
        return ps[0:65, (5 + ph) * 512 + hk * 256 + off: (5 + ph) * 512 + hk * 256 + off + n]

    firstb = {0: True, 1: True}
    for G in range(4):
        ph = 0 if G == 0 else 1
        for hk in range(2):
            sbk = 3 + cnt["s"] % 2
            ei = cnt["s"] % 2
            cnt["s"] += 1
            P.add("tensor", lambda e, G=G, hk=hk, sbk=sbk: e.matmul(PB(sbk, 4 * TS, 0, TS), lhsT=kTs[64 * hk:64 * hk + 64, G, :], rhs=qTs[64 * hk:64 * hk + 64, 4 * G:4 * G + 4, :], start=True, stop=True),
                  reads=["kTs", "qTs"], writes=[("pb", sbk)])
            P.add("scalar", lambda e, sbk=sbk, ei=ei: e.activation(out=Eb[ei][0:TS, 0:4 * TS], in_=PB(sbk, 4 * TS, 0, TS), func=AF.Exp, scale=0.125), reads=[("pb", sbk)], writes=[("E", ei)])
            mk = bc_mid(masknew, (0 if DIL[G] == 1 else 1) * TS, 2 * TS, TS, 4, TS)
            P.add("vector", lambda e, ei=ei, mk=mk: e.tensor_tensor(out=Pm[ei][0:TS, 0:4 * TS], in0=Eb[ei][0:TS, 0:4 * TS], in1=mk, op=ALU.mult), reads=[("E", ei), "masknew"], writes=[("P", ei)])
            st_flag = firstb[ph]
            firstb[ph] = False
            P.add("tensor", lambda e, ei=ei, G=G, hk=hk, ph=ph, st_flag=st_flag: e.matmul(accT(ph, hk), lhsT=Vs[:, 2 * G + hk, :], rhs=Pm[ei][0:TS, 0:4 * TS], start=st_flag, stop=False, skip_group_check=True),
                  reads=[("P", ei), "Vs"], writes=[("pb", 5 + ph)])
    TILES = [(0, 0, 1), (1, 0, 1)] + [(2, r, 4) for r in range(4)] + [(3, r, 16) for r in range(4)]
    kcbs = [kcb, sb("kcb2", [128, 10, 128], BF)]
    vcas = [vca, sb("vca2", [128, 10, 2, 65], BF)]
    kcTs = [kcT, sb("kcT2", [128, 10, 128], BF)]
    Pcs = [Pc, sb("Pc2", [128, 320], BF)]
    vca2_ = vcas[1]
    P.add("vector", lambda e: e.memset(vca2_[:].rearrange("p a b c -> p (a b c)"), 1.0), writes=[("vca", 1)])
    mkc = bass.AP(maskc[:].tensor, 0, [[160, 128], [16, 10], [0, 2], [1, 16]])

    def s_stage1(n):
        cb = n % 2
        kcb_, vca_, kcT_, Pc_, E_ = kcbs[cb], vcas[cb], kcTs[cb], Pcs[cb], Eb[cb]
        ct_ = ctile[cb]
        sbk = 3 + cb
        for ti, (G, r, stp) in enumerate(TILES):
            src = bass.AP(caches[G].tensor, n * CACHE_L[G] * 256 + r * 256, [[stp * 256, 128], [1, 256]])
            P.dma("sync", ct_[:, ti, :], src, writes=[("ctile", cb)])
        P.add("scalar", lambda e: e.activation(out=kcb_[:], in_=ct_[:, :, 0:128], func=AF.Copy), reads=[("ctile", cb)], writes=[("kcb", cb)])
        P.add("vector", lambda e: e.tensor_copy(out=vca_[:, :, :, 0:64], in_=ct_[:, :, 128:256].rearrange("p t (a b) -> p t a b", b=64)), reads=[("ctile", cb)], writes=[("vca", cb)])
        for b0 in (0, 5):
            transpose_in(128, ("kcb", cb), lambda c, b0=b0: kcb_[:, b0 + c, :], 5, kcT_[:, b0:b0 + 5, :], ("kcT", cb))
        for ti, (G, r, stp) in enumerate(TILES):
            for hk in range(2):
                for j in range(4):
                    P.add("tensor", lambda e, ti=ti, hk=hk, j=j, G=G: e.matmul(PB(sbk, 4, (ti * 2 + hk) * 16 + 4 * j), lhsT=kcT_[64 * hk:64 * hk + 64, ti, :], rhs=qTs[64 * hk:64 * hk + 64, 4 * G + j, 4 * n:4 * n + 4], start=(ti == 0 and hk == 0 and j == 0), stop=False, skip_group_check=True),
                          reads=[("kcT", cb), "qTs"], writes=[("pb", sbk)])
        P.add("scalar", lambda e: e.activation(out=E_[:, 0:320], in_=PB(sbk, 320), func=AF.Exp, scale=0.125), reads=[("pb", sbk)], writes=[("E", cb)])
        P.add("vector", lambda e: e.tensor_tensor(out=Pc_[:].rearrange("p (t h c) -> p t h c", h=2, c=16), in0=E_[:, 0:320].rearrange("p (t h c) -> p t h c", h=2, c=16), in1=mkc, op=ALU.mult), reads=[("E", cb), "maskc"], writes=[("Pc", cb)])

    def s_stage2(n):
        cb = n % 2
        vca_, Pc_ = vcas[cb], Pcs[cb]
        for ti, (G, r, stp) in enumerate(TILES):
            ph = 0 if G == 0 else 1
            for hk in range(2):
                for j in range(4):
                    c0 = (5 + ph) * 512 + hk * 256 + j * TS + 4 * n
                    P.add("tensor", lambda e, ti=ti, hk=hk, j=j, c0=c0: e.matmul(ps[0:65, c0:c0 + 4], lhsT=vca_[:, ti, hk, :], rhs=Pc_[:, (ti * 2 + hk) * 16 + 4 * j:(ti * 2 + hk) * 16 + 4 * j + 4], start=False, stop=False, skip_group_check=True),
                          reads=[("Pc", cb), ("vca", cb)], writes=[("pb", 5 + ph)])

    s_stage1(0)
    for n in range(NS):
        if n + 1 < NS:
            s_stage1(n + 1)
        s_stage2(n)
    accT_all = bass.AP(ps[:].tensor, 5 * 512, [[4096, 65], [512, 2], [256, 2], [1, 4 * TS]])
    P.add("vector", lambda e: e.tensor_copy(out=oT[:].rearrange("p (a b) c -> p a b c", b=2), in_=accT_all), reads=[("pb", 5), ("pb", 6)], writes=["oT"])

    def acc_phase_s(ph):
        for hk in range(2):
            for j in range(4):
                P.add("tensor", lambda e, hk=hk, j=j, ph=ph: e.transpose(out=PB(5 + hk, 65, j * 128, TS), in_=oT[0:65, ph * 2 + hk, j * TS:(j + 1) * TS], identity=ident_f[0:65, 0:65]),
                      reads=["oT", "ident_f"], writes=[("pb", 5 + hk)])

    l0_back(TS, 0, 0, x1d[S:S + TS, :], acc_phase_s)

    fence()
    stB.close()
    cur["st"] = st
    load_Wo(w_out1)
    U128 = sb("U128", [128, 128]); cmask128 = sb("cmask128", [128, 128]); sellast = sb("sellast", [128, 128])
    U_s = sb("U_s", [TS, TS]); cmask_s = sb("cmask_s", [TS, TS]); selend_s = sb("selend_s", [TS, TS])
    E4 = sb("E4", [4, 4, 128]); bgb = sb("bgb", [128, 8]); mhb = sb("mhb", [128, D])
    onehot = sb("onehot", [TS, NS])
    for tl, nm in ((U128, "U128"), (cmask128, "cmask128"), (sellast, "sellast128"), (U_s, "U_s"), (cmask_s, "cmask_s"),
                   (selend_s, "selend_s"), (E4, "E4"), (onehot, "onehot_s")):
        P.dma("sync", tl[:], cd[nm], writes=[nm])
    P.dma("sync", bgb[:], bg_bc, writes=["bgb"])
    P.dma("sync", mhb[:], mh_bc, writes=["mhb"])
    vaug = sb("vaug", [128, 4, 257], BF)
    gt = sb("gt", [128, 8]); sp = sb("sp", [128, 4]); a_t = sb("a_t", [128, 4]); aT = sb("aT", [4, 128])
    Dm = sb("Dm", [128, 4, 128]); pw = sb("pw", [128, 4, 128]); cm = sb("cm", [128, 4]); gg = sb("gg", [128, 4]); negg = sb("negg", [128, 4])
    mpv = sb("mpv", [128, 4]); cw = sb("cw", [128, 4]); mtok = sb("mtok", [128, 4]); emm = sb("emm", [128, 4]); wsb = sb("wsb", [128, 4]); dec = sb("dec", [128, 4])
    tmp4 = sb("tmp4", [128, 4]); dint = sb("dint", [128, 4]); den = sb("den", [128, 4]); rdd = sb("rdd", [128, 4])
    onec = sb("onec", [128, 1]); scb = sb("scb", [128, 4, 128], BF)
    Cst = sb("Cst", [128, 4, 2, 257]); Cb = sb("Cb", [128, 4, 2, 257], BF)
    sth = sb("sth", [128, 4, 6]); mvh = sb("mvh", [128, 4, 2]); rsh = sb("rsh", [128, 4])
    tmpq = rbuf
    P.add("gpsimd", lambda e: e.memset(onec[:], 1.0), writes=["onec"])
    P.add("gpsimd", lambda e: e.memset(vaug[:].rearrange("p a b -> p (a b)"), 1.0), writes=["vaug", ("vaug1", 0)])
    P.add("gpsimd", lambda e: e.memset(Cst[:].rearrange("p a b c -> p (a b c)"), 0.0), writes=["Cst"])
    P.add("gpsimd", lambda e: e.memset(Cb[:].rearrange("p a b c -> p (a b c)"), 0.0), writes=["Cb"])
    P.add("gpsimd", lambda e: e.memset(mtok[:], 0.0), writes=["mtok"])
    hnum = mixpre
    Hs = [H, H]; Hbs = [Hb, Hb]; vaugs = [vaug, vaug]; gts = [gt, gt]

    def mlstm(T, x_ap, s2, out_dram, sample, p=0, mode="all", deftail=False, cast_eng="gpsimd"):
        H_ = Hs[p]; Hb_ = Hbs[p]; sz_ = sz[p]; vaug_ = vaugs[p]; gt_ = gts[p]
        if sample:
            kH, kHb, ksz, kva, kgt = "H", "Hb", ("sz", 0), "vaug", "gt"
        else:
            kH, kHb, ksz, kva, kgt = ("H1", p), ("Hb1", p), ("sz", p), ("vaug1", p), ("gt1", p)
        DR = 4
        def evac(cg, bank, n):
            src = PB(bank, n, 0, T)
            k_ = ("pb", bank)
            if cg < 2:
                if sample:
                    P.add("scalar", lambda e: e.activation(out=Hb_[0:T, cg * 512:(cg + 1) * 512], in_=src, func=AF.Copy), reads=[k_], writes=[kHb])
                else:
                    P.add("vector", lambda e: e.tensor_copy(out=Hb_[0:T, cg * 512:(cg + 1) * 512], in_=src), reads=[k_], writes=[kHb])
                if sample:
                    P.add("scalar", lambda e: e.activation(out=H_[0:T, 2048 + cg * 512: 2048 + (cg + 1) * 512], in_=src, func=AF.Copy), reads=[k_], writes=[kH])
            elif cg < 4:
                c0 = (cg - 2) * 512
                P.add("scalar", lambda e: e.activation(out=H_[0:T, c0:c0 + 512], in_=src, func=AF.Copy, scale=1.0 / 16.0), reads=[k_], writes=[kH])
                P.add("gpsimd", lambda e: e.tensor_copy(out=Hb_[0:T, 1024 + c0:1024 + c0 + 512], in_=H_[0:T, c0:c0 + 512]), reads=[kH], writes=[kHb])
            elif cg < 6:
                h0 = 2 * (cg - 4)
                if sample:
                    P.add("scalar", lambda e: e.activation(out=vaug_[0:T, h0:h0 + 2, 0:256], in_=src.rearrange("p (a b) -> p a b", b=256), func=AF.Copy), reads=[k_], writes=[kva])
                else:
                    P.add("vector", lambda e: e.tensor_copy(out=vaug_[0:T, h0:h0 + 2, 0:256], in_=src.rearrange("p (a b) -> p a b", b=256)), reads=[k_], writes=[kva])
                if sample:
                    P.add("scalar", lambda e: e.activation(out=vf[0:T, h0:h0 + 2, :], in_=src.rearrange("p (a b) -> p a b", b=256), func=AF.Copy), reads=[k_], writes=["vf"])
            elif cg < 8:
                c0 = (cg - 6) * 512
                P.add("scalar", lambda e: e.activation(out=H_[0:T, 1024 + c0:1024 + c0 + 512], in_=src, func=AF.Sigmoid), reads=[k_], writes=[kH])
            elif cg < 10:
                c0 = (cg - 8) * 512
                P.add("scalar", lambda e: e.activation(out=sz_[0:T, c0:c0 + 512], in_=src, func=AF.Silu), reads=[k_], writes=[ksz])
            else:
                P.add("vector", lambda e: e.tensor_tensor(out=gt_[0:T, :], in0=src, in1=bgb[0:T, :], op=ALU.add), reads=[k_, "bgb"], writes=[kgt])
        if mode in ("all", "front"):
            load_and_project(T, x_ap, s2, IN1, evac, defer=(mode == "front"), atomic=True, cast_eng=cast_eng)
        if mode == "front":
            return
        Um, cmk, sel = (U_s, cmask_s, selend_s) if sample else (U128, cmask128, sellast)
        Uk, cmkk, selk = ("U_s", "cmask_s", "selend_s") if sample else ("U128", "cmask128", "sellast128")
        G7 = lambda off, n=4: PB(7, n, off, T)
        P.add("scalar", lambda e: e.activation(out=sp[0:T, :], in_=gt_[0:T, 4:8], func=AF.Exp, scale=-1.0), reads=[kgt], writes=["sp"])
        P.add("scalar", lambda e: e.activation(out=sp[0:T, :], in_=sp[0:T, :], func=AF.Ln, bias=onec[0:T, :], scale=1.0), reads=["sp", "onec"], writes=["sp"])
        P.add("tensor", lambda e: e.matmul(G7(0), lhsT=Um[0:T, 0:T], rhs=sp[0:T, :], start=True, stop=True, skip_group_check=True), reads=["sp", Uk], writes=[("pb", 7)])
        P.add("vector", lambda e: e.tensor_tensor(out=a_t[0:T, :], in0=G7(0), in1=gt_[0:T, 0:4], op=ALU.add), reads=[("pb", 7), kgt], writes=["a_t"])
        if sample:
            P.dma("sync", mpv[0:T, :], dr["cmax_tok"], writes=["mpv"])
        else:
            P.add("tensor", lambda e: e.matmul(G7(8), lhsT=sel[0:T, 0:T], rhs=mtok[0:T, :], start=False, stop=False, skip_group_check=True), reads=["mtok", selk], writes=[("pb", 7)])
            P.add("vector", lambda e: e.tensor_copy(out=mpv[0:T, :], in_=G7(8)), reads=[("pb", 7)], writes=["mpv"])
        P.add("tensor", lambda e: e.transpose(out=ps[0:4, 7 * 512 + 128: 7 * 512 + 128 + T], in_=a_t[0:T, :], identity=ident_f[0:T, 0:T]), reads=["a_t", "ident_f"], writes=[("pb", 7)])
        P.add("vector", lambda e: e.tensor_copy(out=aT[:, 0:T], in_=ps[0:4, 7 * 512 + 128: 7 * 512 + 128 + T]), reads=[("pb", 7)], writes=["aT"])
        drain(DR)
        for h in range(4):
            P.add("tensor", lambda e, h=h: e.matmul(PB(4, T, h * 128, T), lhsT=E4[0:4, h, 0:T], rhs=aT[0:4, 0:T], start=(h == 0), stop=False, skip_group_check=True), reads=["aT", "E4"], writes=[("pb", 4)])
        cmk_bc = bc_mid(cmk, 0, T, T, 4, T)
        A_ps = bass.AP(ps[:].tensor, 4 * 512, [[4096, T], [128, 4], [1, T]])
        P.add("vector", lambda e: e.tensor_tensor(out=Dm[0:T, :, 0:T], in0=A_ps, in1=cmk_bc, op=ALU.add), reads=[("pb", 4), cmkk], writes=["Dm"])
        P.add("vector", lambda e: e.tensor_reduce(out=cm[0:T, :], in_=Dm[0:T, :, 0:T], axis=AX.X, op=ALU.max), reads=["Dm"], writes=["cm"])
        P.add("vector", lambda e: e.tensor_tensor(out=gg[0:T, :], in0=cm[0:T, :], in1=mpv[0:T, :], op=ALU.max), reads=["cm", "mpv"], writes=["gg"])
        drain(DR)
        P.add("vector", lambda e: e.tensor_scalar(out=negg[0:T, :], in0=gg[0:T, :], scalar1=-1.0, scalar2=None, op0=ALU.mult), reads=["gg"], writes=["negg"])
        for h in range(4):
            P.add("scalar", lambda e, h=h: e.activation(out=pw[0:T, h, 0:T], in_=Dm[0:T, h, 0:T], func=AF.Exp, bias=negg[0:T, h:h + 1], scale=1.0), reads=["Dm", "negg"], writes=["pw"])
        drain(DR)
        P.add("vector", lambda e: e.tensor_tensor(out=tmp4[0:T, :], in0=mpv[0:T, :], in1=gg[0:T, :], op=ALU.subtract), reads=["mpv", "gg"], writes=["tmp4"])
        P.add("scalar", lambda e: e.activation(out=cw[0:T, :], in_=tmp4[0:T, :], func=AF.Exp), reads=["tmp4"], writes=["cw"])
        drain(DR)
        P.add("tensor", lambda e: e.matmul(G7(12), lhsT=sel[0:T, 0:T], rhs=gg[0:T, :], start=False, stop=False, skip_group_check=True), reads=["gg", selk], writes=[("pb", 7)])
        P.add("vector", lambda e: e.tensor_tensor(out=mtok[0:T, :], in0=gg[0:T, :], in1=G7(0), op=ALU.subtract), reads=["gg", ("pb", 7)], writes=["mtok"])
        P.add("scalar", lambda e: e.activation(out=emm[0:T, :], in_=mtok[0:T, :], func=AF.Exp, scale=-1.0), reads=["mtok"], writes=["emm"])
        P.add("vector", lambda e: e.tensor_tensor(out=tmp4[0:T, :], in0=a_t[0:T, :], in1=G7(12), op=ALU.subtract), reads=["a_t", ("pb", 7), "cw"], writes=["tmp4"])
        P.add("scalar", lambda e: e.activation(out=wsb[0:T, :], in_=tmp4[0:T, :], func=AF.Exp), reads=["tmp4"], writes=["wsb"])
        P.add("vector", lambda e: e.tensor_tensor(out=tmp4[0:T, :], in0=mpv[0:T, :], in1=G7(12), op=ALU.subtract), reads=["mpv", ("pb", 7), "wsb"], writes=["tmp4"])
        P.add("scalar", lambda e: e.activation(out=dec[0:T, :], in_=tmp4[0:T, :], func=AF.Exp), reads=["tmp4"], writes=["dec"])
        drain(DR)
        transpose_in(T, kHb, lambda c: Hb_[0:T, c * 128:(c + 1) * 128], 8, qT[:, 0:8, 0:T], "qT")
        transpose_in(T, kHb, lambda c: Hb_[0:T, 1024 + c * 128:1024 + (c + 1) * 128], 8, qT[:, 8:16, 0:T], "qT")
        for h in range(4):
            for c in range(2):
                P.add("tensor", lambda e, h=h, c=c: e.matmul(PB(3, T, h * 128, T), lhsT=qT[:, 2 * h + c, 0:T], rhs=qT[:, 8 + 2 * h + c, 0:T], start=(h == 0 and c == 0), stop=False, skip_group_check=True),
                      reads=["qT"], writes=[("pb", 3)])
        drain(DR)
        QK_ps = bass.AP(ps[:].tensor, 3 * 512, [[4096, T], [128, 4], [1, T]])
        P.add("vector", lambda e: e.tensor_tensor(out=Dm[0:T, :, 0:T], in0=QK_ps, in1=pw[0:T, :, 0:T], op=ALU.mult), reads=[("pb", 3), "pw", "cm"], writes=["Dm"])
        P.add("vector", lambda e: e.tensor_reduce(out=dint[0:T, :], in_=Dm[0:T, :, 0:T], axis=AX.X, op=ALU.add), reads=["Dm"], writes=["dint"])
        drain(DR)
        P.add("gpsimd", lambda e: e.tensor_copy(out=scb[0:T, :, 0:T], in_=Dm[0:T, :, 0:T]), reads=["Dm"], writes=["scb"])
        transpose_in(T, "scb", lambda c: scb[0:T, c, 0:T], 4, Eb[0][0:T, 0:4 * T], ("E", 0), W=T)
        for h in range(4):
            P.add("tensor", lambda e, h=h: e.matmul(PB(5 + h // 2, 256, (h % 2) * 256, T), lhsT=Eb[0][0:T, h * T:(h + 1) * T], rhs=vaug_[0:T, h, 0:256], start=(h % 2 == 0), stop=False, skip_group_check=True),
                  reads=[("E", 0), kva], writes=[("pb", 5 + h // 2)])
        drain(DR)
        if not sample:
            for h in range(4):
                for c in range(2):
                    P.add("tensor", lambda e, h=h, c=c: e.matmul(PB(h // 2, 256, (h % 2) * 256), lhsT=qT[:, 2 * h + c, :], rhs=Cb[:, h, c, 0:256], start=(h % 2 == 0 and c == 0), stop=False, skip_group_check=True),
                          reads=["qT", "Cb"], writes=[("pb", h // 2)])
            for h in range(4):
                for c in range(2):
                    P.add("tensor", lambda e, h=h, c=c: e.matmul(PB(7, 1, 20 + h), lhsT=qT[:, 2 * h + c, :], rhs=Cb[:, h, c, 256:257], start=False, stop=False, skip_group_check=True),
                          reads=["qT", "Cb"], writes=[("pb", 7)])
            for h in range(4):
                P.add("scalar", lambda e, h=h: e.activation(out=tmpq[:, h * 256:(h + 1) * 256], in_=PB(h // 2, 256, (h % 2) * 256), func=AF.Copy, scale=cw[:, h:h + 1]), reads=[("pb", h // 2), "cw"], writes=["rbuf"])
            P.add("vector", lambda e: e.tensor_tensor(out=den[:, :], in0=G7(20), in1=cw[:, :], op=ALU.mult), reads=[("pb", 7), "cw"], writes=["den"])
        else:
            sample_inter(T)
        drain(DR)
        P.add("vector", lambda e: e.tensor_tensor(out=hnum[0:T, :], in0=ps[0:T, 5 * 512:7 * 512], in1=tmpq[0:T, :], op=ALU.add), reads=[("pb", 5), ("pb", 6), "rbuf"], writes=["mixpre"])
        P.add("vector", lambda e: e.tensor_tensor(out=den[0:T, :], in0=den[0:T, :], in1=dint[0:T, :], op=ALU.add), reads=["den", "dint"], writes=["den"])
        P.add("vector", lambda e: e.tensor_scalar(out=tmp4[0:T, :], in0=den[0:T, :], scalar1=-1.0, scalar2=None, op0=ALU.mult), reads=["den", "dec"], writes=["tmp4"])
        P.add("vector", lambda e: e.tensor_tensor(out=den[0:T, :], in0=den[0:T, :], in1=tmp4[0:T, :], op=ALU.max), reads=["den", "tmp4"], writes=["den"])
        P.add("vector", lambda e: e.tensor_tensor(out=den[0:T, :], in0=den[0:T, :], in1=emm[0:T, :], op=ALU.max), reads=["den", "emm"], writes=["den"])
        P.add("vector", lambda e: e.reciprocal(out=rdd[0:T, :], in_=den[0:T, :]), reads=["den"], writes=["rdd"])
        for h in range(4):
            P.add("vector", lambda e, h=h: e.scalar_tensor_tensor(out=hnum[0:T, h * 256:(h + 1) * 256], in0=hnum[0:T, h * 256:(h + 1) * 256], scalar=rdd[0:T, h:h + 1], in1=H_[0:T, 1024 + h * 256:1024 + (h + 1) * 256], op0=ALU.mult, op1=ALU.mult),
                  reads=["mixpre", "rdd", kH], writes=["mixpre"])
            P.add("vector", lambda e, h=h: e.bn_stats(out=sth[0:T, h, :], in_=hnum[0:T, h * 256:(h + 1) * 256]), reads=["mixpre"], writes=["sth"])
            P.add("vector", lambda e, h=h: e.bn_aggr(out=mvh[0:T, h, :], in_=sth[0:T, h, :]), reads=["sth"], writes=["mvh"])
        drain(DR)
        P.add("vector", lambda e: e.tensor_scalar(out=rsh[0:T, :], in0=mvh[0:T, :, 1], scalar1=MH_EPS, scalar2=None, op0=ALU.add), reads=["mvh"], writes=["rsh"])
        P.add("scalar", lambda e: e.activation(out=rsh[0:T, :], in_=rsh[0:T, :], func=AF.Sqrt), reads=["rsh"], writes=["rsh"])
        P.add("vector", lambda e: e.reciprocal(out=rsh[0:T, :], in_=rsh[0:T, :]), reads=["rsh"], writes=["rsh"])
        for h in range(4):
            P.add("vector", lambda e, h=h: e.tensor_scalar(out=hnum[0:T, h * 256:(h + 1) * 256], in0=hnum[0:T, h * 256:(h + 1) * 256], scalar1=mvh[0:T, h, 0:1], scalar2=rsh[0:T, h:h + 1], op0=ALU.subtract, op1=ALU.mult),
                  reads=["mixpre", "mvh", "rsh"], writes=["mixpre"])
        drain(DR)
        P.add("gpsimd", lambda e: e.tensor_tensor(out=hnum[0:T, :], in0=hnum[0:T, :], in1=mhb[0:T, :], op=ALU.mult), reads=["mixpre", "mhb"], writes=["mixpre"])
        P.add("gpsimd", lambda e: e.tensor_tensor(out=mixb[0:T, :], in0=hnum[0:T, :], in1=sz_[0:T, :], op=ALU.mult), reads=["mixpre", ksz], writes=["mixb"])
        tail(T, s2, 1, out_dram, defer=deftail, atomic=True)
        drain(DR)
        if not sample:
            for h in range(4):
                P.add("gpsimd", lambda e, h=h: e.tensor_scalar(out=Hb_[:, 2048 + h * 256:2048 + (h + 1) * 256], in0=H_[:, h * 256:(h + 1) * 256], scalar1=wsb[:, h:h + 1], scalar2=None, op0=ALU.mult), reads=[kH, "wsb"], writes=[kHb])
            for h in range(4):
                for c in range(2):
                    bank = 3 + (h * 2 + c) % 2
                    P.add("tensor", lambda e, h=h, c=c, bank=bank: e.matmul(PB(bank, 257), lhsT=Hb_[:, 2048 + h * 256 + c * 128: 2048 + h * 256 + (c + 1) * 128], rhs=vaug_[:, h, :], start=True, stop=True),
                          reads=[kHb, kva], writes=[("pb", bank)])
                    P.add("vector", lambda e, h=h, c=c, bank=bank: e.scalar_tensor_tensor(out=Cst[:, h, c, :], in0=Cst[:, h, c, :], scalar=dec[:, h:h + 1], in1=PB(bank, 257), op0=ALU.mult, op1=ALU.add),
                          reads=["Cst", "dec", ("pb", bank)], writes=["Cst"])
            P.add("gpsimd", lambda e: e.tensor_copy(out=Cb[:].rearrange("p a b c -> p (a b c)"), in_=Cst[:].rearrange("p a b c -> p (a b c)")), reads=["Cst"], writes=["Cb"])
        else:
            sample_state(T)

    cmax_tok = din("cmax_tok", [TS, 4])
    cnorm_tok = din("cnorm_tok", [TS, 1024])

    def sample_inter(T):
        for c in range(8):
            P.add("tensor", lambda e, c=c: e.transpose(out=PB(4, T, c * T), in_=H[0:T, 2048 + c * 128: 2048 + (c + 1) * 128], identity=ident_f[0:T, 0:T]), reads=["H", "ident_f"], writes=[("pb", 4)])
        P.add("vector", lambda e: e.tensor_copy(out=qTf[:].rearrange("p a b -> p (a b)"), in_=PB(4, 8 * T)), reads=[("pb", 4)], writes=["qTf"])
        P.add("gpsimd", lambda e: e.memset(tmpq[0:T, :], 0.0), writes=["rbuf"])
        for n in range(NS):
            for h in range(4):
                cb = (n * 4 + h) % 2
                P.dma("sync", Cs[cb][:], cmem[n, h].rearrange("(c p) v -> p c v", p=128), writes=[("Cs", cb)])
                for c in range(2):
                    P.add("tensor", lambda e, h=h, c=c, cb=cb: e.matmul(PB(h // 2, 256, (h % 2) * 256, T), lhsT=qTf[:, 2 * h + c, :], rhs=Cs[cb][:, c, :], start=(h % 2 == 0 and c == 0), stop=False, skip_group_check=True),
                          reads=["qTf", ("Cs", cb)], writes=[("pb", h // 2)])
            P.add("vector", lambda e, n=n: e.scalar_tensor_tensor(out=tmpq[0:T, :], in0=ps[0:T, 0:1024], scalar=onehot[0:T, n:n + 1], in1=tmpq[0:T, :], op0=ALU.mult, op1=ALU.add), reads=[("pb", 0), ("pb", 1), "onehot_s", "rbuf"], writes=["rbuf"])
        for h in range(4):
            P.add("vector", lambda e, h=h: e.tensor_scalar(out=tmpq[0:T, h * 256:(h + 1) * 256], in0=tmpq[0:T, h * 256:(h + 1) * 256], scalar1=cw[0:T, h:h + 1], scalar2=None, op0=ALU.mult), reads=["rbuf", "cw"], writes=["rbuf"])
        P.dma("sync", kwf[0:TS, :], cnorm_tok, writes=["mixpre"])
        P.add("gpsimd", lambda e: e.tensor_tensor(out=kwn[0:T, :], in0=H[0:T, 2048:3072], in1=kwf[0:T, :], op=ALU.mult), reads=["H", "mixpre"], writes=["kwn"])
        P.add("vector", lambda e: e.tensor_reduce(out=den[0:T, :], in_=kwn[0:T, :].rearrange("p (h d) -> p h d", d=256), axis=AX.X, op=ALU.add), reads=["kwn"], writes=["den"])
        P.add("vector", lambda e: e.tensor_tensor(out=den[0:T, :], in0=den[0:T, :], in1=cw[0:T, :], op=ALU.mult), reads=["den", "cw"], writes=["den"])

    def sample_state(T):
        for h in range(4):
            P.add("gpsimd", lambda e, h=h: e.tensor_scalar(out=kwf[0:T, h * 256:(h + 1) * 256], in0=H[0:T, h * 256:(h + 1) * 256], scalar1=wsb[0:T, h:h + 1], scalar2=None, op0=ALU.mult), reads=["H", "wsb"], writes=["mixpre"])
        P.add("gpsimd", lambda e: e.memset(quart[:], 0.25), writes=["quart"])
        P.add("vector", lambda e: e.tensor_tensor(out=Xd[0:T, :, :], in0=bc_last(onehot, 0, NS, T, NS, 1, 4), in1=bc_mid(dec, 0, 4, T, NS, 4), op=ALU.mult), reads=["dec", "onehot_s"], writes=["Xd"])
        P.add("tensor", lambda e: e.matmul(PB(7, 4 * NS, 64), lhsT=quart[0:T, :], rhs=Xd[0:T].rearrange("p a b -> p (a b)"), start=False, stop=False, skip_group_check=True), reads=["Xd", "quart"], writes=[("pb", 7)])
        P.add("vector", lambda e: e.tensor_copy(out=decb[:].rearrange("p a b -> p (a b)"), in_=PB(7, 4 * NS, 64)), reads=[("pb", 7)], writes=["decb"])
        P.add("tensor", lambda e: e.matmul(ps[0:NS, 7 * 512 + 256: 7 * 512 + 260], lhsT=onehot[0:T, :], rhs=dec[0:T, :], start=False, stop=False, skip_group_check=True), reads=["dec", "onehot_s"], writes=[("pb", 7)])
        P.add("vector", lambda e: e.tensor_scalar(out=decn[:], in0=ps[0:NS, 7 * 512 + 256: 7 * 512 + 260], scalar1=0.25, scalar2=None, op0=ALU.mult), reads=[("pb", 7)], writes=["decn"])
        P.dma("sync", kwn[0:NS, :], cnorm, writes=["kwn"])
        for i in range(2):
            P.add("tensor", lambda e, i=i: e.matmul(ps[0:NS, (3 + i) * 512:(4 + i) * 512], lhsT=onehot[0:T, :], rhs=kwf[0:T, i * 512:(i + 1) * 512], start=True, stop=True), reads=["mixpre", "onehot_s"], writes=[("pb", 3 + i)])
        for h in range(4):
            P.add("vector", lambda e, h=h: e.scalar_tensor_tensor(out=kwn[0:NS, h * 256:(h + 1) * 256], in0=kwn[0:NS, h * 256:(h + 1) * 256], scalar=decn[:, h:h + 1], in1=ps[0:NS, 3 * 512 + h * 256: 3 * 512 + (h + 1) * 256], op0=ALU.mult, op1=ALU.add),
                  reads=["kwn", "decn", ("pb", 3), ("pb", 4)], writes=["kwn"])
        P.dma("gpsimd", cnorm_s, kwn[0:NS, :], reads=["kwn"])
        P.dma("gpsimd", cmax_s, mtok[0:T, :], reads=["mtok"])
        for n in range(NS):
            P.add("gpsimd", lambda e, n=n: e.tensor_scalar(out=kwn[0:T, :], in0=kwf[0:T, :], scalar1=onehot[0:T, n:n + 1], scalar2=None, op0=ALU.mult), reads=["mixpre", "onehot_s"], writes=["kwn"])
            for h in range(4):
                cb = (n * 4 + h) % 2
                P.dma("sync", Cs[cb][:], cmem[n, h].rearrange("(c p) v -> p c v", p=128), writes=[("Cs", cb)])
                for c in range(2):
                    bank = 3 + c
                    P.add("tensor", lambda e, h=h, c=c, bank=bank: e.matmul(PB(bank, 256), lhsT=kwn[0:T, h * 256 + c * 128: h * 256 + (c + 1) * 128], rhs=vf[0:T, h, :], start=True, stop=True), reads=["kwn", "vf"], writes=[("pb", bank)])
                    P.add("vector", lambda e, h=h, c=c, bank=bank, cb=cb, n=n: e.scalar_tensor_tensor(out=Cn[cb][:, c, :], in0=Cs[cb][:, c, :], scalar=decb[:, n, h:h + 1], in1=PB(bank, 256), op0=ALU.mult, op1=ALU.add),
                          reads=[("Cs", cb), "decb", ("pb", bank)], writes=[("Cn", cb)])
                P.dma("scalar", cmem_s[n, h].rearrange("(c p) v -> p c v", p=128), Cn[cb][:], reads=[("Cn", cb)])

    P.dma("sync", lng[:], lng_bc[:, 1, :], writes=["lng"])
    P.dma("sync", lnb[:], lnb_bc[:, 1, :], writes=["lnb"])
    stC = contextlib.ExitStack()
    cur["st"] = stC
    Hs[1] = sb("H2", [128, 2048]); Hbs[1] = sb("Hb2", [128, 3072], BF); vaugs[1] = sb("vaug2", [128, 4, 257], BF); gts[1] = sb("gt2", [128, 8])
    sz[1] = sb("sz2", [128, D]); xin[1] = sb("xin2", [128, D])
    cur["st"] = st
    va2 = vaugs[1]
    P.add("gpsimd", lambda e: e.memset(va2[:].rearrange("p a b -> p (a b)"), 1.0), writes=[("vaug1", 1)])
    mlstm(128, x1d[0:128, :], 0, None, False, p=0, mode="front", cast_eng="scalar")
    drain()
    for c in range(NT):
        if c + 1 < NT:
            mlstm(128, x1d[(c + 1) * 128:(c + 2) * 128, :], (c + 1) % 2, None, False, p=(c + 1) % 2, mode="front")
        mlstm(128, None, c % 2, yp[c * 128:(c + 1) * 128, :], False, p=c % 2, mode="rest", deftail=True)
    drain()
    for h in range(4):
        P.dma("gpsimd", cmem_p[h].rearrange("(c p) v -> p c v", p=128), Cst[:, h, :, 0:256], reads=["Cst"])
        P.dma("gpsimd", cnorm_p[h].rearrange("(c p) -> p c", p=128), Cst[:, h, :, 256], reads=["Cst"], allow_slow_non_contiguous=True)
    P.dma("gpsimd", cmax_p, mtok[127:128, :], reads=["mtok"])
    fence()
    stC.close()
    Hs[1] = Hs[0]; Hbs[1] = Hbs[0]; vaugs[1] = vaugs[0]; gts[1] = gts[0]; sz[1] = sz[0]; xin[1] = xin[0]
    Cs = [sb("Cs%d" % i, [128, 2, 256]) for i in range(2)]
    Cn = [sb("Cn%d" % i, [128, 2, 256]) for i in range(2)]
    qTf = sb("qTf", [128, 8, TS])
    vf = sb("vf", [TS, 4, 256])
    kwf = mixpre; kwn = sb("kwn", [TS, 1024])
    decb = sb("decb", [128, NS, 4]); decn = sb("decn", [NS, 4]); quart = sb("quart", [TS, 128]); Xd = sb("Xd", [TS, NS, 4])

    mlstm(TS, x1d[S:S + TS, :], 0, ys[:, :], True)

    if _LIMIT[0] is not None:
        print('total ops', len(P.ops))
        P.ops = P.ops[:_LIMIT[0]]
    P.emit(st)
    return nc, st, consts


def core_inputs(inp, core, NT, NS, consts, xp=None):
    TS = 4 * NS
    b = core % 4
    s0 = core * NS
    f = {}
    f["xp"] = np.ascontiguousarray(inp["x_prompt"][b] if xp is None else xp)
    f["xs"] = np.ascontiguousarray(inp["x_sample"][s0:s0 + NS].reshape(TS, D))
    for g, nm in enumerate(("cache_a_kv", "cache_b0_kv", "cache_b1_kv", "cache_b2_kv")):
        f["c%d" % g] = np.ascontiguousarray(inp[nm][0, s0:s0 + NS].reshape(NS, CACHE_L[g], 256))
    f["cmem"] = np.ascontiguousarray(inp["state_c_mem"][0, s0:s0 + NS])
    f["cnorm"] = np.ascontiguousarray(inp["state_c_norm"][0, s0:s0 + NS].reshape(NS, 1024))
    f["cmax"] = np.ascontiguousarray(inp["state_c_max"][0, s0:s0 + NS])
    f["w_in0"] = np.ascontiguousarray(inp["w_in0"][0])
    f["w_out0"] = np.ascontiguousarray(inp["w_out0"][0])
    f["w_in1"] = np.ascontiguousarray(inp["w_in1"][0])
    f["w_out1"] = np.ascontiguousarray(inp["w_out1"][0])
    f["sink_bc"] = np.ascontiguousarray(np.broadcast_to(inp["sinks0"][0][None, :], (128, 8)))
    f["bg_bc"] = np.ascontiguousarray(np.broadcast_to(inp["b_gates1"][0][None, :], (128, 8)))
    f["mh_bc"] = np.ascontiguousarray(np.broadcast_to(inp["mh_norm1"][0][None, :], (128, 1024)))
    f["lng_bc"] = np.ascontiguousarray(np.broadcast_to(inp["ln_g"][None, :, :], (128, 2, 1024)))
    f["lnb_bc"] = np.ascontiguousarray(np.broadcast_to(inp["ln_b"][None, :, :], (128, 2, 1024)))
    f["cmax_tok"] = np.repeat(f["cmax"], 4, axis=0)
    f["cnorm_tok"] = np.repeat(f["cnorm"], 4, axis=0)
    for k, v in consts.items():
        f["k_" + k] = np.ascontiguousarray(v)
    return {k: np.asarray(v, np.float32) for k, v in f.items()}


NT_FULL = 32
NS_FULL = 16
_CACHE = {}


def kernel(**inp):
    inp = {k: np.asarray(v) for k, v in inp.items()}
    if "nc" not in _CACHE:
        nc, st, consts = build(NT_FULL, NS_FULL)
        _CACHE["nc"] = (nc, consts)
    nc, consts = _CACHE["nc"]
    in_maps = [core_inputs(inp, c, NT_FULL, NS_FULL, consts) for c in range(8)]
    res = run_bass_kernel_spmd(nc, in_maps, core_ids=list(range(8)))
    R = res.results
    f32 = np.float32
    yp = np.stack([R[b]["yp"] for b in range(4)], 0).astype(f32)
    ys = np.concatenate([R[c]["ys"].reshape(NS_FULL, 4, D) for c in range(8)], 0).astype(f32)
    outs = [yp, ys]
    for g in range(4):
        L = CACHE_L[g]
        outs.append(np.stack([R[b]["kvp%d" % g].reshape(L, 2, 2, 64) for b in range(4)], 0)[None].astype(f32))
    outs.append(np.stack([R[b]["cmem_p"] for b in range(4)], 0)[None].astype(f32))
    outs.append(np.stack([R[b]["cnorm_p"] for b in range(4)], 0)[None].astype(f32))
    outs.append(np.stack([R[b]["cmax_p"].reshape(4) for b in range(4)], 0)[None].astype(f32))
    for g in range(4):
        L = CACHE_L[g]
        outs.append(np.concatenate([R[c]["kvs%d" % g].reshape(NS_FULL, L, 2, 2, 64) for c in range(8)], 0)[None].astype(f32))
    outs.append(np.concatenate([R[c]["cmem_s"] for c in range(8)], 0)[None].astype(f32))
    outs.append(np.concatenate([R[c]["cnorm_s"].reshape(NS_FULL, 4, 256) for c in range(8)], 0)[None].astype(f32))
    outs.append(np.concatenate([R[c]["cmax_s"].reshape(NS_FULL, 4, 4)[:, 3, :] for c in range(8)], 0)[None].astype(f32))
    return tuple(outs)
```
